# Optimizing a Trainium2 kernel written in Bass

```python
import math
import jax, jax.numpy as jnp
from jax import lax
import numpy as np

D_MODEL = 1024
BATCH = 8
SEQ = 2048
DEPTH = 2

GRID_W = 64
CTX_LEN = 256
BRANCH_W = D_MODEL
CHUNK = 128
EPS = 1e-6

SSD_INNER = BRANCH_W
SSD_HEADDIM = 64
SSD_HEADS = SSD_INNER // SSD_HEADDIM
SSD_STATE = 128
SSD_GROUPS = 2
SSD_CONV = 5
SSD_CONV_CH = SSD_INNER + 2 * SSD_GROUPS * SSD_STATE
SSD_COLS = SSD_CONV_CH + SSD_INNER + 2 * SSD_HEADS

ML_INNER = BRANCH_W
ML_HEADS = 8
ML_HEADDIM = ML_INNER // ML_HEADS
ML_CONV = 5
ML_COLS = 4 * ML_INNER + 4 * ML_HEADS

HY_INNER = BRANCH_W
HY_ORDER = 2
HY_SHORT = 3
HY_BANDS = 16
HY_FEAT = 1 + 2 * HY_BANDS
HY_FFN = 64
HY_COLS = (HY_ORDER + 1) * HY_INNER

N_BRANCH = 3
REC_COLS = SSD_COLS + ML_COLS
IN_COLS = REC_COLS + HY_COLS + N_BRANCH * D_MODEL
MLP_HIDDEN = 4 * D_MODEL

kernel_name = 'hybrid_ssd_mlstm_hyena_dit_block'


def rmsnorm(x, w):
    xf = x.astype(jnp.float32)
    xf = xf * lax.rsqrt(jnp.mean(xf * xf, axis=-1, keepdims=True) + EPS)
    return (xf * w.astype(jnp.float32)).astype(x.dtype)


def modulate(h, shift, scale):
    return h * (1.0 + scale) + shift


def dwconv(u, w, b):
    k = w.shape[0]
    y = lax.conv_general_dilated(u, w[:, None, :].astype(u.dtype), (1,), [(k // 2, k // 2)],
                                 dimension_numbers=('NWC', 'WIO', 'NWC'),
                                 feature_group_count=u.shape[-1])
    return y + b.astype(u.dtype)


def to_colmajor(t, rows):
    b, L = t.shape[:2]
    rest = t.shape[2:]
    return t.reshape(b, rows, GRID_W, *rest).swapaxes(1, 2).reshape(b, L, *rest)


def from_colmajor(t, rows):
    b, L = t.shape[:2]
    rest = t.shape[2:]
    return t.reshape(b, GRID_W, rows, *rest).swapaxes(1, 2).reshape(b, L, *rest)


def chunked_scan(q, k, v, log_a, log_w, s0, m0, need_y=True):
    f32 = jnp.float32
    bsz, L, H, dk = q.shape
    dv = v.shape[-1]
    nc = L // CHUNK
    q = q.astype(f32).reshape(bsz, nc, CHUNK, H, dk)
    k = k.astype(f32).reshape(bsz, nc, CHUNK, H, dk)
    v = v.astype(f32).reshape(bsz, nc, CHUNK, H, dv)
    cum = jnp.cumsum(log_a.astype(f32).reshape(bsz, nc, CHUNK, H), axis=2).transpose(0, 1, 3, 2)
    w = log_w.astype(f32).reshape(bsz, nc, CHUNK, H).transpose(0, 1, 3, 2)
    last = cum[..., -1]
    g = last[..., None] - cum + w
    m_loc = jnp.max(g, axis=-1)
    local = jnp.einsum('bchs,bcshv,bcshk->bchvk', jnp.exp(g - m_loc[..., None]), v, k)

    def step(carry, inp):
        s, m = carry
        a, ml, loc = inp
        m_new = jnp.maximum(a + m, ml)
        s_new = jnp.exp(a + m - m_new)[..., None, None] * s + jnp.exp(ml - m_new)[..., None, None] * loc
        return (s_new, m_new), (s, m)

    (s_fin, m_fin), (s_prev, m_prev) = lax.scan(
        step, (s0.astype(f32), m0.astype(f32)),
        (last.transpose(1, 0, 2), m_loc.transpose(1, 0, 2), local.transpose(1, 0, 2, 3, 4)))
    if not need_y:
        return None, None, (s_fin, m_fin)
    s_prev = s_prev.transpose(1, 0, 2, 3, 4)
    m_prev = m_prev.transpose(1, 0, 2)
    inter = cum + m_prev[..., None]
    causal = jnp.tril(jnp.ones((CHUNK, CHUNK), dtype=bool))
    dlog = jnp.where(causal, cum[..., :, None] - cum[..., None, :] + w[..., None, :], -jnp.inf)
    m_row = jnp.maximum(inter, jnp.max(dlog, axis=-1))
    p = jnp.exp(dlog - m_row[..., None]) * jnp.einsum('bcthk,bcshk->bchts', q, k)
    y = (jnp.einsum('bchts,bcshv->bcthv', p, v)
         + jnp.einsum('bcthk,bchvk->bcthv', q, s_prev)
         * jnp.exp(inter - m_row).transpose(0, 1, 3, 2)[..., None])
    return (y.reshape(bsz, L, H, dv), m_row.transpose(0, 1, 3, 2).reshape(bsz, L, H), (s_fin, m_fin))


def bidir_scan(q, k, v, log_a, log_w, init_f, init_b, need_y=True):
    fl = lambda t: jnp.flip(t, axis=1)
    fwd = chunked_scan(q, k, v, log_a[:, :, 0], log_w[:, :, 0], *init_f, need_y=need_y)
    bwd = chunked_scan(fl(q), fl(k), fl(v), fl(log_a[:, :, 1]), fl(log_w[:, :, 1]), *init_b, need_y=need_y)
    if need_y:
        bwd = (fl(bwd[0]), fl(bwd[1]), bwd[2])
    return fwd, bwd


def ssd_prep(u, conv_w, conv_b, dt_bias, a_log):
    bsz, L, _ = u.shape
    xbc, z, dt_raw = jnp.split(u, [SSD_CONV_CH, SSD_CONV_CH + SSD_INNER], axis=-1)
    xbc = jax.nn.silu(dwconv(xbc, conv_w, conv_b))
    xs, bm, cm = jnp.split(xbc, [SSD_INNER, SSD_INNER + SSD_GROUPS * SSD_STATE], axis=-1)
    xs = xs.reshape(bsz, L, SSD_HEADS, SSD_HEADDIM)
    rep = SSD_HEADS // SSD_GROUPS
    bm = jnp.repeat(bm.reshape(bsz, L, SSD_GROUPS, SSD_STATE), rep, axis=2)
    cm = jnp.repeat(cm.reshape(bsz, L, SSD_GROUPS, SSD_STATE), rep, axis=2)
    dt = jax.nn.softplus(dt_raw.astype(jnp.float32).reshape(bsz, L, 2, SSD_HEADS) + dt_bias.astype(jnp.float32))
    log_a = -dt * jnp.exp(a_log.astype(jnp.float32))
    return xs, bm, cm, z, dt, log_a


def ssd_out(xs, z, fwd, bwd, d_skip):
    bsz, L = xs.shape[:2]
    y = (fwd[0] * jnp.exp(fwd[1])[..., None] + bwd[0] * jnp.exp(bwd[1])[..., None]
         + d_skip.astype(jnp.float32)[:, None] * xs.astype(jnp.float32))
    return y.reshape(bsz, L, SSD_INNER).astype(z.dtype) * jax.nn.silu(z)


def ssd_branch(u_ctx, u_lat, need_ctx, conv_w, conv_b, dt_bias, a_log, d_skip, norm_w):
    bsz = u_lat.shape[0]
    zero = (jnp.zeros((bsz, SSD_HEADS, SSD_HEADDIM, SSD_STATE), jnp.float32),
            jnp.zeros((bsz, SSD_HEADS), jnp.float32))
    xc, bc, cc, zc, dtc, lac = ssd_prep(u_ctx, conv_w, conv_b, dt_bias, a_log)
    fc, bwc = bidir_scan(cc, bc, xc, lac, jnp.log(dtc), zero, zero, need_y=need_ctx)
    y_ctx = rmsnorm(ssd_out(xc, zc, fc, bwc, d_skip), norm_w) if need_ctx else None
    xl, bl, cl, zl, dtl, lal = ssd_prep(u_lat, conv_w, conv_b, dt_bias, a_log)
    fw, bw = bidir_scan(cl, bl, xl, lal, jnp.log(dtl), fc[2], bwc[2])
    return y_ctx, rmsnorm(ssd_out(xl, zl, fw, bw, d_skip), norm_w)


def ml_prep(u, conv_w, conv_b, gate_b):
    bsz, L, _ = u.shape
    qk, vv, o, gates = jnp.split(u, [2 * ML_INNER, 3 * ML_INNER, 4 * ML_INNER], axis=-1)
    qk = jax.nn.silu(dwconv(qk, conv_w, conv_b))
    q, k = jnp.split(qk, 2, axis=-1)
    q = q.reshape(bsz, L, ML_HEADS, ML_HEADDIM)
    k = k.reshape(bsz, L, ML_HEADS, ML_HEADDIM) * (ML_HEADDIM ** -0.5)
    vv = vv.reshape(bsz, L, ML_HEADS, ML_HEADDIM)
    v_aug = jnp.concatenate([vv, jnp.ones_like(vv[..., :1])], axis=-1)
    gates = gates.astype(jnp.float32).reshape(bsz, L, 2, 2, ML_HEADS) + gate_b.astype(jnp.float32)
    log_w = gates[:, :, :, 0]
    log_a = jax.nn.log_sigmoid(gates[:, :, :, 1])
    return q, k, v_aug, o, log_a, log_w


def ml_out(o, fwd, bwd, norm_w):
    bsz, L = o.shape[:2]

    def cell(r):
        y, m, _ = r
        return y[..., :ML_HEADDIM] / jnp.maximum(jnp.abs(y[..., ML_HEADDIM:]), jnp.exp(-m)[..., None])

    h = cell(fwd) + cell(bwd)
    h = rmsnorm(h, norm_w.reshape(ML_HEADS, ML_HEADDIM)).reshape(bsz, L, ML_INNER)
    return (jax.nn.sigmoid(o.astype(jnp.float32)) * h).astype(o.dtype)


def ml_branch(u_ctx, u_lat, rows, need_ctx, conv_w, conv_b, gate_b, norm_w):
    bsz = u_lat.shape[0]
    zero = (jnp.zeros((bsz, ML_HEADS, ML_HEADDIM + 1, ML_HEADDIM), jnp.float32),
            jnp.zeros((bsz, ML_HEADS), jnp.float32))
    qc, kc, vc, oc, lac, lwc = ml_prep(u_ctx, conv_w, conv_b, gate_b)
    fc, bwc = bidir_scan(qc, kc, vc, lac, lwc, zero, zero, need_y=need_ctx)
    y_ctx = ml_out(oc, fc, bwc, norm_w) if need_ctx else None
    ql, kl, vl, ol, lal, lwl = ml_prep(to_colmajor(u_lat, rows), conv_w, conv_b, gate_b)
    fw, bw = bidir_scan(ql, kl, vl, lal, lwl, fc[2], bwc[2])
    return y_ctx, from_colmajor(ml_out(ol, fw, bw, norm_w), rows)


def hyena_filters(L, w1, b1, w2, b2, w3, decay):
    f32 = jnp.float32
    t = jnp.arange(L, dtype=f32)
    t_norm = t / L
    bands = jnp.linspace(1e-4, HY_BANDS - 1, HY_BANDS, dtype=f32)
    ang = (2.0 * math.pi / L) * t[:, None] * bands[None, :]
    feats = jnp.concatenate([t_norm[:, None], jnp.cos(ang), -jnp.sin(ang)], axis=-1)
    hid = jnp.sin(feats @ w1.astype(f32) + b1.astype(f32))
    hid = jnp.sin(hid @ w2.astype(f32) + b2.astype(f32))
    h = (hid @ w3.astype(f32)).reshape(L, HY_ORDER, 2, HY_INNER)
    h = h * jnp.exp(-t_norm[:, None, None, None] * jnp.abs(decay.astype(f32)))
    h_f, h_b = h[:, :, 0], h[:, :, 1]
    filt = jnp.concatenate([h_f.at[0].add(h_b[0]), jnp.zeros((1, HY_ORDER, HY_INNER), f32),
                            jnp.flip(h_b[1:], axis=0)], axis=0)
    return jnp.fft.rfft(filt, axis=0)


def fftconv(u, filt_f):
    L = u.shape[1]
    uf = jnp.fft.rfft(u.astype(jnp.float32), n=2 * L, axis=1)
    return jnp.fft.irfft(uf * filt_f[None], n=2 * L, axis=1)[:, :L]


def hyena_mix(u, conv_w, conv_b, w1, b1, w2, b2, w3, decay, skip):
    L = u.shape[1]
    filt_f = hyena_filters(L, w1, b1, w2, b2, w3, decay)
    z, x1, x2 = jnp.split(dwconv(u, conv_w, conv_b).astype(jnp.float32), 3, axis=-1)
    skip = skip.astype(jnp.float32)
    for n, gate in enumerate((x1, x2)):
        z = gate * (fftconv(z, filt_f[:, n]) + skip[n] * z)
    return z.astype(u.dtype)


def merge(g, ys, w_branch, w_out):
    bsz, L, _ = g.shape
    gate = jax.nn.sigmoid(g.reshape(bsz, L, N_BRANCH, D_MODEL))
    p = jnp.einsum('blnw,nwd->blnd', jnp.stack(ys, axis=2), w_branch)
    return jnp.sum(gate * p, axis=2) @ w_out


def token_mixer(h_ctx, h_lat, rows, need_ctx, w_in, ssd_p, ml_p, hy_p, w_branch, w_out):
    p_lat = h_lat @ w_in
    p_ctx = h_ctx @ (w_in if need_ctx else w_in[:, :REC_COLS])
    ssd_l, ml_l, hy_l, g_l = jnp.split(p_lat, [SSD_COLS, REC_COLS, REC_COLS + HY_COLS], axis=-1)
    ys_c, ys_l = ssd_branch(p_ctx[..., :SSD_COLS], ssd_l, need_ctx, *ssd_p)
    ym_c, ym_l = ml_branch(p_ctx[..., SSD_COLS:REC_COLS], ml_l, rows, need_ctx, *ml_p)
    yh_l = hyena_mix(hy_l, *hy_p)
    out_lat = merge(g_l, (ys_l, ym_l, yh_l), w_branch, w_out)
    if need_ctx:
        yh_c = hyena_mix(p_ctx[..., REC_COLS:REC_COLS + HY_COLS], *hy_p)
        out_ctx = merge(p_ctx[..., REC_COLS + HY_COLS:], (ys_c, ym_c, yh_c), w_branch, w_out)
    else:
        out_ctx = None
    return out_ctx, out_lat


def sq_relu_mlp(h, w1, w2):
    return jnp.square(jax.nn.relu(h @ w1)) @ w2


def setup_inputs(seed: int = 0) -> dict:
    key = jax.random.key(seed)
    ks = iter(jax.random.split(key, 48))
    f32 = jnp.float32

    def nrm(shape, scale):
        return scale * jax.random.normal(next(ks), shape, f32)

    def unif(shape, lo, hi):
        return jax.random.uniform(next(ks), shape, f32, lo, hi)

    D = D_MODEL
    x = nrm((BATCH, SEQ, D), 1.0)
    c = nrm((BATCH, D), 1.0)
    ctx = nrm((BATCH, CTX_LEN, D), 1.0)
    c_ctx = nrm((D,), 1.0)
    norm1_w = 1.0 + nrm((DEPTH, D), 0.02)
    mod_w = nrm((DEPTH, D, 6 * D), 0.5 * D ** -0.5)
    mod_b = nrm((DEPTH, 6 * D), 0.01)
    w_in = nrm((DEPTH, D, IN_COLS), D ** -0.5)
    ssd_conv_w = nrm((DEPTH, SSD_CONV, SSD_CONV_CH), SSD_CONV ** -0.5)
    ssd_conv_b = nrm((DEPTH, SSD_CONV_CH), 0.01)
    dt0 = jnp.exp(unif((DEPTH, 2, SSD_HEADS), math.log(1e-3), math.log(1e-1)))
    ssd_dt_bias = dt0 + jnp.log(-jnp.expm1(-dt0))
    ssd_a_log = jnp.log(unif((DEPTH, 2, SSD_HEADS), 1.0, 16.0))
    ssd_d = 1.0 + nrm((DEPTH, SSD_HEADS), 0.1)
    ssd_norm_w = 1.0 + nrm((DEPTH, SSD_INNER), 0.02)
    ml_conv_w = nrm((DEPTH, ML_CONV, 2 * ML_INNER), ML_CONV ** -0.5)
    ml_conv_b = nrm((DEPTH, 2 * ML_INNER), 0.01)
    ml_gate_b = nrm((DEPTH, 2, 2, ML_HEADS), 0.1).at[:, :, 1].add(jnp.linspace(3.0, 6.0, ML_HEADS, dtype=f32))
    ml_norm_w = 1.0 + nrm((DEPTH, ML_INNER), 0.02)
    hy_conv_w = nrm((DEPTH, HY_SHORT, HY_COLS), HY_SHORT ** -0.5)
    hy_conv_b = nrm((DEPTH, HY_COLS), 0.01)
    hy_ffn_w1 = nrm((DEPTH, HY_FEAT, HY_FFN), HY_FEAT ** -0.5)
    hy_ffn_b1 = nrm((DEPTH, HY_FFN), 0.1)
    hy_ffn_w2 = nrm((DEPTH, HY_FFN, HY_FFN), HY_FFN ** -0.5)
    hy_ffn_b2 = nrm((DEPTH, HY_FFN), 0.1)
    hy_ffn_w3 = nrm((DEPTH, HY_FFN, HY_ORDER * 2 * HY_INNER), 0.1 * HY_FFN ** -0.5)
    hy_decay = unif((DEPTH, HY_ORDER, 2, HY_INNER), 3.0, 15.0)
    hy_skip = nrm((DEPTH, HY_ORDER, HY_INNER), 1.0)
    w_branch = nrm((DEPTH, N_BRANCH, BRANCH_W, D), BRANCH_W ** -0.5)
    w_out = nrm((DEPTH, D, D), D ** -0.5)
    norm2_w = 1.0 + nrm((DEPTH, D), 0.02)
    mlp_w1 = nrm((DEPTH, D, MLP_HIDDEN), D ** -0.5)
    mlp_w2 = nrm((DEPTH, MLP_HIDDEN, D), MLP_HIDDEN ** -0.5)
    norm_f_w = 1.0 + nrm((D,), 0.02)
    return {'x': x, 'c': c, 'ctx': ctx, 'c_ctx': c_ctx, 'norm1_w': norm1_w, 'mod_w': mod_w,
            'mod_b': mod_b, 'w_in': w_in, 'ssd_conv_w': ssd_conv_w, 'ssd_conv_b': ssd_conv_b,
            'ssd_dt_bias': ssd_dt_bias, 'ssd_a_log': ssd_a_log, 'ssd_d': ssd_d, 'ssd_norm_w': ssd_norm_w,
            'ml_conv_w': ml_conv_w, 'ml_conv_b': ml_conv_b, 'ml_gate_b': ml_gate_b, 'ml_norm_w': ml_norm_w,
            'hy_conv_w': hy_conv_w, 'hy_conv_b': hy_conv_b, 'hy_ffn_w1': hy_ffn_w1, 'hy_ffn_b1': hy_ffn_b1,
            'hy_ffn_w2': hy_ffn_w2, 'hy_ffn_b2': hy_ffn_b2, 'hy_ffn_w3': hy_ffn_w3, 'hy_decay': hy_decay,
            'hy_skip': hy_skip, 'w_branch': w_branch, 'w_out': w_out, 'norm2_w': norm2_w,
            'mlp_w1': mlp_w1, 'mlp_w2': mlp_w2, 'norm_f_w': norm_f_w}


def reference(x, c, ctx, c_ctx, norm1_w, mod_w, mod_b, w_in, ssd_conv_w, ssd_conv_b, ssd_dt_bias,
              ssd_a_log, ssd_d, ssd_norm_w, ml_conv_w, ml_conv_b, ml_gate_b, ml_norm_w, hy_conv_w,
              hy_conv_b, hy_ffn_w1, hy_ffn_b1, hy_ffn_w2, hy_ffn_b2, hy_ffn_w3, hy_decay, hy_skip,
              w_branch, w_out, norm2_w, mlp_w1, mlp_w2, norm_f_w):
    rows = x.shape[1] // GRID_W
    xc = ctx
    for l in range(DEPTH):
        need_ctx = l < DEPTH - 1
        mod_lat = jnp.split((jax.nn.silu(c) @ mod_w[l] + mod_b[l])[:, None, :], 6, axis=-1)
        mod_ctx = jnp.split((jax.nn.silu(c_ctx) @ mod_w[l] + mod_b[l])[None, None, :], 6, axis=-1)
        h_lat = modulate(rmsnorm(x, norm1_w[l]), mod_lat[0], mod_lat[1])
        h_ctx = modulate(rmsnorm(xc, norm1_w[l]), mod_ctx[0], mod_ctx[1])
        o_ctx, o_lat = token_mixer(
            h_ctx, h_lat, rows, need_ctx, w_in[l],
            (ssd_conv_w[l], ssd_conv_b[l], ssd_dt_bias[l], ssd_a_log[l], ssd_d[l], ssd_norm_w[l]),
            (ml_conv_w[l], ml_conv_b[l], ml_gate_b[l], ml_norm_w[l]),
            (hy_conv_w[l], hy_conv_b[l], hy_ffn_w1[l], hy_ffn_b1[l], hy_ffn_w2[l], hy_ffn_b2[l],
             hy_ffn_w3[l], hy_decay[l], hy_skip[l]),
            w_branch[l], w_out[l])
        x = x + mod_lat[2] * o_lat
        x = x + mod_lat[5] * sq_relu_mlp(modulate(rmsnorm(x, norm2_w[l]), mod_lat[3], mod_lat[4]),
                                         mlp_w1[l], mlp_w2[l])
        if need_ctx:
            xc = xc + mod_ctx[2] * o_ctx
            xc = xc + mod_ctx[5] * sq_relu_mlp(modulate(rmsnorm(xc, norm2_w[l]), mod_ctx[3], mod_ctx[4]),
                                               mlp_w1[l], mlp_w2[l])
    return rmsnorm(x, norm_f_w)
```

```python
import contextlib
import math
import numpy as np
import concourse.bass as bass
import concourse.mybir as mybir
from concourse.bass_utils import run_bass_kernel_spmd

F32 = mybir.dt.float32
BF16 = mybir.dt.bfloat16
AF = mybir.ActivationFunctionType
ALU = mybir.AluOpType

D = 1024
TC = 256
TL = 2048
T = TC + TL
NCH = T // 128
DEPTH = 2
EPS = 1e-6
CT0 = 1
LT0 = 259
TP = 2308
SSD_COLS = 2592
ML0 = 2592
REC_COLS = 6720
HY0 = 6720
G0 = 9792
IN_COLS = 12864
NEG = -30000.0

EPOCH = 30000
SYNC_SAME = ("act", "dve", "pool")


class Buf:
    __slots__ = ("name", "w", "r", "excl")

    def __init__(self, name="", excl=False):
        self.name = name
        self.w = None
        self.r = []
        self.excl = excl


class Prog:
    ENG = ("pe", "act", "dve", "pool", "sp")

    def __init__(self, nc, st, n_dma_sems=56):
        self.nc = nc
        self.st = st
        self.streams = {e: [] for e in self.ENG}
        self.cnt = {e: 0 for e in self.ENG}
        self.cur_sem = {}
        self.seen = {e: {} for e in self.ENG}
        for e in ("pe", "act", "dve", "pool"):
            self.cur_sem[e] = self._new_sem()
        self.dma_sems = [self._new_sem() for _ in range(n_dma_sems)]
        self.dma_val = [0] * n_dma_sems
        self.dma_rr = 0
        self.n_sw = 12
        self.dma_rr_sw = 0
        self.qrr = 0
        self.nops = 0

    def _new_sem(self):
        self.nsem = getattr(self, "nsem", 0) + 1
        return self.st.enter_context(self.nc.semaphore("sem%d" % self.nsem))

    def _need(self, eng, tok, waits, raw=True):
        if tok is None:
            return
        sem, val, src = tok
        if src == eng and (eng not in SYNC_SAME or not raw):
            return
        k = id(sem)
        if self.seen[eng].get(k, 0) >= val:
            return
        self.seen[eng][k] = val
        waits.append((sem, val))

    def _deps(self, eng, reads, writes):
        waits = []
        for b in reads:
            self._need(eng, b.w, waits)
        for b in writes:
            self._need(eng, b.w, waits)
            for t in b.r:
                self._need(eng, t, waits)
        return waits

    def _commit(self, tok, reads, writes):
        for b in reads:
            b.r.append(tok)
            if len(b.r) > 48:
                last = {}
                for t in b.r:
                    last[(id(t[0]))] = t if (id(t[0]) not in last or last[id(t[0])][1] < t[1]) else last[id(t[0])]
                b.r = list(last.values())
        for b in writes:
            b.w = tok
            b.r = []

    def op(self, eng, fn, reads=(), writes=()):
        writes = list(writes) + [b for b in reads if b.excl and b not in writes]
        reads = [b for b in reads if not b.excl]
        waits = self._deps(eng, reads, writes)
        if self.cnt[eng] >= EPOCH:
            self.cur_sem[eng] = self._new_sem()
            self.cnt[eng] = 0
        self.cnt[eng] += 1
        sem = self.cur_sem[eng]
        tok = (sem, self.cnt[eng], eng)
        self.streams[eng].append((waits, fn, (sem, 1)))
        self._commit(tok, reads, writes)
        self.nops += 1
        return tok

    def dma(self, q, fn, reads=(), writes=()):
        if q is None:
            q = "sp"
        waits = self._deps(q, reads, writes)
        if q == "pool":
            i = self.dma_rr_sw
            self.dma_rr_sw = (self.dma_rr_sw + 1) % self.n_sw
        else:
            i = self.n_sw + self.dma_rr
            self.dma_rr = (self.dma_rr + 1) % (len(self.dma_sems) - self.n_sw)
        sem = self.dma_sems[i]
        if self.dma_val[i] > 0:
            self._need(q, (sem, self.dma_val[i], "dma"), waits)
        self.dma_val[i] += 16
        tok = (sem, self.dma_val[i], "dma")
        self.streams[q].append((waits, fn, (sem, 16)))
        self._commit(tok, reads, writes)
        self.nops += 1
        return tok

    def finish(self, eng="sp"):
        waits = []
        for e in ("pe", "act", "dve", "pool"):
            if self.cnt[e] > 0:
                self._need(eng, (self.cur_sem[e], self.cnt[e], e), waits)
        for i, s in enumerate(self.dma_sems):
            if self.dma_val[i] > 0:
                self._need(eng, (s, self.dma_val[i], "dma"), waits)
        self.streams[eng].append((waits, None, None))

    def emit(self):
        nc = self.nc
        engmap = {"pe": "tensor", "act": "scalar", "dve": "vector", "pool": "gpsimd", "sp": "sync"}
        with nc.Block() as block:
            for e in self.ENG:
                stream = self.streams[e]
                if not stream:
                    continue

                def body(eng, stream=stream):
                    for waits, fn, inc in stream:
                        for (s, v) in waits:
                            eng.wait_ge(s, v)
                        if fn is not None:
                            ins = fn(eng)
                            if inc is not None:
                                ins.then_inc(inc[0], inc[1])
                getattr(block, engmap[e])(body)


_FREED = {}
_SCOPES = []


def newbuf(name="", excl=False):
    b = Buf(name, excl)
    b.r = list(_FREED.values())
    if _SCOPES:
        _SCOPES[-1].append(b)
    return b


@contextlib.contextmanager
def scope():
    bufs = []
    _SCOPES.append(bufs)
    with contextlib.ExitStack() as lst:
        yield lst
    _SCOPES.pop()
    for b in bufs:
        for t in ([b.w] if b.w is not None else []) + list(b.r):
            k = id(t[0])
            if k not in _FREED or _FREED[k][1] < t[1]:
                _FREED[k] = t


class Rot:
    def __init__(self, st, alloc, name, shape, dtype, n, excl=False):
        self.tiles = [st.enter_context(alloc("%s%d" % (name, i), shape, dtype)) for i in range(n)]
        self.bufs = [newbuf("%s%d" % (name, i), excl) for i in range(n)]
        self.i = 0

    def next(self):
        i = self.i
        self.i = (self.i + 1) % len(self.tiles)
        return self.tiles[i], self.bufs[i]


PP_N1W, PP_N2W, PP_NF, PP_MODB, PP_SCB, PP_SCW, PP_MCB, PP_MCW, PP_SNW, PP_MNW, PP_HB1, PP_HB2, PP_N = (
    0, 16, 32, 40, 136, 160, 280, 312, 472, 488, 504, 506, 512)
RP_DTB, RP_ALOG, RP_DSK, RP_MGB, RP_HCB, RP_HCW, RP_HDEC, RP_HSKIP, RP_N = (
    0, 32, 64, 1088, 1120, 4192, 13408, 17504, 19552)
CS_TRIU, CS_TRIL, CS_MASKF, CS_MASKB, CS_IDENT, CS_ONES, CS_N = 0, 128, 256, 384, 512, 640, 768


def _wp(L):
    return ((L + 1 + 127) // 128) * 128


class Builder:
    def __init__(self, debug=None, nlayers=DEPTH):
        self.debug = debug or {}
        self.nlayers = nlayers
        self.nc = bass.Bass("TRN2", target_bir_lowering=False)
        self.dbg_out = {}
        _FREED.clear()
        del _SCOPES[:]

    def din(self, name, shape, dt=F32):
        return self.nc.dram_tensor(name, list(shape), dt, kind="ExternalInput").ap()

    def dscr(self, name, shape, dt=F32):
        return self.nc.dram_tensor(name, list(shape), dt, kind="Internal").ap()

    def dout(self, name, shape, dt=F32):
        return self.nc.dram_tensor(name, list(shape), dt, kind="ExternalOutput").ap()

    def mm(self, out, lhsT, rhs, start, stop, reads, writes, skip=False):
        self.P.op("pe", lambda e: e.matmul(out, lhsT=lhsT, rhs=rhs, start=start, stop=stop, skip_group_check=skip),
                  reads, writes)

    def tr(self, out, in_, reads, writes):
        ident = self.identb[:]
        self.P.op("pe", lambda e: e.matmul(out, lhsT=in_, rhs=ident, start=True, stop=True), reads + [self.Bconst], writes)

    def act(self, out, in_, func, reads, writes, **kw):
        self.P.op("act", lambda e: e.activation(out=out, in_=in_, func=func, **kw), reads, writes)

    def tt(self, eng, out, in0, in1, op, reads, writes):
        self.P.op(eng, lambda e: e.tensor_tensor(out=out, in0=in0, in1=in1, op=op), reads, writes)

    def ts(self, eng, out, in0, s1, s2, op0, op1, reads, writes):
        if op1 is None:
            self.P.op(eng, lambda e: e.tensor_scalar(out=out, in0=in0, scalar1=s1, scalar2=None, op0=op0), reads, writes)
        else:
            self.P.op(eng, lambda e: e.tensor_scalar(out=out, in0=in0, scalar1=s1, scalar2=s2, op0=op0, op1=op1),
                      reads, writes)

    def stt(self, out, in0, scalar, in1, op0, op1, reads, writes):
        self.P.op("dve", lambda e: e.scalar_tensor_tensor(out=out, in0=in0, scalar=scalar, in1=in1, op0=op0, op1=op1),
                  reads, writes)

    def cp(self, eng, out, in_, reads, writes):
        self.P.op(eng, lambda e: e.tensor_copy(out=out, in_=in_), reads, writes)

    def recip(self, out, in_, reads, writes):
        self.P.op("dve", lambda e: e.reciprocal(out=out, in_=in_), reads, writes)

    def memset(self, eng, ap, val, writes):
        self.P.op(eng, lambda e: e.memset(ap, val), [], writes)

    def dma(self, q, out, in_, reads, writes):
        self.P.dma(q, lambda e: e.dma_start(out=out, in_=in_), reads, writes)

    def sbt(self, name, shape, dt):
        self.uid = getattr(self, "uid", 0) + 1
        return self.nc.sbuf_tensor("%s_%d" % (name, self.uid), list(shape), dt)

    def sb(self, name, shape, dt):
        return self.st.enter_context(self.sbt("sb_" + name, list(shape), dt))

    def dump(self, name, ap_sb, shape, buf, dt=F32):
        o = self.dout("dbg_" + name, shape, dt)
        self.dbg_out[name] = ("dbg_" + name, tuple(shape))
        self.dma("sp", o, ap_sb, [buf], [Buf()])

    def hcols(self, k, c, perm=False):
        if c < 2:
            return self.hT[:, k, CT0 + 128 * c: CT0 + 128 * c + 128]
        cl = c - 2
        if not perm:
            return self.hT[:, k, LT0 + 128 * cl: LT0 + 128 * cl + 128]
        return self.hTp[:, k, 128 * cl:128 * cl + 128]

    def htile(self, k, ti, perm=False):
        if ti == 0:
            return self.hT[:, k, CT0:CT0 + TC], TC
        q = ti - 1
        if not perm:
            return self.hT[:, k, LT0 + 512 * q: LT0 + 512 * q + 512], 512
        return self.hTp[:, k, 512 * q:512 * q + 512], 512

    def load_w(self, src, n, q="pool"):
        t, b = self.wrot.next()
        self.dma(q, t[:, :, 0:n], src.rearrange("(k p) n -> p k n", p=128), [], [b])
        return t, b

    def build(self):
        nc = self.nc
        with contextlib.ExitStack() as st:
            self.st = st
            self.P = Prog(nc, st)
            self.declare()
            self.setup()
            stop = self.debug.get("stop")
            for l in range(self.nlayers):
                need_ctx = l < DEPTH - 1
                self.stage_mod(l)
                self.stage_norm(l, first=True)
                if stop == "norm%d" % l:
                    self.dump("hT", self.hT[:], [128, 8, TP], self.BhT, BF16)
                    break
                if not self.debug.get("skip_scan"):
                    self.stage_scan(l, "ssd", need_ctx)
                    if stop == "ssd%d" % l:
                        break
                    self.stage_scan(l, "ml", need_ctx)
                    if stop == "ml%d" % l:
                        break
                self.stage_hyena(l, need_ctx)
                if stop == "hy%d" % l:
                    break
                self.stage_merge(l, need_ctx)
                if stop == "merge%d" % l:
                    break
                self.stage_norm(l, first=False, need_ctx=need_ctx)
                self.stage_mlp(l, need_ctx)
                if stop == "mlp%d" % l:
                    break
            else:
                self.stage_final()
            dmp = self.debug.get("dump")
            if dmp:
                srcs = {"ybT0": (self.ybT[0], [D, T], BF16, self.BybT[0]), "ybT1": (self.ybT[1], [D, T], BF16, self.BybT[1]),
                        "ybT2": (self.ybT[2], [D, T], BF16, self.BybT[2]), "xres": (self.xres, [D, T], F32, self.Bxres),
                        "hyK": (self.hyK, [2, _wp(TL), 2048], F32, self.BhyK),
                        "ys_tok": (self.ys_tok, [T, D], F32, self.Bys)}
                src, shp, dt, bf = srcs[dmp]
                o = self.dout("dbg_" + dmp, shp, dt)
                if len(shp) == 3:
                    o = o.rearrange("a r c -> (a r) c")
                    src = src.rearrange("a r c -> (a r) c")
                for r0 in range(0, o.shape[0], 128):
                    self.dma("sp", o[r0:r0 + 128, :], src[r0:r0 + 128, :], [bf], [Buf()])
            self.P.finish("sp")
            self.P.emit()
        return nc

    def declare(self):
        s = self
        s.xT = s.din("xT", [D, T])
        s.cvec = s.din("cvec", [128, 16])
        s.mod_w = s.din("mod_w", [DEPTH, D, 6 * D])
        s.w_in = s.din("w_in", [DEPTH, D, IN_COLS])
        s.w_branch = s.din("w_branch", [DEPTH, 3, D, D])
        s.w_out = s.din("w_out", [DEPTH, D, D])
        s.mlp_w1 = s.din("mlp_w1", [DEPTH, D, 4 * D])
        s.mlp_w2 = s.din("mlp_w2", [DEPTH, 4 * D, D])
        s.pp_d = s.din("pp", [128, PP_N])
        s.rowp = s.din("rowp", [DEPTH, RP_N])
        s.hyw1 = s.din("hyw1", [DEPTH, 33, 64])
        s.hyw2 = s.din("hyw2", [DEPTH, 64, 64])
        s.hyw3 = s.din("hyw3", [DEPTH, 64, 4096])
        s.cst_d = s.din("cst", [128, CS_N])
        s.sel_d = s.din("sel", [16, 16 * 128])
        s.feats = {TL: s.din("featsL", [33, TL]), TC: s.din("featsC", [33, TC])}
        s.tneg_d = s.din("tneg", [128, 18])
        s.dft = {}
        for L, nm in ((TL, "L"), (TC, "C")):
            nw = _wp(L) // 128
            nt = L // 128
            s.dft[L] = dict(
                ct=s.din("ct" + nm, [nw, 128, nt * 128], BF16), st=s.din("st" + nm, [nw, 128, nt * 128], BF16),
                ci=s.din("ci" + nm, [nt, 128, nw * 128], BF16), si=s.din("si" + nm, [nt, 128, nw * 128], BF16))
        s.out = s.dout("out", [D, TL])
        s.xres = s.dscr("xres", [D, T])
        s.ys_tok = s.dscr("ys_tok", [T, D])
        s.ybT = [s.dscr("ybT%d" % n, [D, T], BF16) for n in range(3)]
        s.hsd = s.dscr("hsd", [2, TL, 2048], BF16)
        s.hyK = s.dscr("hyK", [2, _wp(TL), 2048])
        s.Bxres = Buf("xres")
        s.Bys = Buf("ys_tok")
        s.BybT = [Buf("ybT%d" % n) for n in range(3)]
        s.Bhsd = Buf("hsd")
        s.BhyK = Buf("hyK")

    def setup(self):
        s = self
        nc, st = s.nc, s.st
        s.ps = Rot(st, nc.psum_tensor, "ps", [128, 512], F32, 4, excl=True)
        s.pacc = Rot(st, nc.psum_tensor, "pacc", [128, 512], F32, 4, excl=True)
        s.Bconst = Buf("const")
        s.cst = s.sb("cst", [128, CS_N], F32)
        s.sel = s.sb("sel", [16, 16 * 128], F32)
        s.pp = s.sb("pp", [128, PP_N], F32)
        s.identb = s.sb("identb", [128, 128], BF16)
        s.tneg = s.sb("tneg", [128, 18], F32)
        s.csil = s.sb("csil", [128, 16], F32)
        s.mod = s.sb("mod", [128, 96], F32)
        s.gv = s.sb("gv", [128, 64], F32)
        s.Bmod = Buf("mod")
        s.hT = s.sb("hT", [128, 8, TP], BF16)
        s.BhT = Buf("hT")
        s.wrot = Rot(st, nc.sbuf_tensor, "wr", [128, 8, 512], BF16, 3)
        c = [s.Bconst]
        s.dma("sp", s.cst[:], s.cst_d, [], c)
        s.dma("sp", s.sel[:], s.sel_d, [], c)
        s.dma("sp", s.pp[:], s.pp_d, [], c)
        s.dma("sp", s.tneg[:], s.tneg_d, [], c)
        s.dma("sp", s.csil[:], s.cvec, [], c)
        s.dma("pool", s.identb[:], s.cst_d[:, CS_IDENT:CS_IDENT + 128], [], c)
        s.act(s.csil[:], s.csil[:], AF.Silu, c, c)
        s.memset("dve", s.hT[:], 0.0, [s.BhT])
        for k in range(8):
            s.dma(None, s.xres[k * 128:(k + 1) * 128, :], s.xT[k * 128:(k + 1) * 128, :], [], [s.Bxres])

    def triu(self):
        return self.cst[:, CS_TRIU:CS_TRIU + 128]

    def tril(self):
        return self.cst[:, CS_TRIL:CS_TRIL + 128]

    def ones(self):
        return self.cst[:, CS_ONES:CS_ONES + 128]

    def identf(self):
        return self.cst[:, CS_IDENT:CS_IDENT + 128]

    def stage_mod(self, l):
        s = self
        with scope() as lst:
            wm = [lst.enter_context(s.sbt("wm%d" % i, [128, 8, 512], F32)) for i in range(2)]
            wmb = [newbuf(), newbuf()]
            pt, pb = s.ps.next()
            src = s.mod_w[l].rearrange("(k p) n -> p k n", p=128)
            csv = s.csil[:, :].rearrange("p (k w) -> p k w", w=2)
            for blk in range(12):
                t, b = wm[blk % 2], wmb[blk % 2]
                s.dma(None, t[:], src[:, :, blk * 512:(blk + 1) * 512], [], [b])
                for jt in range(4):
                    j = blk * 4 + jt
                    for k in range(8):
                        s.mm(pt[:, 2 * j:2 * j + 2], t[:, k, jt * 128:(jt + 1) * 128], csv[:, k, :],
                             k == 0, k == 7, [b, s.Bconst], [pb])
            modv = s.mod[:, :].rearrange("p (j w) -> p j w", w=2)
            mb = s.pp[:, PP_MODB + l * 48:PP_MODB + l * 48 + 48].unsqueeze(2).broadcast_to([128, 48, 2])
            s.tt("dve", modv, pt[:, 0:96].rearrange("p (j w) -> p j w", w=2), mb, ALU.add, [pb, s.Bconst], [s.Bmod])
            for i, (m, npp) in enumerate(((1, PP_N1W), (4, PP_N2W))):
                gsl = s.gv[:, 16 * i:16 * i + 16].rearrange("p (j w) -> p j w", w=2)
                msl = s.mod[:, m * 16:(m + 1) * 16].rearrange("p (j w) -> p j w", w=2)
                nw = s.pp[:, npp + l * 8:npp + l * 8 + 8].unsqueeze(2).broadcast_to([128, 8, 2])
                s.stt(gsl, msl, 1.0, nw, ALU.add, ALU.mult, [s.Bmod, s.Bconst], [s.Bmod])

    def modcol(self, m, j, which):
        c = (m * 8 + j) * 2 + which
        return self.mod[:, c:c + 1]

    def gcol(self, i, j, which):
        c = 16 * i + j * 2 + which
        return self.gv[:, c:c + 1]

    def stage_norm(self, l, first, need_ctx=True):
        s = self
        gi = 0 if first else 1
        mshift = 0 if first else 3
        with scope() as lst:
            xt = Rot(lst, s.sbt, "nx", [128, 8, 512], F32, 2)
            sq = Rot(lst, s.sbt, "nsq", [128, 8, 512], F32, 1)
            rs = Rot(lst, s.sbt, "nrs", [128, 512], F32, 2)
            tm = Rot(lst, s.sbt, "ntm", [128, 512], F32, 2)
            for ti in range(5):
                if ti == 0 and not need_ctx:
                    continue
                n = TC if ti == 0 else 512
                t0 = 0 if ti == 0 else TC + 512 * (ti - 1)
                which = 1 if ti == 0 else 0
                x, xb = xt.next()
                s.dma(None, x[:, :, 0:n], s.xres[:, t0:t0 + n].rearrange("(k p) t -> p k t", p=128), [s.Bxres], [xb])
                q, qb = sq.next()
                s.act(q[:, :, 0:n], x[:, :, 0:n], AF.Square, [xb], [qb])
                pt, pb = s.ps.next()
                for k in range(8):
                    s.mm(pt[:, 0:n], s.ones(), q[:, k, 0:n], k == 0, k == 7, [qb, s.Bconst], [pb])
                r, rb = rs.next()
                s.act(r[:, 0:n], pt[:, 0:n], AF.Sqrt, [pb], [rb], scale=1.0 / D, bias=EPS)
                s.recip(r[:, 0:n], r[:, 0:n], [rb], [rb])
                for k in range(8):
                    t, tb = tm.next()
                    s.stt(t[:, 0:n], x[:, k, 0:n], s.gcol(gi, k, which), r[:, 0:n], ALU.mult, ALU.mult,
                          [xb, rb, s.Bmod], [tb])
                    dst, _ = s.htile(k, ti)
                    s.act(dst, t[:, 0:n], AF.Identity, [tb, s.Bmod], [s.BhT], bias=s.modcol(mshift, k, which))

    def stage_gates_prep(self, l):
        pass

    def conv_tile(self, l, col0, cwcol, cbcol, perm, cin, Bcin, acc, Bacc, dst, Bdst):
        s = self
        w, wb = s.load_w(s.w_in[l][:, col0:col0 + 128], 128)
        for ti in range(5):
            pt, pb = s.ps.next()
            n = TC if ti == 0 else 512
            for k in range(8):
                rhs, _ = s.htile(k, ti, perm)
                out = pt[:, 0:n]
                s.mm(out, w[:, k, 0:128], rhs, k == 0, k == 7, [wb, s.BhT], [pb])
            off = 2 if ti == 0 else 262 + 512 * (ti - 1)
            s.act(cin[:, off:off + n], pt[:, 0:n], AF.Copy, [pb], [Bcin])
        NV = 2308
        if s.debug.get("conv_stop") == "mm":
            return
        s.ts("dve", acc[:, 0:NV], cin[:, 0:NV], s.pp[:, cwcol:cwcol + 1], None, ALU.mult, None, [Bcin, s.Bconst], [Bacc])
        for k in range(1, 5):
            s.stt(acc[:, 0:NV], cin[:, k:k + NV], s.pp[:, cwcol + k:cwcol + k + 1], acc[:, 0:NV], ALU.mult, ALU.add,
                  [Bcin, Bacc, s.Bconst], [Bacc])
        if s.debug.get("conv_stop") == "dve":
            return
        s.act(dst[:, 0:TC], acc[:, 0:TC], AF.Silu, [Bacc, s.Bconst], [Bdst], bias=s.pp[:, cbcol:cbcol + 1])
        s.act(dst[:, TC:T], acc[:, 260:260 + TL], AF.Silu, [Bacc, s.Bconst], [Bdst], bias=s.pp[:, cbcol:cbcol + 1])

    def stage_scan(self, l, kind, need_ctx):
        s = self
        ssd = kind == "ssd"
        nhd = 16 if ssd else 8
        NG = 2 * nhd
        W = 512 if ssd else 129
        VW = 512 if ssd else 130
        nunits = 2 if ssd else 8
        sc = 1.0 if ssd else 128.0 ** -0.5
        first_c = 0 if need_ctx else 2
        with scope() as lst:
            def A(name, shape, dt, stack=None):
                return (stack or lst).enter_context(s.sbt("sc_" + kind + name, list(shape), dt))
            if not ssd:
                s.hTp = A("hTp", [128, 8, TL], BF16)
                for k in range(8):
                    src = s.hT[:, k, LT0:LT0 + TL].rearrange("p (r w) -> p w r", w=64)
                    dst = s.hTp[:, k, :].rearrange("p (w r) -> p w r", r=32)
                    if k % 2 == 0:
                        s.act(dst, src, AF.Copy, [s.BhT], [s.BhT])
                    else:
                        s.P.op("dve", (lambda o, i_: (lambda e: e.tensor_copy(out=o, in_=i_)))(dst, src), [s.BhT], [s.BhT])
            la = A("la", [128, NCH, NG], F32)
            wl = A("wl", [128, NCH, NG], F32)
            cum = A("cum", [128, NCH, NG], F32)
            tot = A("tot", [128, NCH, NG], F32)
            bias = A("bias", [128, NCH, NG], F32)
            ecum = A("ecum", [128, NCH, NG], F32)
            gst = A("gst", [128, NCH, NG], F32)
            atot = A("atot", [128, NCH, NG], F32)
            Bd = newbuf("decay")
            with scope() as pl_:
                tmpa = A("tmpa", [128, NCH, 32], F32, pl_)
                tmpb = A("tmpb", [128, NCH, 32], F32, pl_)
                rowb = A("rowb", [128, 64], F32, pl_)
                Bt = newbuf("dtmp")
                gcol0 = 2560 if ssd else ML0 + 4096
                wg, wgb = s.load_w(s.w_in[l][:, gcol0:gcol0 + 32], 32)
                if ssd:
                    s.dma("sp", rowb[:, 0:64], s.rowp[l:l + 1, RP_DTB:RP_DTB + 64].broadcast_to([128, 64]), [], [Bt])
                    s.act(rowb[:, 32:64], rowb[:, 32:64], AF.Exp, [Bt], [Bt])
                else:
                    s.dma("sp", rowb[:, 0:32], s.rowp[l:l + 1, RP_MGB:RP_MGB + 32].broadcast_to([128, 32]), [], [Bt])
                for half in range(2):
                    pt, pb = s.ps.next()
                    for ci in range(9):
                        c = half * 9 + ci
                        for k in range(8):
                            s.mm(pt[:, ci * 32:ci * 32 + 32], s.hcols(k, c, perm=not ssd), wg[:, k, 0:32], k == 0, k == 7,
                                 [wgb, s.BhT], [pb])
                    s.tt("dve", tmpa[:, half * 9:half * 9 + 9, :], pt[:, 0:288].rearrange("p (c g) -> p c g", g=32),
                         rowb[:, 0:32].unsqueeze(1).broadcast_to([128, 9, 32]), ALU.add, [pb, Bt], [Bt])
                if ssd:
                    s.act(tmpb[:], tmpa[:], AF.Exp, [Bt], [Bt])
                    s.act(tmpb[:], tmpb[:], AF.Ln, [Bt], [Bt], bias=1.0)
                    s.act(wl[:], tmpb[:], AF.Ln, [Bt], [Bd])
                    s.stt(la[:], tmpb[:], -1.0, rowb[:, 32:64].unsqueeze(1).broadcast_to([128, NCH, 32]), ALU.mult, ALU.mult,
                          [Bt], [Bd])
                else:
                    for d in range(2):
                        s.act(wl[:, :, d * 8:d * 8 + 8], tmpa[:, :, d * 16:d * 16 + 8], AF.Copy, [Bt], [Bd])
                        s.act(tmpb[:, :, d * 8:d * 8 + 8], tmpa[:, :, d * 16 + 8:d * 16 + 16], AF.Exp, [Bt], [Bt], scale=-1.0)
                    s.act(tmpb[:, :, 0:16], tmpb[:, :, 0:16], AF.Ln, [Bt], [Bt], bias=1.0)
                    s.ts("dve", la[:], tmpb[:, :, 0:16], -1.0, None, ALU.mult, None, [Bt], [Bd])
            for half in range(2):
                pc, pcb = s.ps.next()
                ptt, ptb = s.ps.next()
                for ci in range(9):
                    c = half * 9 + ci
                    o = ci * NG
                    s.mm(pc[:, o:o + nhd], s.triu(), la[:, c, 0:nhd], True, True, [Bd, s.Bconst], [pcb])
                    s.mm(pc[:, o + nhd:o + NG], s.tril(), la[:, c, nhd:NG], True, True, [Bd, s.Bconst], [pcb])
                    s.mm(ptt[:, o:o + NG], s.ones(), la[:, c, :], True, True, [Bd, s.Bconst], [ptb])
                s.act(cum[:, half * 9:half * 9 + 9, :], pc[:, 0:9 * NG].rearrange("p (c g) -> p c g", g=NG), AF.Copy,
                      [pcb], [Bd])
                s.act(tot[:, half * 9:half * 9 + 9, :], ptt[:, 0:9 * NG].rearrange("p (c g) -> p c g", g=NG), AF.Copy,
                      [ptb], [Bd])
            s.tt("dve", bias[:], wl[:], cum[:], ALU.subtract, [Bd], [Bd])
            s.act(ecum[:], cum[:], AF.Exp, [Bd], [Bd])
            s.tt("dve", gst[:], bias[:], tot[:], ALU.add, [Bd], [Bd])
            s.act(gst[:], gst[:], AF.Exp, [Bd], [Bd])
            s.act(atot[:], tot[:], AF.Exp, [Bd], [Bd])
            if s.debug.get("dump_decay") == kind:
                s.dump("la", la[:], [128, NCH, NG], Bd)
                s.dump("cum", cum[:], [128, NCH, NG], Bd)
                s.dump("wl", wl[:], [128, NCH, NG], Bd)
            if s.debug.get("scan_stop") == "decay":
                return
            QT = A("QT", [128, T], BF16)
            KT = A("KT", [128, T], BF16)
            Ktok = A("Ktok", [128, NCH, 128], BF16)
            V = A("V", [128, NCH, VW], BF16)
            Sin = [A("Sin0", [128, NCH, VW], BF16), A("Sin1", [128, NCH, VW], BF16)]
            S = [A("S0", [128, VW], F32), A("S1", [128, VW], F32)]
            BQT, BKT, BKtok, BV = newbuf(), newbuf(), newbuf(), newbuf()
            BSin = [newbuf(), newbuf()]
            BS = [newbuf(), newbuf()]
            ssq = A("ssq", [128, 2 * NCH], F32)
            Bssq = newbuf()
            s.memset("dve", ssq[:], 0.0, [Bssq])
            if ssd:
                dsk = A("dsk", [128, 1024], F32)
                Bdsk = newbuf()
                s.dma("sp", dsk[:], s.rowp[l:l + 1, RP_DSK:RP_DSK + 1024].broadcast_to([128, 1024]), [], [Bdsk])
            else:
                s.memset("dve", V[:], 0.0, [BV])
                s.memset("dve", V[:, :, 128:129], 1.0, [BV])

            for u in range(nunits):
                with scope() as cl_:
                    cin = A("cin", [128, 2312], F32, cl_)
                    acc = A("acc", [128, 2312], F32, cl_)
                    Bcin, Bacc = newbuf(), newbuf()
                    s.memset("dve", cin[:], 0.0, [Bcin])
                    if ssd:
                        g = u
                        cout = Rot(cl_, s.sbt, "sc_cout", [128, T], BF16, 1)
                        for i in range(4):
                            co, cob = cout.next()
                            idx = 4 * g + i
                            s.conv_tile(l, 512 * g + 128 * i, PP_SCW + 60 * l + 5 * idx, PP_SCB + 12 * l + idx, False,
                                        cin, Bcin, acc, Bacc, co, cob)
                            if s.debug.get("conv_stop") in ("mm", "dve", "silu"):
                                return
                            for c0 in range(0, NCH, 4):
                                ncs = min(4, NCH - c0)
                                po, pob = s.ps.next()
                                for ci in range(ncs):
                                    c = c0 + ci
                                    s.tr(po[:, ci * 128:ci * 128 + 128], co[:, c * 128:c * 128 + 128], [cob], [pob])
                                s.act(V[:, c0:c0 + ncs, i * 128:i * 128 + 128],
                                      po[:, 0:ncs * 128].rearrange("p (c e) -> p c e", e=128), AF.Copy, [pob], [BV])
                            if s.debug.get("conv_stop") == "tr1":
                                return
                        if s.debug.get("conv_stop") == "x":
                            return
                        s.conv_tile(l, 1024 + 128 * g, PP_SCW + 60 * l + 5 * (8 + g), PP_SCB + 12 * l + 8 + g, False,
                                    cin, Bcin, acc, Bacc, KT, BKT)
                        s.conv_tile(l, 1280 + 128 * g, PP_SCW + 60 * l + 5 * (10 + g), PP_SCB + 12 * l + 10 + g, False,
                                    cin, Bcin, acc, Bacc, QT, BQT)
                    else:
                        hd = u
                        s.conv_tile(l, ML0 + 128 * hd, PP_MCW + 80 * l + 5 * hd, PP_MCB + 16 * l + hd, True,
                                    cin, Bcin, acc, Bacc, QT, BQT)
                        s.conv_tile(l, ML0 + 1024 + 128 * hd, PP_MCW + 80 * l + 5 * (8 + hd), PP_MCB + 16 * l + 8 + hd, True,
                                    cin, Bcin, acc, Bacc, KT, BKT)
                if s.debug.get("conv_stop") == "bc":
                    return
                if ssd:
                    wz, wzb = s.load_w(s.w_in[l][:, 1536 + 512 * u:1536 + 512 * u + 512], 512)
                else:
                    hd = u
                    wv, wvb = s.load_w(s.w_in[l][:, ML0 + 2048 + 128 * hd:ML0 + 2048 + 128 * hd + 128], 128)
                    for c in range(NCH):
                        pt, pb = s.ps.next()
                        for k in range(8):
                            s.mm(pt[:, 0:128], s.hcols(k, c, True), wv[:, k, 0:128], k == 0, k == 7, [wvb, s.BhT], [pb])
                        s.act(V[:, c, 0:128], pt[:, 0:128], AF.Copy, [pb], [BV])
                    wz, wzb = s.load_w(s.w_in[l][:, ML0 + 3072 + 128 * hd:ML0 + 3072 + 128 * hd + 128], 128)
                for c0 in range(0, NCH, 4):
                    ncs = min(4, NCH - c0)
                    po, pob = s.ps.next()
                    for ci in range(ncs):
                        c = c0 + ci
                        s.tr(po[:, ci * 128:ci * 128 + 128], KT[:, c * 128:c * 128 + 128], [BKT], [pob])
                    s.act(Ktok[:, c0:c0 + ncs, :], po[:, 0:ncs * 128].rearrange("p (c e) -> p c e", e=128), AF.Copy,
                          [pob], [BKtok], scale=sc)

                if s.debug.get("scan_stop") == "conv":
                    return

                def gsl(arr, c, d):
                    if ssd:
                        return arr[:, c, d * 16 + 8 * u:d * 16 + 8 * u + 8].unsqueeze(2).broadcast_to([128, 8, 64])
                    return arr[:, c, d * 8 + u:d * 8 + u + 1]

                with scope() as ul_:
                    ScT = Rot(ul_, s.sbt, "sc_ScT" + kind, [128, 128], BF16, 3)
                    DT = Rot(ul_, s.sbt, "sc_DT" + kind, [128, 4, 128], BF16, 2)
                    PT = Rot(ul_, s.sbt, "sc_PT" + kind, [128, 4, 128], BF16, 8)
                    Vw = Rot(ul_, s.sbt, "sc_Vw" + kind, [128, VW], BF16, 3)
                    zs = Rot(ul_, s.sbt, "sc_zs" + kind, [128, W if ssd else 128], F32, 2 if ssd else 3)
                    t1 = Rot(ul_, s.sbt, "sc_t1" + kind, [128, VW], F32, 2)
                    t2 = Rot(ul_, s.sbt, "sc_t2" + kind, [128, VW], F32, 2)
                    t3 = Rot(ul_, s.sbt, "sc_t3" + kind, [128, VW], F32, 2)
                    yo = Rot(ul_, s.sbt, "sc_yo" + kind, [128, VW], F32 if ssd else BF16, 2 if ssd else 3)
                    sm = Rot(ul_, s.sbt, "sc_sm" + kind, [128, 8], F32, 2)
                    ymc = Rot(ul_, s.sbt, "sc_ymc" + kind, [128, 128], BF16, 2)
                    cTr = Rot(ul_, s.sbt, "sc_cT" + kind, [16, 128], F32, 4)
                    orders = [list(range(NCH)), [1, 0] + list(range(NCH - 1, 1, -1))]
                    for d in range(2):
                        s.memset("dve", S[d][:], 0.0, [BS[d]])
                    for oi in range(NCH):
                        for d in range(2):
                            c = orders[d][oi]
                            s.P.op("dve", (lambda o, i_: (lambda e: e.tensor_copy(out=o, in_=i_)))(Sin[d][:, c, :], S[d][:]),
                                   [BS[d]], [BSin[d]])
                            if oi == NCH - 1:
                                continue
                            vw, vwb = Vw.next()
                            if ssd:
                                s.tt("pool", vw[:, :].rearrange("p (h e) -> p h e", e=64),
                                     V[:, c, :].rearrange("p (h e) -> p h e", e=64), gsl(gst, c, d), ALU.mult,
                                     [BV, Bd], [vwb])
                            else:
                                s.act(vw[:, :], V[:, c, :], AF.Identity, [BV, Bd], [vwb], scale=gsl(gst, c, d))
                            pl, plb = s.ps.next()
                            s.mm(pl[:, 0:W], Ktok[:, c, :], vw[:, 0:W], True, True, [BKtok, vwb], [plb])
                            if ssd:
                                sv = S[d][:, :].rearrange("p (h e) -> p h e", e=64)
                                s.tt("dve", sv, sv, gsl(atot, c, d), ALU.mult, [BS[d], Bd], [BS[d]])
                                s.tt("dve", S[d][:, 0:W], S[d][:, 0:W], pl[:, 0:W], ALU.add, [BS[d], plb], [BS[d]])
                            else:
                                s.stt(S[d][:, 0:W], S[d][:, 0:W], gsl(atot, c, d), pl[:, 0:W], ALU.mult, ALU.add,
                                      [BS[d], plb, Bd], [BS[d]])

                    if s.debug.get("scan_stop") == "pass1":
                        return
                    def front(c):
                        cs = slice(c * 128, c * 128 + 128)
                        pS, pSb = s.ps.next()
                        s.mm(pS[:, 0:128], KT[:, cs], QT[:, cs], True, True, [BKT, BQT], [pSb])
                        sct, sctb = ScT.next()
                        s.ts("dve", sct[:], pS[:, 0:128], sc, None, ALU.mult, None, [pSb], [sctb])
                        pz, pzb = s.ps.next()
                        nz = 512 if ssd else 128
                        for k in range(8):
                            s.mm(pz[:, 0:nz], s.hcols(k, c, perm=not ssd), wz[:, k, 0:nz], k == 0, k == 7,
                                 [wzb, s.BhT], [pzb])
                        z, zb = zs.next()
                        s.act(z[:, 0:nz], pz[:, 0:nz], AF.Silu if ssd else AF.Sigmoid, [pzb], [zb])
                        aset = c % 2
                        pA0, pAb0 = s.pacc.tiles[aset], s.pacc.bufs[aset]
                        pA = [pA0, pA0 if ssd else pA0[:, 256:512]]
                        pts = []
                        for d in range(2):
                            mask = s.cst[:, CS_MASKF:CS_MASKF + 128] if d == 0 else s.cst[:, CS_MASKB:CS_MASKB + 128]
                            pc_, pcb_ = s.ps.next()
                            s.mm(pc_[0:nhd, 0:128], la[:, c, d * nhd:(d + 1) * nhd], s.triu() if d == 0 else s.tril(),
                                 True, True, [Bd, s.Bconst], [pcb_])
                            cT, cTb = cTr.next()
                            s.cp("dve", cT[0:nhd, :], pc_[0:nhd, 0:128], [pcb_], [cTb])
                            ngrp = 2 if ssd else 1
                            for hq in range(ngrp):
                                nh4 = 4 if ssd else 1
                                pD, pDb = s.ps.next()
                                dt_, dtb = DT.next()
                                for hh in range(nh4):
                                    hl = (8 * u + hq * 4 + hh) if ssd else u
                                    s.mm(pD[:, hh * 128:hh * 128 + 128], s.sel[0:nhd, hl * 128:hl * 128 + 128],
                                         cT[0:nhd, :], True, False, [cTb, s.Bconst], [pDb])
                                    s.mm(pD[:, hh * 128:hh * 128 + 128], s.identf(), mask, False, True, [s.Bconst], [pDb])
                                    s.act(dt_[:, hh, :], pD[:, hh * 128:hh * 128 + 128], AF.Exp, [pDb, Bd], [dtb],
                                          bias=bias[:, c, d * nhd + hl:d * nhd + hl + 1])
                                p_, ptb_ = PT.next()
                                s.tt("dve", p_[:, 0:nh4, :], dt_[:, 0:nh4, :],
                                     sct[:, :].unsqueeze(1).broadcast_to([128, nh4, 128]), ALU.mult, [dtb, sctb], [ptb_])
                                pts.append((p_, ptb_, d, hq, nh4))
                        return (z, zb, pA, pAb0, pts)

                    def frontB(c, fr):
                        z, zb, pA, pAb0, pts = fr
                        for (p_, ptb_, d, hq, nh4) in pts:
                            for hh in range(nh4):
                                if ssd:
                                    h = hq * 4 + hh
                                    s.mm(pA[0][:, h * 64:h * 64 + 64], p_[:, hh, :], V[:, c, h * 64:h * 64 + 64],
                                         d == 0 and h == 0, d == 1, [ptb_, BV], [pAb0], skip=True)
                                else:
                                    s.mm(pA[d][:, 0:W], p_[:, 0, :], V[:, c, 0:W], True, True, [ptb_, BV], [pAb0])

                    def back(c, fr):
                        z, zb, pA, pAb0, pts = fr
                        cs = slice(c * 128, c * 128 + 128)
                        pAb = [pAb0, pAb0]
                        if ssd:
                            pB = [s.pacc.tiles[2], s.pacc.tiles[3]]
                            pBb = [s.pacc.bufs[2], s.pacc.bufs[3]]
                        else:
                            pB = [s.pacc.tiles[2], s.pacc.tiles[2][:, 256:512]]
                            pBb = [s.pacc.bufs[2], s.pacc.bufs[2]]
                        for d in range(2):
                            s.mm(pB[d][:, 0:W], QT[:, cs], Sin[d][:, c, 0:W], True, True, [BQT, BSin[d]], [pBb[d]])
                        if ssd:
                            a1, a1b = t1.next()
                            a2, a2b = t2.next()
                            a3, a3b = t3.next()
                            v3 = lambda t: t[:, :].rearrange("p (h e) -> p h e", e=64)
                            s.tt("dve", v3(a1), pB[0][:, 0:512].rearrange("p (h e) -> p h e", e=64), gsl(ecum, c, 0),
                                 ALU.mult, [pBb[0], Bd], [a1b])
                            s.tt("dve", v3(a2), pB[1][:, 0:512].rearrange("p (h e) -> p h e", e=64), gsl(ecum, c, 1),
                                 ALU.mult, [pBb[1], Bd], [a2b])
                            s.tt("pool", a3[:, :], V[:, c, :], dsk[:, 512 * u:512 * u + 512], ALU.mult, [BV, Bdsk], [a3b])
                            s.tt("pool", a1[:, :], a1[:, :], a2[:, :], ALU.add, [a1b, a2b], [a1b])
                            s.tt("pool", a1[:, :], a1[:, :], a3[:, :], ALU.add, [a1b, a3b], [a1b])
                            s.tt("dve", a1[:, :], a1[:, :], pA[0][:, 0:512], ALU.add, [a1b, pAb[0]], [a1b])
                            y, yb = yo.next()
                            s.tt("dve", y[:, :], a1[:, :], z[:, 0:512], ALU.mult, [a1b, zb], [yb])
                            s.act(a2[:, :], y[:, :], AF.Square, [yb, a2b], [a2b, Bssq],
                                  accum_out=ssq[:, 2 * c + u:2 * c + u + 1])
                            s.dma(None, s.ys_tok[c * 128:c * 128 + 128, 512 * u:512 * u + 512], y[:, :], [yb], [s.Bys])
                        else:
                            yd = []
                            ydb = []
                            small, smb = sm.next()
                            for d in range(2):
                                ta, tab = (t1 if d == 0 else t2).next()
                                s.act(ta[:, 0:W], pA[d][:, 0:W], AF.Copy, [pAb[d]], [tab])
                                s.stt(ta[:, 0:W], pB[d][:, 0:W], gsl(ecum, c, d), ta[:, 0:W], ALU.mult, ALU.add,
                                      [pBb[d], tab, Bd], [tab])
                                s.act(small[:, d:d + 1], ta[:, 128:129], AF.Abs, [tab], [smb])
                                s.ts("dve", small[:, d:d + 1], small[:, d:d + 1], 1.0, None, ALU.max, None, [smb], [smb])
                                s.recip(small[:, 2 + d:3 + d], small[:, d:d + 1], [smb], [smb])
                                yd.append(ta)
                                ydb.append(tab)
                            a3, a3b = t3.next()
                            s.ts("dve", a3[:, 0:128], yd[0][:, 0:128], small[:, 2:3], None, ALU.mult, None,
                                 [ydb[0], smb], [a3b])
                            s.stt(a3[:, 0:128], yd[1][:, 0:128], small[:, 3:4], a3[:, 0:128], ALU.mult, ALU.add,
                                  [ydb[1], smb, a3b], [a3b])
                            s.act(yd[0][:, 0:128], a3[:, 0:128], AF.Square, [a3b, ydb[0]], [ydb[0], smb],
                                  accum_out=small[:, 4:5])
                            s.act(small[:, 5:6], small[:, 4:5], AF.Sqrt, [smb], [smb], scale=1.0 / 128.0, bias=EPS)
                            s.recip(small[:, 6:7], small[:, 5:6], [smb], [smb])
                            y, yb = yo.next()
                            s.stt(y[:, 0:128], a3[:, 0:128], small[:, 6:7], z[:, 0:128], ALU.mult, ALU.mult,
                                  [a3b, smb, zb], [yb])
                            return (y, yb)
                        return None

                    def tail(c, yy):
                        y, yb = yy
                        po_, pob = s.ps.next()
                        po = po_[:, 0:128]
                        s.tr(po, y[:, 0:128], [yb], [pob])
                        ym, ymb = ymc.next()
                        s.ts("dve", ym[:, :], po, s.pp[:, PP_MNW + 8 * l + u:PP_MNW + 8 * l + u + 1], None, ALU.mult, None,
                             [pob, s.Bconst], [ymb])
                        s.dma(None, s.ybT[1][u * 128:u * 128 + 128, c * 128:c * 128 + 128], ym[:, :], [ymb],
                              [s.BybT[1]])

                    chunks = list(range(first_c, NCH))
                    fr = front(chunks[0])
                    pend = None
                    for ci, c in enumerate(chunks):
                        nxt = front(chunks[ci + 1]) if ci + 1 < len(chunks) else None
                        frontB(c, fr)
                        if pend is not None:
                            tail(*pend)
                        yy = back(c, fr)
                        pend = (c, yy) if yy is not None else None
                        fr = nxt
                    if pend is not None:
                        tail(*pend)

            if s.debug.get("scan_dump") == kind:
                s.dump("V", V[:], [128, NCH, VW], BV, BF16)
                s.dump("KT", KT[:], [128, T], BKT, BF16)
                s.dump("QT", QT[:], [128, T], BQT, BF16)
                s.dump("Ktok", Ktok[:], [128, NCH, 128], BKtok, BF16)
                s.dump("Sin0", Sin[0][:], [128, NCH, VW], BSin[0], BF16)
                s.dump("Sin1", Sin[1][:], [128, NCH, VW], BSin[1], BF16)
                s.dump("ssq", ssq[:], [128, 2 * NCH], Bssq)
            if ssd:
                ytr = Rot(lst, s.sbt, "sc_ytr", [128, 1024], F32, 2)
                ynr = Rot(lst, s.sbt, "sc_ynr", [128, 1024], BF16, 2)
                ysc = Rot(lst, s.sbt, "sc_ysc", [128, 8, 128], BF16, 2)
                sm2 = Rot(lst, s.sbt, "sc_sm2", [128, 8], F32, 2)
                for c in range(first_c, NCH):
                    yt, ytb = ytr.next()
                    s.dma(None, yt[:, :], s.ys_tok[c * 128:c * 128 + 128, :], [s.Bys], [ytb])
                    small, smb = sm2.next()
                    s.tt("dve", small[:, 0:1], ssq[:, 2 * c:2 * c + 1], ssq[:, 2 * c + 1:2 * c + 2], ALU.add, [Bssq], [smb])
                    s.act(small[:, 1:2], small[:, 0:1], AF.Sqrt, [smb], [smb], scale=1.0 / 1024.0, bias=EPS)
                    s.recip(small[:, 2:3], small[:, 1:2], [smb], [smb])
                    yn, ynb = ynr.next()
                    s.ts("dve", yn[:, :], yt[:, :], small[:, 2:3], None, ALU.mult, None, [ytb, smb], [ynb])
                    yc, ycb = ysc.next()
                    for j0 in range(0, 8, 4):
                        po, pob = s.ps.next()
                        for ji in range(4):
                            j = j0 + ji
                            s.tr(po[:, ji * 128:ji * 128 + 128], yn[:, j * 128:j * 128 + 128], [ynb], [pob])
                        for ji in range(4):
                            j = j0 + ji
                            s.act(yc[:, j, :], po[:, ji * 128:ji * 128 + 128], AF.Identity, [pob, s.Bconst], [ycb],
                                  scale=s.pp[:, PP_SNW + 8 * l + j:PP_SNW + 8 * l + j + 1])
                    s.dma(None, s.ybT[0][:, c * 128:c * 128 + 128].rearrange("(j p) t -> p j t", p=128), yc[:, :, :],
                          [ycb], [s.BybT[0]])

    def stage_hyena(self, l, need_ctx):
        self.hyena_seq(l, TL, LT0, TC, 0)
        if need_ctx:
            self.hyena_seq(l, TC, CT0, 0, 16)

    def bload(self, dst, row_ap, n, buf):
        self.dma("sp", dst, row_ap.broadcast_to([128, n]), [], [buf])

    def hyena_seq(self, l, L, col0, tok0, tn0):
        s = self
        nt = L // 128
        nw = _wp(L) // 128
        dft = s.dft[L]
        TWO_PI = 2.0 * math.pi
        MAGIC = 12582912.0
        with scope() as lst:
            def A(name, shape, dt):
                return lst.enter_context(s.sbt("hy_" + name, list(shape), dt))
            feats = A("feats", [33, L], F32)
            w1 = A("w1", [33, 64], F32)
            w2 = A("w2", [64, 64], F32)
            w3 = A("w3", [64, 4096], F32)
            hid = [A("hid1", [64, L], F32), A("hid2", [64, L], F32)]
            absdec = A("absdec", [128, 4096], F32)
            hrow = A("hrow", [128, 4096], F32)
            Bw, Bh, Bdec, Brow = newbuf(), [newbuf(), newbuf()], newbuf(), newbuf()
            ta = Rot(lst, s.sbt, "hy_ta", [64, 512], F32, 2)
            tb = Rot(lst, s.sbt, "hy_tb", [64, 512], F32, 2)
            win = Rot(lst, s.sbt, "hy_win", [128, 512], F32, 2)
            hso = Rot(lst, s.sbt, "hy_hso", [128, 2048], BF16, 2)
            hdo = Rot(lst, s.sbt, "hy_hdo", [128, 2048], BF16, 2)
            s.dma("sp", feats[:], s.feats[L], [], [Bw])
            s.dma("sp", w1[:], s.hyw1[l], [], [Bw])
            s.dma("sp", w2[:], s.hyw2[l], [], [Bw])
            s.dma("sp", w3[:], s.hyw3[l], [], [Bw])
            s.bload(absdec[:], s.rowp[l:l + 1, RP_HDEC:RP_HDEC + 4096], 4096, Bdec)
            s.act(absdec[:], absdec[:], AF.Abs, [Bdec], [Bdec])
            nq = max(1, L // 512)
            qn = min(512, L)
            for layer in range(2):
                wgt = w1 if layer == 0 else w2
                bcol = (PP_HB1 if layer == 0 else PP_HB2) + l
                for q in range(nq):
                    pt, pb = s.ps.next()
                    src = feats[:, q * qn:(q + 1) * qn] if layer == 0 else hid[0][:, q * qn:(q + 1) * qn]
                    s.mm(pt[0:64, 0:qn], wgt[:, :], src, True, True, [Bw] + ([Bh[0]] if layer else []), [pb])
                    a, ab = ta.next()
                    b, bb = tb.next()
                    s.ts("dve", a[:, 0:qn], pt[0:64, 0:qn], s.pp[0:64, bcol:bcol + 1], None, ALU.add, None, [pb, s.Bconst], [ab])
                    s.ts("dve", b[:, 0:qn], a[:, 0:qn], math.pi, None, ALU.is_gt, None, [ab], [bb])
                    s.stt(a[:, 0:qn], b[:, 0:qn], -TWO_PI, a[:, 0:qn], ALU.mult, ALU.add, [bb, ab], [ab])
                    s.ts("dve", b[:, 0:qn], a[:, 0:qn], -math.pi, None, ALU.is_lt, None, [ab], [bb])
                    s.stt(a[:, 0:qn], b[:, 0:qn], TWO_PI, a[:, 0:qn], ALU.mult, ALU.add, [bb, ab], [ab])
                    s.act(hid[layer][:, q * qn:(q + 1) * qn], a[:, 0:qn], AF.Sin, [ab], [Bh[layer]])
            for tc in range(nt):
                for cb in range(8):
                    pt, pb = s.ps.next()
                    s.mm(pt[:, 0:512], hid[1][:, tc * 128:tc * 128 + 128], w3[:, cb * 512:cb * 512 + 512], True, True,
                         [Bh[1], Bw], [pb])
                    w_, wb_ = win.next()
                    s.act(w_[:, :], absdec[:, cb * 512:cb * 512 + 512], AF.Exp, [Bdec, s.Bconst], [wb_],
                          scale=s.tneg[:, tn0 + tc:tn0 + tc + 1])
                    s.tt("dve", hrow[:, cb * 512:cb * 512 + 512], pt[:, 0:512], w_[:, :], ALU.mult, [pb, wb_], [Brow])
                hs, hsb = hso.next()
                hd, hdb = hdo.next()
                for o in range(2):
                    hf = hrow[:, o * 2048:o * 2048 + 1024]
                    hb = hrow[:, o * 2048 + 1024:o * 2048 + 2048]
                    s.tt("dve", hs[:, o * 1024:o * 1024 + 1024], hf, hb, ALU.add, [Brow], [hsb])
                    s.tt("pool", hd[:, o * 1024:o * 1024 + 1024], hb, hf, ALU.subtract, [Brow], [hdb])
                s.dma(None, s.hsd[0][tc * 128:tc * 128 + 128, :], hs[:, :], [hsb], [s.Bhsd])
                s.dma(None, s.hsd[1][tc * 128:tc * 128 + 128, :], hd[:, :], [hdb], [s.Bhsd])
            if s.debug.get("hy_dump") and L == TL:
                s.dump("hid1", hid[0][:], [64, L], Bh[0])
                s.dump("hid2", hid[1][:], [64, L], Bh[1])
                s.dump("hrow", hrow[:], [128, 4096], Brow)
                s.dump("hs", hs[:], [128, 2048], hsb, BF16)
        with scope() as lst:
            hsb_ = Rot(lst, s.sbt, "hy_hsb", [128, nt, 1024], BF16, 1)
            hdb_ = Rot(lst, s.sbt, "hy_hdb", [128, nt, 1024], BF16, 1)
            ctl = Rot(lst, s.sbt, "hy_ct", [128, nt * 128], BF16, 2)
            stl = Rot(lst, s.sbt, "hy_st", [128, nt * 128], BF16, 2)
            kr = Rot(lst, s.sbt, "hy_kr", [128, 512], F32, 2)
            ki = Rot(lst, s.sbt, "hy_ki", [128, 512], F32, 2)
            for cbp in range(2):
                hsT, hsB = hsb_.next()
                hdT, hdB = hdb_.next()
                for tc in range(nt):
                    s.dma(None, hsT[:, tc, :], s.hsd[0][tc * 128:tc * 128 + 128, cbp * 1024:cbp * 1024 + 1024], [s.Bhsd], [hsB])
                    s.dma(None, hdT[:, tc, :], s.hsd[1][tc * 128:tc * 128 + 128, cbp * 1024:cbp * 1024 + 1024], [s.Bhsd], [hdB])
                for wt in range(nw):
                    ct, ctb = ctl.next()
                    st_, stb = stl.next()
                    s.dma(None, ct[:], dft["ct"][wt], [], [ctb])
                    s.dma(None, st_[:], dft["st"][wt], [], [stb])
                    for half in range(2):
                        cb = cbp * 2 + half
                        hc = slice(half * 512, half * 512 + 512)
                        pR, pRb = s.ps.next()
                        pI, pIb = s.ps.next()
                        for tc in range(nt):
                            s.mm(pR[:, 0:512], ct[:, tc * 128:tc * 128 + 128], hsT[:, tc, hc], tc == 0, tc == nt - 1,
                                 [ctb, hsB], [pRb])
                        for tc in range(nt):
                            s.mm(pI[:, 0:512], st_[:, tc * 128:tc * 128 + 128], hdT[:, tc, hc], tc == 0, tc == nt - 1,
                                 [stb, hdB], [pIb])
                        kre, kreb = kr.next()
                        kim, kimb = ki.next()
                        s.act(kre[:, :], pR[:, 0:512], AF.Copy, [pRb], [kreb])
                        s.cp("dve", kim[:, :], pI[:, 0:512], [pIb], [kimb])
                        s.dma(None, s.hyK[0][wt * 128:wt * 128 + 128, cb * 512:cb * 512 + 512], kre[:, :], [kreb], [s.BhyK])
                        s.dma(None, s.hyK[1][wt * 128:wt * 128 + 128, cb * 512:cb * 512 + 512], kim[:, :], [kimb], [s.BhyK])
        if s.debug.get("hy_stop") == "filt":
            return
        for cb2 in range(2):
            with scope() as lst:
                z = lst.enter_context(s.sbt("hy_z", [128, nt, 512], BF16))
                Yre = lst.enter_context(s.sbt("hy_yre", [128, nw, 512], BF16))
                Yim = lst.enter_context(s.sbt("hy_yim", [128, nw, 512], BF16))
                Wk = [lst.enter_context(s.sbt("hy_wk%d" % i, [128, 8, 512], BF16)) for i in range(3)]
                rowt = lst.enter_context(s.sbt("hy_rowt", [128, 3, 512], F32))
                cbias = lst.enter_context(s.sbt("hy_cbias", [128, 512], F32))
                skipb = lst.enter_context(s.sbt("hy_skipb", [128, 512], F32))
                Bz, BY, BWk, Brow, Bcb, Bsk = newbuf(), newbuf(), newbuf(), newbuf(), newbuf(), newbuf()
                kr = Rot(lst, s.sbt, "hy_kr2", [128, 512], F32, 2)
                ki = Rot(lst, s.sbt, "hy_ki2", [128, 512], F32, 2)
                tmp = [Rot(lst, s.sbt, "hy_tmp%d" % i, [128, 512], F32, 1) for i in range(4)]
                xg = Rot(lst, s.sbt, "hy_xg", [128, 512], F32, 2)
                yb16 = Rot(lst, s.sbt, "hy_yb16", [128, 512], BF16, 2)
                yT = Rot(lst, s.sbt, "hy_yT", [128, 4, 128], BF16, 2)

                def prep_part(part):
                    c0 = HY0 + part * 1024 + cb2 * 512
                    w, wb = s.load_w(s.w_in[l][:, c0:c0 + 512], 512)
                    for tap in range(3):
                        o = RP_HCW + tap * 3072 + part * 1024 + cb2 * 512
                        s.bload(rowt[:, tap, :], s.rowp[l:l + 1, o:o + 512], 512, Brow)
                    o = RP_HCB + part * 1024 + cb2 * 512
                    s.bload(cbias[:, :], s.rowp[l:l + 1, o:o + 512], 512, Bcb)
                    for tap in range(3):
                        s.tt("dve" if tap != 1 else "pool", Wk[tap][:], w[:, :, :],
                             rowt[:, tap, :].unsqueeze(1).broadcast_to([128, 8, 512]), ALU.mult, [wb, Brow], [BWk])

                def proj3(tc):
                    pt, pb = s.ps.next()
                    i = 0
                    for tap in range(3):
                        for k in range(8):
                            c_ = col0 + tc * 128 + tap - 1
                            s.mm(pt[:, 0:512], s.hT[:, k, c_:c_ + 128], Wk[tap][:, k, :], i == 0, i == 23, [s.BhT, BWk], [pb])
                            i += 1
                    return pt, pb

                prep_part(0)
                for tc in range(nt):
                    pt, pb = proj3(tc)
                    s.tt("dve", z[:, tc, :], pt[:, 0:512], cbias[:, :], ALU.add, [pb, Bcb], [Bz])
                for n in range(2):
                    prep_part(n + 1)
                    o = RP_HSKIP + n * 1024 + cb2 * 512
                    s.bload(skipb[:, :], s.rowp[l:l + 1, o:o + 512], 512, Bsk)
                    with scope() as fl_:
                        ctl = Rot(fl_, s.sbt, "hy_ct2", [128, nt * 128], BF16, 2)
                        stl = Rot(fl_, s.sbt, "hy_st2", [128, nt * 128], BF16, 2)
                        for wt in range(nw):
                            ct, ctb = ctl.next()
                            st_, stb = stl.next()
                            s.dma(None, ct[:], dft["ct"][wt], [], [ctb])
                            s.dma(None, st_[:], dft["st"][wt], [], [stb])
                            kre, kreb = kr.next()
                            kim, kimb = ki.next()
                            kc = n * 1024 + cb2 * 512
                            s.dma(None, kre[:, :], s.hyK[0][wt * 128:wt * 128 + 128, kc:kc + 512], [s.BhyK], [kreb])
                            s.dma(None, kim[:, :], s.hyK[1][wt * 128:wt * 128 + 128, kc:kc + 512], [s.BhyK], [kimb])
                            pR, pRb = s.ps.next()
                            pS, pSb = s.ps.next()
                            for tc in range(nt):
                                s.mm(pR[:, 0:512], ct[:, tc * 128:tc * 128 + 128], z[:, tc, :], tc == 0, tc == nt - 1,
                                     [ctb, Bz], [pRb])
                            for tc in range(nt):
                                s.mm(pS[:, 0:512], st_[:, tc * 128:tc * 128 + 128], z[:, tc, :], tc == 0, tc == nt - 1,
                                     [stb, Bz], [pSb])
                            t = [r_.next() for r_ in tmp]
                            s.tt("dve", t[0][0][:, :], pR[:, 0:512], kre[:, :], ALU.mult, [pRb, kreb], [t[0][1]])
                            s.tt("dve", t[1][0][:, :], pS[:, 0:512], kim[:, :], ALU.mult, [pSb, kimb], [t[1][1]])
                            s.tt("dve", t[2][0][:, :], pR[:, 0:512], kim[:, :], ALU.mult, [pRb, kimb], [t[2][1]])
                            s.tt("dve", t[3][0][:, :], pS[:, 0:512], kre[:, :], ALU.mult, [pSb, kreb], [t[3][1]])
                            s.tt("pool", Yre[:, wt, :], t[0][0][:, :], t[1][0][:, :], ALU.add, [t[0][1], t[1][1]], [BY])
                            s.tt("pool", Yim[:, wt, :], t[2][0][:, :], t[3][0][:, :], ALU.subtract, [t[2][1], t[3][1]], [BY])
                    with scope() as il_:
                        cil = Rot(il_, s.sbt, "hy_ci2", [128, nw * 128], BF16, 2)
                        sil = Rot(il_, s.sbt, "hy_si2", [128, nw * 128], BF16, 2)
                        for tc in range(nt):
                            ci, cib = cil.next()
                            si, sib = sil.next()
                            s.dma(None, ci[:], dft["ci"][tc], [], [cib])
                            s.dma(None, si[:], dft["si"][tc], [], [sib])
                            pY, pYb = s.ps.next()
                            for wc in range(nw):
                                s.mm(pY[:, 0:512], ci[:, wc * 128:wc * 128 + 128], Yre[:, wc, :], wc == 0, False,
                                     [cib, BY], [pYb])
                            for wc in range(nw):
                                s.mm(pY[:, 0:512], si[:, wc * 128:wc * 128 + 128], Yim[:, wc, :], False, wc == nw - 1,
                                     [sib, BY], [pYb])
                            pX, pXb = proj3(tc)
                            x_, xb_ = xg.next()
                            s.tt("dve", x_[:, :], pX[:, 0:512], cbias[:, :], ALU.add, [pXb, Bcb], [xb_])
                            t0, t0b = tmp[0].next()
                            s.tt("pool", t0[:, :], z[:, tc, :], skipb[:, :], ALU.mult, [Bz, Bsk], [t0b])
                            s.tt("dve", t0[:, :], t0[:, :], pY[:, 0:512], ALU.add, [t0b, pYb], [t0b])
                            if n == 0:
                                s.tt("dve", z[:, tc, :], t0[:, :], x_[:, :], ALU.mult, [t0b, xb_], [Bz])
                            else:
                                y_, yb_ = yb16.next()
                                s.tt("dve", y_[:, :], t0[:, :], x_[:, :], ALU.mult, [t0b, xb_], [yb_])
                                po, pob = s.ps.next()
                                for j in range(4):
                                    s.tr(po[:, j * 128:j * 128 + 128], y_[:, j * 128:j * 128 + 128], [yb_], [pob])
                                yt_, ytb_ = yT.next()
                                s.act(yt_[:, :, :], po[:, 0:512].rearrange("p (j t) -> p j t", t=128), AF.Copy, [pob], [ytb_])
                                tk = tok0 + tc * 128
                                s.dma(None, s.ybT[2][cb2 * 512:cb2 * 512 + 512, tk:tk + 128].rearrange("(j p) t -> p j t", p=128),
                                      yt_[:, :, :], [ytb_], [s.BybT[2]])

    def tok_tiles(self, need_ctx):
        out = []
        if need_ctx:
            out.append((0, TC, 0, 1))
        for q in range(4):
            out.append((q + 1, 512, TC + 512 * q, 0))
        return out

    def resid_update(self, pO, pOb, n, dt2, off, gate_m, which, xr, ob):
        s = self
        xt, xb = xr.next()
        rows = slice(dt2 * 128, dt2 * 128 + 128)
        s.dma(None, xt[:, 0:n], s.xres[rows, off:off + n], [s.Bxres], [xb])
        xo, xob = ob.next()
        s.stt(xo[:, 0:n], pO[:, 0:n], s.modcol(gate_m, dt2, which), xt[:, 0:n], ALU.mult, ALU.add,
              [pOb, xb, s.Bmod], [xob])
        s.dma(None, s.xres[rows, off:off + n], xo[:, 0:n], [xob], [s.Bxres])

    def stage_merge(self, l, need_ctx):
        s = self
        tiles = s.tok_tiles(need_ctx)
        with scope() as lst:
            mT = lst.enter_context(s.sbt("mg_mT", [128, 8, T], BF16))
            yT = lst.enter_context(s.sbt("mg_yT", [128, 8, T], BF16))
            BmT, ByT = newbuf(), newbuf()
            sg = Rot(lst, s.sbt, "mg_sg", [128, 512], F32, 2)
            tm = Rot(lst, s.sbt, "mg_tm", [128, 512], F32, 2)
            with scope() as l2:
                yP = l2.enter_context(s.sbt("mg_yP", [128, 8, TL], BF16))
                ByP = newbuf()
                for n in range(3):
                    for k in range(8):
                        rows = slice(k * 128, k * 128 + 128)
                        if n == 1:
                            s.dma(None, yT[:, k, 0:TC], s.ybT[n][rows, 0:TC], [s.BybT[n]], [ByT])
                            s.dma(None, yP[:, k, :], s.ybT[n][rows, TC:T], [s.BybT[n]], [ByP])
                            src = yP[:, k, :].rearrange("p (w r) -> p r w", r=32)
                            dst = yT[:, k, TC:T].rearrange("p (r w) -> p r w", w=64)
                            if k % 2 == 0:
                                s.act(dst, src, AF.Copy, [ByP], [ByT])
                            else:
                                s.P.op("dve", (lambda o, i_: (lambda e: e.tensor_copy(out=o, in_=i_)))(dst, src), [ByP], [ByT])
                        else:
                            s.dma(None, yT[:, k, :], s.ybT[n][rows, :], [s.BybT[n]], [ByT])
                    for dt in range(8):
                        wb_t, wbb = s.load_w(s.w_branch[l][n][:, dt * 128:dt * 128 + 128], 128)
                        c0 = G0 + n * 1024 + dt * 128
                        wg_t, wgb = s.load_w(s.w_in[l][:, c0:c0 + 128], 128)
                        for (ti, nt_, off, which) in tiles:
                            pP, pPb = s.ps.next()
                            for k in range(8):
                                s.mm(pP[:, 0:nt_], wb_t[:, k, 0:128], yT[:, k, off:off + nt_], k == 0, k == 7, [wbb, ByT], [pPb])
                            pG, pGb = s.ps.next()
                            for k in range(8):
                                s.mm(pG[:, 0:nt_], wg_t[:, k, 0:128], s.htile(k, ti)[0], k == 0, k == 7, [wgb, s.BhT], [pGb])
                            g, gb = sg.next()
                            s.act(g[:, 0:nt_], pG[:, 0:nt_], AF.Sigmoid, [pGb], [gb])
                            if n == 0:
                                s.tt("dve", mT[:, dt, off:off + nt_], pP[:, 0:nt_], g[:, 0:nt_], ALU.mult, [pPb, gb], [BmT])
                            else:
                                t, tb = tm.next()
                                s.tt("dve", t[:, 0:nt_], pP[:, 0:nt_], g[:, 0:nt_], ALU.mult, [pPb, gb], [tb])
                                s.tt("dve", mT[:, dt, off:off + nt_], mT[:, dt, off:off + nt_], t[:, 0:nt_], ALU.add,
                                     [tb, BmT], [BmT])
            if s.debug.get("merge_dump"):
                s.dump("mT", mT[:], [128, 8, T], BmT, BF16)
            wo = lst.enter_context(s.sbt("mg_wo", [128, 8, D], BF16))
            Bwo = newbuf()
            s.dma("pool", wo[:], s.w_out[l].rearrange("(k p) n -> p k n", p=128), [], [Bwo])
            xr = Rot(lst, s.sbt, "mg_xr", [128, 512], F32, 2)
            ob = Rot(lst, s.sbt, "mg_ob", [128, 512], F32, 2)
            for (ti, nt_, off, which) in tiles:
                for dt2 in range(8):
                    pO, pOb = s.ps.next()
                    for k in range(8):
                        s.mm(pO[:, 0:nt_], wo[:, k, dt2 * 128:dt2 * 128 + 128], mT[:, k, off:off + nt_], k == 0, k == 7,
                             [Bwo, BmT], [pOb])
                    s.resid_update(pO, pOb, nt_, dt2, off, 2, which, xr, ob)

    def stage_mlp(self, l, need_ctx):
        s = self
        tiles = s.tok_tiles(need_ctx)
        with scope() as lst:
            hid = lst.enter_context(s.sbt("ml_hid", [128, 32, 512], BF16))
            Bhid = newbuf()
            rl = Rot(lst, s.sbt, "ml_rl", [128, 512], BF16, 2)
            xr = Rot(lst, s.sbt, "ml_xr", [128, 512], F32, 2)
            ob = Rot(lst, s.sbt, "ml_ob", [128, 512], F32, 2)
            for (ti, nt_, off, which) in tiles:
                for fb in range(8):
                    w1t, w1b = s.load_w(s.mlp_w1[l][:, fb * 512:fb * 512 + 512], 512)
                    for fj in range(4):
                        f = fb * 4 + fj
                        pH, pHb = s.ps.next()
                        for k in range(8):
                            s.mm(pH[:, 0:nt_], w1t[:, k, fj * 128:fj * 128 + 128], s.htile(k, ti)[0], k == 0, k == 7,
                                 [w1b, s.BhT], [pHb])
                        r, rb = rl.next()
                        s.act(r[:, 0:nt_], pH[:, 0:nt_], AF.Relu, [pHb], [rb])
                        s.tt("dve", hid[:, f, 0:nt_], r[:, 0:nt_], r[:, 0:nt_], ALU.mult, [rb], [Bhid])
                for dt2 in range(8):
                    w2t, w2b = s.wrot.next()
                    w2v = w2t[:, :, :].rearrange("p k (a n) -> p (k a) n", n=128)
                    s.dma("pool", w2v, s.mlp_w2[l][:, dt2 * 128:dt2 * 128 + 128].rearrange("(f p) n -> p f n", p=128),
                          [], [w2b])
                    pO, pOb = s.ps.next()
                    for f in range(32):
                        s.mm(pO[:, 0:nt_], w2v[:, f, :], hid[:, f, 0:nt_], f == 0, f == 31, [w2b, Bhid], [pOb])
                    s.resid_update(pO, pOb, nt_, dt2, off, 5, which, xr, ob)

    def stage_final(self):
        s = self
        with scope() as lst:
            xt = Rot(lst, s.sbt, "fx", [128, 8, 512], F32, 2)
            sq = Rot(lst, s.sbt, "fsq", [128, 8, 512], F32, 1)
            rs = Rot(lst, s.sbt, "frs", [128, 512], F32, 2)
            ot = Rot(lst, s.sbt, "fo", [128, 8, 512], F32, 2)
            for q in range(4):
                t0 = TC + 512 * q
                x, xb = xt.next()
                s.dma(None, x[:, :, :], s.xres[:, t0:t0 + 512].rearrange("(k p) t -> p k t", p=128), [s.Bxres], [xb])
                q_, qb = sq.next()
                s.act(q_[:, :, :], x[:, :, :], AF.Square, [xb], [qb])
                pt, pb = s.ps.next()
                for k in range(8):
                    s.mm(pt[:, 0:512], s.ones(), q_[:, k, :], k == 0, k == 7, [qb, s.Bconst], [pb])
                r, rb = rs.next()
                s.act(r[:, :], pt[:, 0:512], AF.Sqrt, [pb], [rb], scale=1.0 / D, bias=EPS)
                s.recip(r[:, :], r[:, :], [rb], [rb])
                o, ob_ = ot.next()
                for k in range(8):
                    s.stt(o[:, k, :], x[:, k, :], s.pp[:, PP_NF + k:PP_NF + k + 1], r[:, :], ALU.mult, ALU.mult,
                          [xb, rb, s.Bconst], [ob_])
                s.dma(None, s.out[:, 512 * q:512 * q + 512].rearrange("(k p) t -> p k t", p=128), o[:, :, :], [ob_], [Buf()])


def _bf16(a):
    import ml_dtypes
    return np.ascontiguousarray(a.astype(np.float32)).astype(ml_dtypes.bfloat16)


def _consts():
    k = np.arange(128)
    cst = np.zeros((128, CS_N), np.float32)
    cst[:, CS_TRIU:CS_TRIU + 128] = (k[:, None] <= k[None, :])
    cst[:, CS_TRIL:CS_TRIL + 128] = (k[:, None] >= k[None, :])
    cst[:, CS_MASKF:CS_MASKF + 128] = np.where(k[:, None] <= k[None, :], 0.0, NEG)
    cst[:, CS_MASKB:CS_MASKB + 128] = np.where(k[:, None] >= k[None, :], 0.0, NEG)
    cst[:, CS_IDENT:CS_IDENT + 128] = np.eye(128)
    cst[:, CS_ONES:CS_ONES + 128] = 1.0
    sel = np.zeros((16, 16, 128), np.float32)
    for h in range(16):
        sel[h, h, :] = 1.0
    out = {"cst": cst, "sel": sel.reshape(16, 16 * 128)}
    tneg = np.zeros((128, 18), np.float32)
    for j in range(16):
        tneg[:, j] = -(j * 128 + k) / float(TL)
    for j in range(2):
        tneg[:, 16 + j] = -(j * 128 + k) / float(TC)
    out["tneg"] = tneg
    for L, nm in ((TL, "L"), (TC, "C")):
        t = np.arange(L, dtype=np.float32)
        t_norm = t / np.float32(L)
        bands = np.linspace(1e-4, 15.0, 16, dtype=np.float32)
        ang = (np.float32(2.0 * math.pi / L) * t[:, None] * bands[None, :]).astype(np.float32)
        feats = np.concatenate([t_norm[:, None], np.cos(ang), -np.sin(ang)], axis=-1).astype(np.float32)
        out["feats" + nm] = np.ascontiguousarray(feats.T)
        wp = _wp(L)
        nw, nt = wp // 128, L // 128
        sidx = np.arange(L, dtype=np.float64)
        widx = np.arange(wp, dtype=np.float64)
        ph = np.pi * np.outer(sidx, widx) / L
        valid = (widx <= L)[None, :]
        ctm = np.where(valid, np.cos(ph), 0.0)
        stm = np.where(valid, np.sin(ph), 0.0)
        cw = np.where((widx == 0) | (widx == L), 1.0, 2.0) * (widx <= L) / (2.0 * L)
        cim = (cw[:, None] * np.cos(ph.T))
        sim = -(cw[:, None] * np.sin(ph.T))
        def tile_fwd(m):
            return _bf16(m.reshape(nt, 128, nw, 128).transpose(2, 1, 0, 3).reshape(nw, 128, nt * 128))

        def tile_inv(m):
            return _bf16(m.reshape(nw, 128, nt, 128).transpose(2, 1, 0, 3).reshape(nt, 128, nw * 128))
        out["ct" + nm] = tile_fwd(ctm)
        out["st" + nm] = tile_fwd(stm)
        out["ci" + nm] = tile_inv(cim)
        out["si" + nm] = tile_inv(sim)
    return out


def _fm(v, n):
    return np.ascontiguousarray(np.asarray(v, np.float32).reshape(n, 128).T)


def _prep_shared(inp):
    f = lambda n: np.asarray(inp[n], np.float32)
    pp = np.zeros((128, PP_N), np.float32)
    for l in range(DEPTH):
        pp[:, PP_N1W + 8 * l:PP_N1W + 8 * l + 8] = _fm(f("norm1_w")[l], 8)
        pp[:, PP_N2W + 8 * l:PP_N2W + 8 * l + 8] = _fm(f("norm2_w")[l], 8)
        pp[:, PP_MODB + 48 * l:PP_MODB + 48 * l + 48] = _fm(f("mod_b")[l], 48)
        pp[:, PP_SCB + 12 * l:PP_SCB + 12 * l + 12] = _fm(f("ssd_conv_b")[l], 12)
        cw = f("ssd_conv_w")[l]
        pp[:, PP_SCW + 60 * l:PP_SCW + 60 * l + 60] = cw.reshape(5, 12, 128).transpose(2, 1, 0).reshape(128, 60)
        pp[:, PP_MCB + 16 * l:PP_MCB + 16 * l + 16] = _fm(f("ml_conv_b")[l], 16)
        cw = f("ml_conv_w")[l]
        pp[:, PP_MCW + 80 * l:PP_MCW + 80 * l + 80] = cw.reshape(5, 16, 128).transpose(2, 1, 0).reshape(128, 80)
        pp[:, PP_SNW + 8 * l:PP_SNW + 8 * l + 8] = _fm(f("ssd_norm_w")[l], 8)
        pp[:, PP_MNW + 8 * l:PP_MNW + 8 * l + 8] = _fm(f("ml_norm_w")[l], 8)
        pp[0:64, PP_HB1 + l] = f("hy_ffn_b1")[l]
        pp[0:64, PP_HB2 + l] = f("hy_ffn_b2")[l]
    pp[:, PP_NF:PP_NF + 8] = _fm(f("norm_f_w"), 8)
    rowp = np.zeros((DEPTH, RP_N), np.float32)
    for l in range(DEPTH):
        rowp[l, RP_DTB:RP_DTB + 32] = f("ssd_dt_bias")[l].reshape(-1)
        rowp[l, RP_ALOG:RP_ALOG + 32] = f("ssd_a_log")[l].reshape(-1)
        rowp[l, RP_DSK:RP_DSK + 1024] = np.repeat(f("ssd_d")[l], 64)
        rowp[l, RP_MGB:RP_MGB + 32] = f("ml_gate_b")[l].reshape(-1)
        rowp[l, RP_HCB:RP_HCB + 3072] = f("hy_conv_b")[l]
        rowp[l, RP_HCW:RP_HCW + 9216] = f("hy_conv_w")[l].reshape(-1)
        rowp[l, RP_HDEC:RP_HDEC + 4096] = f("hy_decay")[l].reshape(-1)
        rowp[l, RP_HSKIP:RP_HSKIP + 2048] = f("hy_skip")[l].reshape(-1)
    sh = {"mod_w": f("mod_w"), "w_in": f("w_in"), "w_branch": f("w_branch"), "w_out": f("w_out"),
          "mlp_w1": f("mlp_w1"), "mlp_w2": f("mlp_w2"), "pp": pp, "rowp": rowp,
          "hyw1": f("hy_ffn_w1"), "hyw2": f("hy_ffn_w2"), "hyw3": f("hy_ffn_w3")}
    sh.update(_consts())
    return sh


def _prep_core(inp, b):
    x = np.asarray(inp["x"], np.float32)[b]
    ctx = np.asarray(inp["ctx"], np.float32)[b]
    xT = np.ascontiguousarray(np.concatenate([ctx.T, x.T], axis=1))
    cvec = np.zeros((128, 8, 2), np.float32)
    cvec[:, :, 0] = _fm(np.asarray(inp["c"], np.float32)[b], 8)
    cvec[:, :, 1] = _fm(np.asarray(inp["c_ctx"], np.float32), 8)
    return {"xT": xT, "cvec": cvec.reshape(128, 16)}


_CACHE = {}


def kernel(**inputs):
    if "nc" not in _CACHE:
        _CACHE["nc"] = Builder().build()
    nc = _CACHE["nc"]
    shared = _prep_shared(inputs)
    in_maps = []
    for b in range(8):
        m = dict(shared)
        m.update(_prep_core(inputs, b))
        in_maps.append(m)
    res = run_bass_kernel_spmd(nc, in_maps, core_ids=list(range(8)))
    out = np.stack([np.ascontiguousarray(res.results[b]["out"].T) for b in range(8)], axis=0)
    return out.astype(np.float32)
```

```python
import contextlib
import math
import numpy as np
import concourse.bass as bass
import concourse.mybir as mybir
from concourse.bass_utils import run_bass_kernel_spmd

F32 = mybir.dt.float32
BF16 = mybir.dt.bfloat16
AF = mybir.ActivationFunctionType
ALU = mybir.AluOpType

D = 1024
TC = 256
TL = 2048
T = TC + TL
NCH = T // 128
DEPTH = 2
EPS = 1e-6
CT0 = 1
LT0 = 259
TP = 2308
SSD_COLS = 2592
ML0 = 2592
REC_COLS = 6720
HY0 = 6720
G0 = 9792
IN_COLS = 12864
NEG = -30000.0

EPOCH = 30000
SYNC_SAME = ("act", "dve", "pool")


class Buf:
    __slots__ = ("name", "w", "r", "excl")

    def __init__(self, name="", excl=False):
        self.name = name
        self.w = None
        self.r = []
        self.excl = excl


class Prog:
    ENG = ("pe", "act", "dve", "pool", "sp")

    def __init__(self, nc, st, n_dma_sems=56):
        self.nc = nc
        self.st = st
        self.streams = {e: [] for e in self.ENG}
        self.cnt = {e: 0 for e in self.ENG}
        self.cur_sem = {}
        self.seen = {e: {} for e in self.ENG}
        for e in ("pe", "act", "dve", "pool"):
            self.cur_sem[e] = self._new_sem()
        self.dma_sems = [self._new_sem() for _ in range(n_dma_sems)]
        self.dma_val = [0] * n_dma_sems
        self.dma_rr = 0
        self.n_sw = 12
        self.dma_rr_sw = 0
        self.qrr = 0
        self.nops = 0

    def _new_sem(self):
        self.nsem = getattr(self, "nsem", 0) + 1
        return self.st.enter_context(self.nc.semaphore("sem%d" % self.nsem))

    def _need(self, eng, tok, waits, raw=True):
        if tok is None:
            return
        sem, val, src = tok
        if src == eng and (eng not in SYNC_SAME or not raw):
            return
        k = id(sem)
        if self.seen[eng].get(k, 0) >= val:
            return
        self.seen[eng][k] = val
        waits.append((sem, val))

    def _deps(self, eng, reads, writes):
        waits = []
        for b in reads:
            self._need(eng, b.w, waits)
        for b in writes:
            self._need(eng, b.w, waits)
            for t in b.r:
                self._need(eng, t, waits)
        return waits

    def _commit(self, tok, reads, writes):
        for b in reads:
            b.r.append(tok)
            if len(b.r) > 48:
                last = {}
                for t in b.r:
                    last[(id(t[0]))] = t if (id(t[0]) not in last or last[id(t[0])][1] < t[1]) else last[id(t[0])]
                b.r = list(last.values())
        for b in writes:
            b.w = tok
            b.r = []

    def op(self, eng, fn, reads=(), writes=()):
        writes = list(writes) + [b for b in reads if b.excl and b not in writes]
        reads = [b for b in reads if not b.excl]
        waits = self._deps(eng, reads, writes)
        if self.cnt[eng] >= EPOCH:
            self.cur_sem[eng] = self._new_sem()
            self.cnt[eng] = 0
        self.cnt[eng] += 1
        sem = self.cur_sem[eng]
        tok = (sem, self.cnt[eng], eng)
        self.streams[eng].append((waits, fn, (sem, 1)))
        self._commit(tok, reads, writes)
        self.nops += 1
        return tok

    def dma(self, q, fn, reads=(), writes=()):
        if q is None:
            q = "sp"
        waits = self._deps(q, reads, writes)
        if q == "pool":
            i = self.dma_rr_sw
            self.dma_rr_sw = (self.dma_rr_sw + 1) % self.n_sw
        else:
            i = self.n_sw + self.dma_rr
            self.dma_rr = (self.dma_rr + 1) % (len(self.dma_sems) - self.n_sw)
        sem = self.dma_sems[i]
        if self.dma_val[i] > 0:
            self._need(q, (sem, self.dma_val[i], "dma"), waits)
        self.dma_val[i] += 16
        tok = (sem, self.dma_val[i], "dma")
        self.streams[q].append((waits, fn, (sem, 16)))
        self._commit(tok, reads, writes)
        self.nops += 1
        return tok

    def finish(self, eng="sp"):
        waits = []
        for e in ("pe", "act", "dve", "pool"):
            if self.cnt[e] > 0:
                self._need(eng, (self.cur_sem[e], self.cnt[e], e), waits)
        for i, s in enumerate(self.dma_sems):
            if self.dma_val[i] > 0:
                self._need(eng, (s, self.dma_val[i], "dma"), waits)
        self.streams[eng].append((waits, None, None))

    def emit(self):
        nc = self.nc
        engmap = {"pe": "tensor", "act": "scalar", "dve": "vector", "pool": "gpsimd", "sp": "sync"}
        with nc.Block() as block:
            for e in self.ENG:
                stream = self.streams[e]
                if not stream:
                    continue

                def body(eng, stream=stream):
                    for waits, fn, inc in stream:
                        for (s, v) in waits:
                            eng.wait_ge(s, v)
                        if fn is not None:
                            ins = fn(eng)
                            if inc is not None:
                                ins.then_inc(inc[0], inc[1])
                getattr(block, engmap[e])(body)


_FREED = {}
_SCOPES = []


def newbuf(name="", excl=False):
    b = Buf(name, excl)
    b.r = list(_FREED.values())
    if _SCOPES:
        _SCOPES[-1].append(b)
    return b


@contextlib.contextmanager
def scope():
    bufs = []
    _SCOPES.append(bufs)
    with contextlib.ExitStack() as lst:
        yield lst
    _SCOPES.pop()
    for b in bufs:
        for t in ([b.w] if b.w is not None else []) + list(b.r):
            k = id(t[0])
            if k not in _FREED or _FREED[k][1] < t[1]:
                _FREED[k] = t


class Rot:
    def __init__(self, st, alloc, name, shape, dtype, n, excl=False):
        self.tiles = [st.enter_context(alloc("%s%d" % (name, i), shape, dtype)) for i in range(n)]
        self.bufs = [newbuf("%s%d" % (name, i), excl) for i in range(n)]
        self.i = 0

    def next(self):
        i = self.i
        self.i = (self.i + 1) % len(self.tiles)
        return self.tiles[i], self.bufs[i]


PP_N1W, PP_N2W, PP_NF, PP_MODB, PP_SCB, PP_SCW, PP_MCB, PP_MCW, PP_SNW, PP_MNW, PP_HB1, PP_HB2, PP_N = (
    0, 16, 32, 40, 136, 160, 280, 312, 472, 488, 504, 506, 512)
RP_DTB, RP_ALOG, RP_DSK, RP_MGB, RP_HCB, RP_HCW, RP_HDEC, RP_HSKIP, RP_N = (
    0, 32, 64, 1088, 1120, 4192, 13408, 17504, 19552)
CS_TRIU, CS_TRIL, CS_MASKF, CS_MASKB, CS_IDENT, CS_ONES, CS_N = 0, 128, 256, 384, 512, 640, 768


def _wp(L):
    return ((L + 1 + 127) // 128) * 128


class Builder:
    def __init__(self, debug=None, nlayers=DEPTH):
        self.debug = debug or {}
        self.nlayers = nlayers
        self.nc = bass.Bass("TRN2", target_bir_lowering=False)
        self.dbg_out = {}
        _FREED.clear()
        del _SCOPES[:]

    def din(self, name, shape, dt=F32):
        return self.nc.dram_tensor(name, list(shape), dt, kind="ExternalInput").ap()

    def dscr(self, name, shape, dt=F32):
        return self.nc.dram_tensor(name, list(shape), dt, kind="Internal").ap()

    def dout(self, name, shape, dt=F32):
        return self.nc.dram_tensor(name, list(shape), dt, kind="ExternalOutput").ap()

    def mm(self, out, lhsT, rhs, start, stop, reads, writes, skip=False):
        self.P.op("pe", lambda e: e.matmul(out, lhsT=lhsT, rhs=rhs, start=start, stop=stop, skip_group_check=skip),
                  reads, writes)

    def tr(self, out, in_, reads, writes):
        ident = self.identb[:]
        self.P.op("pe", lambda e: e.matmul(out, lhsT=in_, rhs=ident, start=True, stop=True), reads + [self.Bconst], writes)

    def act(self, out, in_, func, reads, writes, **kw):
        self.P.op("act", lambda e: e.activation(out=out, in_=in_, func=func, **kw), reads, writes)

    def tt(self, eng, out, in0, in1, op, reads, writes):
        self.P.op(eng, lambda e: e.tensor_tensor(out=out, in0=in0, in1=in1, op=op), reads, writes)

    def ts(self, eng, out, in0, s1, s2, op0, op1, reads, writes):
        if op1 is None:
            self.P.op(eng, lambda e: e.tensor_scalar(out=out, in0=in0, scalar1=s1, scalar2=None, op0=op0), reads, writes)
        else:
            self.P.op(eng, lambda e: e.tensor_scalar(out=out, in0=in0, scalar1=s1, scalar2=s2, op0=op0, op1=op1),
                      reads, writes)

    def stt(self, out, in0, scalar, in1, op0, op1, reads, writes):
        self.P.op("dve", lambda e: e.scalar_tensor_tensor(out=out, in0=in0, scalar=scalar, in1=in1, op0=op0, op1=op1),
                  reads, writes)

    def cp(self, eng, out, in_, reads, writes):
        self.P.op(eng, lambda e: e.tensor_copy(out=out, in_=in_), reads, writes)

    def recip(self, out, in_, reads, writes):
        self.P.op("dve", lambda e: e.reciprocal(out=out, in_=in_), reads, writes)

    def memset(self, eng, ap, val, writes):
        self.P.op(eng, lambda e: e.memset(ap, val), [], writes)

    def dma(self, q, out, in_, reads, writes):
        self.P.dma(q, lambda e: e.dma_start(out=out, in_=in_), reads, writes)

    def sbt(self, name, shape, dt):
        self.uid = getattr(self, "uid", 0) + 1
        return self.nc.sbuf_tensor("%s_%d" % (name, self.uid), list(shape), dt)

    def sb(self, name, shape, dt):
        return self.st.enter_context(self.sbt("sb_" + name, list(shape), dt))

    def dump(self, name, ap_sb, shape, buf, dt=F32):
        o = self.dout("dbg_" + name, shape, dt)
        self.dbg_out[name] = ("dbg_" + name, tuple(shape))
        self.dma("sp", o, ap_sb, [buf], [Buf()])

    def hcols(self, k, c, perm=False):
        if c < 2:
            return self.hT[:, k, CT0 + 128 * c: CT0 + 128 * c + 128]
        cl = c - 2
        if not perm:
            return self.hT[:, k, LT0 + 128 * cl: LT0 + 128 * cl + 128]
        return self.hTp[:, k, 128 * cl:128 * cl + 128]

    def htile(self, k, ti, perm=False):
        if ti == 0:
            return self.hT[:, k, CT0:CT0 + TC], TC
        q = ti - 1
        if not perm:
            return self.hT[:, k, LT0 + 512 * q: LT0 + 512 * q + 512], 512
        return self.hTp[:, k, 512 * q:512 * q + 512], 512

    def load_w(self, src, n, q="pool"):
        t, b = self.wrot.next()
        self.dma(q, t[:, :, 0:n], src.rearrange("(k p) n -> p k n", p=128), [], [b])
        return t, b

    def build(self):
        nc = self.nc
        with contextlib.ExitStack() as st:
            self.st = st
            self.P = Prog(nc, st)
            self.declare()
            self.setup()
            stop = self.debug.get("stop")
            for l in range(self.nlayers):
                need_ctx = l < DEPTH - 1
                self.stage_mod(l)
                self.stage_norm(l, first=True)
                if stop == "norm%d" % l:
                    self.dump("hT", self.hT[:], [128, 8, TP], self.BhT, BF16)
                    break
                if not self.debug.get("skip_scan"):
                    self.stage_scan(l, "ssd", need_ctx)
                    if stop == "ssd%d" % l:
                        break
                    self.stage_scan(l, "ml", need_ctx)
                    if stop == "ml%d" % l:
                        break
                self.stage_hyena(l, need_ctx)
                if stop == "hy%d" % l:
                    break
                self.stage_merge(l, need_ctx)
                if stop == "merge%d" % l:
                    break
                self.stage_norm(l, first=False, need_ctx=need_ctx)
                self.stage_mlp(l, need_ctx)
                if stop == "mlp%d" % l:
                    break
            else:
                self.stage_final()
            dmp = self.debug.get("dump")
            if dmp:
                srcs = {"ybT0": (self.ybT[0], [D, T], BF16, self.BybT[0]), "ybT1": (self.ybT[1], [D, T], BF16, self.BybT[1]),
                        "ybT2": (self.ybT[2], [D, T], BF16, self.BybT[2]), "xres": (self.xres, [D, T], F32, self.BxresL),
                        "hyK": (self.hyK, [2, _wp(TL), 2048], F32, self.BhyK),
                        "ys_tok": (self.ys_tok, [T, D], F32, self.Bys)}
                src, shp, dt, bf = srcs[dmp]
                o = self.dout("dbg_" + dmp, shp, dt)
                if len(shp) == 3:
                    o = o.rearrange("a r c -> (a r) c")
                    src = src.rearrange("a r c -> (a r) c")
                for r0 in range(0, o.shape[0], 128):
                    self.dma("sp", o[r0:r0 + 128, :], src[r0:r0 + 128, :], bf if isinstance(bf, list) else [bf], [Buf()])
            self.P.finish("sp")
            self.P.emit()
        return nc

    def declare(self):
        s = self
        s.xT = s.din("xT", [D, T])
        s.cvec = s.din("cvec", [128, 16])
        s.mod_w = s.din("mod_w", [DEPTH, D, 6 * D])
        s.w_in = s.din("w_in", [DEPTH, D, IN_COLS])
        s.w_branch = s.din("w_branch", [DEPTH, 3, D, D])
        s.w_out = s.din("w_out", [DEPTH, D, D])
        s.mlp_w1 = s.din("mlp_w1", [DEPTH, D, 4 * D])
        s.mlp_w2 = s.din("mlp_w2", [DEPTH, 4 * D, D])
        s.pp_d = s.din("pp", [128, PP_N])
        s.rowp = s.din("rowp", [DEPTH, RP_N])
        s.hyw1 = s.din("hyw1", [DEPTH, 33, 64])
        s.hyw2 = s.din("hyw2", [DEPTH, 64, 64])
        s.hyw3 = s.din("hyw3", [DEPTH, 64, 4096])
        s.cst_d = s.din("cst", [128, CS_N])
        s.sel_d = s.din("sel", [16, 16 * 128])
        s.feats = {TL: s.din("featsL", [33, TL]), TC: s.din("featsC", [33, TC])}
        s.tneg_d = s.din("tneg", [128, 18])
        s.dft = {}
        for L, nm in ((TL, "L"), (TC, "C")):
            nw = _wp(L) // 128
            nt = L // 128
            s.dft[L] = dict(
                ct=s.din("ct" + nm, [nw, 128, nt * 128], BF16), st=s.din("st" + nm, [nw, 128, nt * 128], BF16),
                ci=s.din("ci" + nm, [nt, 128, nw * 128], BF16), si=s.din("si" + nm, [nt, 128, nw * 128], BF16))
        s.out = s.dout("out", [D, TL])
        s.xres = s.dscr("xres", [D, T])
        s.ys_tok = s.dscr("ys_tok", [T, D])
        s.ybT = [s.dscr("ybT%d" % n, [D, T], BF16) for n in range(3)]
        s.hsd = s.dscr("hsd", [2, TL, 2048], BF16)
        s.hyK = s.dscr("hyK", [2, _wp(TL), 2048])
        s.BxresL = [Buf("xres%d" % k) for k in range(8)]
        s.Bys = Buf("ys_tok")
        s.BybT = [Buf("ybT%d" % n) for n in range(3)]
        s.Bhsd = Buf("hsd")
        s.BhyK = Buf("hyK")

    def setup(self):
        s = self
        nc, st = s.nc, s.st
        s.ps = Rot(st, nc.psum_tensor, "ps", [128, 512], F32, 4, excl=True)
        s.pacc = Rot(st, nc.psum_tensor, "pacc", [128, 512], F32, 4, excl=True)
        s.Bconst = Buf("const")
        s.cst = s.sb("cst", [128, CS_N], F32)
        s.sel = s.sb("sel", [16, 16 * 128], F32)
        s.pp = s.sb("pp", [128, PP_N], F32)
        s.identb = s.sb("identb", [128, 128], BF16)
        s.tneg = s.sb("tneg", [128, 18], F32)
        s.csil = s.sb("csil", [128, 16], F32)
        s.mod = s.sb("mod", [128, 96], F32)
        s.gv = s.sb("gv", [128, 64], F32)
        s.Bmod = Buf("mod")
        s.hT = s.sb("hT", [128, 8, TP], BF16)
        s.BhT = Buf("hT")
        s.wrot = Rot(st, nc.sbuf_tensor, "wr", [128, 8, 512], BF16, 3)
        c = [s.Bconst]
        s.dma("sp", s.cst[:], s.cst_d, [], c)
        s.dma("sp", s.sel[:], s.sel_d, [], c)
        s.dma("sp", s.pp[:], s.pp_d, [], c)
        s.dma("sp", s.tneg[:], s.tneg_d, [], c)
        s.dma("sp", s.csil[:], s.cvec, [], c)
        s.dma("pool", s.identb[:], s.cst_d[:, CS_IDENT:CS_IDENT + 128], [], c)
        s.act(s.csil[:], s.csil[:], AF.Silu, c, c)
        s.memset("dve", s.hT[:], 0.0, [s.BhT])
        for k in range(8):
            s.dma(None, s.xres[k * 128:(k + 1) * 128, :], s.xT[k * 128:(k + 1) * 128, :], [], [s.BxresL[k]])

    def triu(self):
        return self.cst[:, CS_TRIU:CS_TRIU + 128]

    def tril(self):
        return self.cst[:, CS_TRIL:CS_TRIL + 128]

    def ones(self):
        return self.cst[:, CS_ONES:CS_ONES + 128]

    def identf(self):
        return self.cst[:, CS_IDENT:CS_IDENT + 128]

    def stage_mod(self, l):
        s = self
        with scope() as lst:
            wm = [lst.enter_context(s.sbt("wm%d" % i, [128, 8, 512], F32)) for i in range(2)]
            wmb = [newbuf(), newbuf()]
            pt, pb = s.ps.next()
            src = s.mod_w[l].rearrange("(k p) n -> p k n", p=128)
            csv = s.csil[:, :].rearrange("p (k w) -> p k w", w=2)
            for blk in range(12):
                t, b = wm[blk % 2], wmb[blk % 2]
                s.dma(None, t[:], src[:, :, blk * 512:(blk + 1) * 512], [], [b])
                for jt in range(4):
                    j = blk * 4 + jt
                    for k in range(8):
                        s.mm(pt[:, 2 * j:2 * j + 2], t[:, k, jt * 128:(jt + 1) * 128], csv[:, k, :],
                             k == 0, k == 7, [b, s.Bconst], [pb])
            modv = s.mod[:, :].rearrange("p (j w) -> p j w", w=2)
            mb = s.pp[:, PP_MODB + l * 48:PP_MODB + l * 48 + 48].unsqueeze(2).broadcast_to([128, 48, 2])
            s.tt("dve", modv, pt[:, 0:96].rearrange("p (j w) -> p j w", w=2), mb, ALU.add, [pb, s.Bconst], [s.Bmod])
            for i, (m, npp) in enumerate(((1, PP_N1W), (4, PP_N2W))):
                gsl = s.gv[:, 16 * i:16 * i + 16].rearrange("p (j w) -> p j w", w=2)
                msl = s.mod[:, m * 16:(m + 1) * 16].rearrange("p (j w) -> p j w", w=2)
                nw = s.pp[:, npp + l * 8:npp + l * 8 + 8].unsqueeze(2).broadcast_to([128, 8, 2])
                s.stt(gsl, msl, 1.0, nw, ALU.add, ALU.mult, [s.Bmod, s.Bconst], [s.Bmod])

    def modcol(self, m, j, which):
        c = (m * 8 + j) * 2 + which
        return self.mod[:, c:c + 1]

    def gcol(self, i, j, which):
        c = 16 * i + j * 2 + which
        return self.gv[:, c:c + 1]

    def stage_norm(self, l, first, need_ctx=True):
        s = self
        gi = 0 if first else 1
        mshift = 0 if first else 3
        with scope() as lst:
            xt = Rot(lst, s.sbt, "nx", [128, 8, 512], F32, 2)
            sq = Rot(lst, s.sbt, "nsq", [128, 8, 512], F32, 1)
            rs = Rot(lst, s.sbt, "nrs", [128, 512], F32, 2)
            tm = Rot(lst, s.sbt, "ntm", [128, 512], F32, 2)
            for ti in range(5):
                if ti == 0 and not need_ctx:
                    continue
                n = TC if ti == 0 else 512
                t0 = 0 if ti == 0 else TC + 512 * (ti - 1)
                which = 1 if ti == 0 else 0
                x, xb = xt.next()
                s.dma(None, x[:, :, 0:n], s.xres[:, t0:t0 + n].rearrange("(k p) t -> p k t", p=128), list(s.BxresL), [xb])
                q, qb = sq.next()
                s.act(q[:, :, 0:n], x[:, :, 0:n], AF.Square, [xb], [qb])
                pt, pb = s.ps.next()
                for k in range(8):
                    s.mm(pt[:, 0:n], s.ones(), q[:, k, 0:n], k == 0, k == 7, [qb, s.Bconst], [pb])
                r, rb = rs.next()
                s.act(r[:, 0:n], pt[:, 0:n], AF.Sqrt, [pb], [rb], scale=1.0 / D, bias=EPS)
                s.recip(r[:, 0:n], r[:, 0:n], [rb], [rb])
                for k in range(8):
                    t, tb = tm.next()
                    s.stt(t[:, 0:n], x[:, k, 0:n], s.gcol(gi, k, which), r[:, 0:n], ALU.mult, ALU.mult,
                          [xb, rb, s.Bmod], [tb])
                    dst, _ = s.htile(k, ti)
                    s.act(dst, t[:, 0:n], AF.Identity, [tb, s.Bmod], [s.BhT], bias=s.modcol(mshift, k, which))

    def stage_gates_prep(self, l):
        pass

    def conv_tile(self, l, col0, cwcol, cbcol, perm, cin, Bcin, acc, Bacc, dst, Bdst):
        s = self
        w, wb = s.load_w(s.w_in[l][:, col0:col0 + 128], 128)
        for ti in range(5):
            pt, pb = s.ps.next()
            n = TC if ti == 0 else 512
            for k in range(8):
                rhs, _ = s.htile(k, ti, perm)
                out = pt[:, 0:n]
                s.mm(out, w[:, k, 0:128], rhs, k == 0, k == 7, [wb, s.BhT], [pb])
            off = 2 if ti == 0 else 262 + 512 * (ti - 1)
            s.act(cin[:, off:off + n], pt[:, 0:n], AF.Copy, [pb], [Bcin])
        NV = 2308
        if s.debug.get("conv_stop") == "mm":
            return
        s.ts("dve", acc[:, 0:NV], cin[:, 0:NV], s.pp[:, cwcol:cwcol + 1], None, ALU.mult, None, [Bcin, s.Bconst], [Bacc])
        for k in range(1, 5):
            s.stt(acc[:, 0:NV], cin[:, k:k + NV], s.pp[:, cwcol + k:cwcol + k + 1], acc[:, 0:NV], ALU.mult, ALU.add,
                  [Bcin, Bacc, s.Bconst], [Bacc])
        if s.debug.get("conv_stop") == "dve":
            return
        s.act(dst[:, 0:TC], acc[:, 0:TC], AF.Silu, [Bacc, s.Bconst], [Bdst], bias=s.pp[:, cbcol:cbcol + 1])
        s.act(dst[:, TC:T], acc[:, 260:260 + TL], AF.Silu, [Bacc, s.Bconst], [Bdst], bias=s.pp[:, cbcol:cbcol + 1])

    def stage_scan(self, l, kind, need_ctx):
        s = self
        ssd = kind == "ssd"
        nhd = 16 if ssd else 8
        NG = 2 * nhd
        W = 512 if ssd else 129
        VW = 512 if ssd else 130
        nunits = 2 if ssd else 8
        sc = 1.0 if ssd else 128.0 ** -0.5
        first_c = 0 if need_ctx else 2
        with scope() as lst:
            def A(name, shape, dt, stack=None):
                return (stack or lst).enter_context(s.sbt("sc_" + kind + name, list(shape), dt))
            if not ssd:
                s.hTp = A("hTp", [128, 8, TL], BF16)
                for k in range(8):
                    src = s.hT[:, k, LT0:LT0 + TL].rearrange("p (r w) -> p w r", w=64)
                    dst = s.hTp[:, k, :].rearrange("p (w r) -> p w r", r=32)
                    if k % 2 == 0:
                        s.act(dst, src, AF.Copy, [s.BhT], [s.BhT])
                    else:
                        s.P.op("dve", (lambda o, i_: (lambda e: e.tensor_copy(out=o, in_=i_)))(dst, src), [s.BhT], [s.BhT])
            la = A("la", [128, NCH, NG], F32)
            wl = A("wl", [128, NCH, NG], F32)
            cum = A("cum", [128, NCH, NG], F32)
            tot = A("tot", [128, NCH, NG], F32)
            bias = A("bias", [128, NCH, NG], F32)
            ecum = A("ecum", [128, NCH, NG], F32)
            gst = A("gst", [128, NCH, NG], F32)
            atot = A("atot", [128, NCH, NG], F32)
            Bd = newbuf("decay")
            with scope() as pl_:
                tmpa = A("tmpa", [128, NCH, 32], F32, pl_)
                tmpb = A("tmpb", [128, NCH, 32], F32, pl_)
                rowb = A("rowb", [128, 64], F32, pl_)
                Bt = newbuf("dtmp")
                gcol0 = 2560 if ssd else ML0 + 4096
                wg, wgb = s.load_w(s.w_in[l][:, gcol0:gcol0 + 32], 32)
                if ssd:
                    s.dma("sp", rowb[:, 0:64], s.rowp[l:l + 1, RP_DTB:RP_DTB + 64].broadcast_to([128, 64]), [], [Bt])
                    s.act(rowb[:, 32:64], rowb[:, 32:64], AF.Exp, [Bt], [Bt])
                else:
                    s.dma("sp", rowb[:, 0:32], s.rowp[l:l + 1, RP_MGB:RP_MGB + 32].broadcast_to([128, 32]), [], [Bt])
                for half in range(2):
                    pt, pb = s.ps.next()
                    for ci in range(9):
                        c = half * 9 + ci
                        for k in range(8):
                            s.mm(pt[:, ci * 32:ci * 32 + 32], s.hcols(k, c, perm=not ssd), wg[:, k, 0:32], k == 0, k == 7,
                                 [wgb, s.BhT], [pb])
                    s.tt("dve", tmpa[:, half * 9:half * 9 + 9, :], pt[:, 0:288].rearrange("p (c g) -> p c g", g=32),
                         rowb[:, 0:32].unsqueeze(1).broadcast_to([128, 9, 32]), ALU.add, [pb, Bt], [Bt])
                if ssd:
                    s.act(tmpb[:], tmpa[:], AF.Exp, [Bt], [Bt])
                    s.act(tmpb[:], tmpb[:], AF.Ln, [Bt], [Bt], bias=1.0)
                    s.act(wl[:], tmpb[:], AF.Ln, [Bt], [Bd])
                    s.stt(la[:], tmpb[:], -1.0, rowb[:, 32:64].unsqueeze(1).broadcast_to([128, NCH, 32]), ALU.mult, ALU.mult,
                          [Bt], [Bd])
                else:
                    for d in range(2):
                        s.act(wl[:, :, d * 8:d * 8 + 8], tmpa[:, :, d * 16:d * 16 + 8], AF.Copy, [Bt], [Bd])
                        s.act(tmpb[:, :, d * 8:d * 8 + 8], tmpa[:, :, d * 16 + 8:d * 16 + 16], AF.Exp, [Bt], [Bt], scale=-1.0)
                    s.act(tmpb[:, :, 0:16], tmpb[:, :, 0:16], AF.Ln, [Bt], [Bt], bias=1.0)
                    s.ts("dve", la[:], tmpb[:, :, 0:16], -1.0, None, ALU.mult, None, [Bt], [Bd])
            for half in range(2):
                pc, pcb = s.ps.next()
                ptt, ptb = s.ps.next()
                for ci in range(9):
                    c = half * 9 + ci
                    o = ci * NG
                    s.mm(pc[:, o:o + nhd], s.triu(), la[:, c, 0:nhd], True, True, [Bd, s.Bconst], [pcb])
                    s.mm(pc[:, o + nhd:o + NG], s.tril(), la[:, c, nhd:NG], True, True, [Bd, s.Bconst], [pcb])
                    s.mm(ptt[:, o:o + NG], s.ones(), la[:, c, :], True, True, [Bd, s.Bconst], [ptb])
                s.act(cum[:, half * 9:half * 9 + 9, :], pc[:, 0:9 * NG].rearrange("p (c g) -> p c g", g=NG), AF.Copy,
                      [pcb], [Bd])
                s.act(tot[:, half * 9:half * 9 + 9, :], ptt[:, 0:9 * NG].rearrange("p (c g) -> p c g", g=NG), AF.Copy,
                      [ptb], [Bd])
            s.tt("dve", bias[:], wl[:], cum[:], ALU.subtract, [Bd], [Bd])
            s.act(ecum[:], cum[:], AF.Exp, [Bd], [Bd])
            s.tt("dve", gst[:], bias[:], tot[:], ALU.add, [Bd], [Bd])
            s.act(gst[:], gst[:], AF.Exp, [Bd], [Bd])
            s.act(atot[:], tot[:], AF.Exp, [Bd], [Bd])
            if s.debug.get("dump_decay") == kind:
                s.dump("la", la[:], [128, NCH, NG], Bd)
                s.dump("cum", cum[:], [128, NCH, NG], Bd)
                s.dump("wl", wl[:], [128, NCH, NG], Bd)
            if s.debug.get("scan_stop") == "decay":
                return
            QT = A("QT", [128, T], BF16)
            KT = A("KT", [128, T], BF16)
            Ktok = A("Ktok", [128, NCH, 128], BF16)
            V = A("V", [128, NCH, VW], BF16)
            Sin = [A("Sin0", [128, NCH, VW], BF16), A("Sin1", [128, NCH, VW], BF16)]
            S = [A("S0", [128, VW], F32), A("S1", [128, VW], F32)]
            BQT, BKT, BKtok, BV = newbuf(), newbuf(), newbuf(), newbuf()
            BSin = [newbuf(), newbuf()]
            BS = [newbuf(), newbuf()]
            ssq = A("ssq", [128, 2 * NCH], F32)
            Bssq = newbuf()
            s.memset("dve", ssq[:], 0.0, [Bssq])
            if ssd:
                dsk = A("dsk", [128, 1024], F32)
                Bdsk = newbuf()
                s.dma("sp", dsk[:], s.rowp[l:l + 1, RP_DSK:RP_DSK + 1024].broadcast_to([128, 1024]), [], [Bdsk])
            else:
                s.memset("dve", V[:], 0.0, [BV])
                s.memset("dve", V[:, :, 128:129], 1.0, [BV])

            for u in range(nunits):
                with scope() as cl_:
                    cin = A("cin", [128, 2312], F32, cl_)
                    acc = A("acc", [128, 2312], F32, cl_)
                    Bcin, Bacc = newbuf(), newbuf()
                    s.memset("dve", cin[:], 0.0, [Bcin])
                    if ssd:
                        g = u
                        cout = Rot(cl_, s.sbt, "sc_cout", [128, T], BF16, 1)
                        for i in range(4):
                            co, cob = cout.next()
                            idx = 4 * g + i
                            s.conv_tile(l, 512 * g + 128 * i, PP_SCW + 60 * l + 5 * idx, PP_SCB + 12 * l + idx, False,
                                        cin, Bcin, acc, Bacc, co, cob)
                            if s.debug.get("conv_stop") in ("mm", "dve", "silu"):
                                return
                            for c0 in range(0, NCH, 4):
                                ncs = min(4, NCH - c0)
                                po, pob = s.ps.next()
                                for ci in range(ncs):
                                    c = c0 + ci
                                    s.tr(po[:, ci * 128:ci * 128 + 128], co[:, c * 128:c * 128 + 128], [cob], [pob])
                                s.act(V[:, c0:c0 + ncs, i * 128:i * 128 + 128],
                                      po[:, 0:ncs * 128].rearrange("p (c e) -> p c e", e=128), AF.Copy, [pob], [BV])
                            if s.debug.get("conv_stop") == "tr1":
                                return
                        if s.debug.get("conv_stop") == "x":
                            return
                        s.conv_tile(l, 1024 + 128 * g, PP_SCW + 60 * l + 5 * (8 + g), PP_SCB + 12 * l + 8 + g, False,
                                    cin, Bcin, acc, Bacc, KT, BKT)
                        s.conv_tile(l, 1280 + 128 * g, PP_SCW + 60 * l + 5 * (10 + g), PP_SCB + 12 * l + 10 + g, False,
                                    cin, Bcin, acc, Bacc, QT, BQT)
                    else:
                        hd = u
                        s.conv_tile(l, ML0 + 128 * hd, PP_MCW + 80 * l + 5 * hd, PP_MCB + 16 * l + hd, True,
                                    cin, Bcin, acc, Bacc, QT, BQT)
                        s.conv_tile(l, ML0 + 1024 + 128 * hd, PP_MCW + 80 * l + 5 * (8 + hd), PP_MCB + 16 * l + 8 + hd, True,
                                    cin, Bcin, acc, Bacc, KT, BKT)
                if s.debug.get("conv_stop") == "bc":
                    return
                if ssd:
                    wz, wzb = s.load_w(s.w_in[l][:, 1536 + 512 * u:1536 + 512 * u + 512], 512)
                else:
                    hd = u
                    wv, wvb = s.load_w(s.w_in[l][:, ML0 + 2048 + 128 * hd:ML0 + 2048 + 128 * hd + 128], 128)
                    for c in range(NCH):
                        pt, pb = s.ps.next()
                        for k in range(8):
                            s.mm(pt[:, 0:128], s.hcols(k, c, True), wv[:, k, 0:128], k == 0, k == 7, [wvb, s.BhT], [pb])
                        s.act(V[:, c, 0:128], pt[:, 0:128], AF.Copy, [pb], [BV])
                    wz, wzb = s.load_w(s.w_in[l][:, ML0 + 3072 + 128 * hd:ML0 + 3072 + 128 * hd + 128], 128)
                for c0 in range(0, NCH, 4):
                    ncs = min(4, NCH - c0)
                    po, pob = s.ps.next()
                    for ci in range(ncs):
                        c = c0 + ci
                        s.tr(po[:, ci * 128:ci * 128 + 128], KT[:, c * 128:c * 128 + 128], [BKT], [pob])
                    s.act(Ktok[:, c0:c0 + ncs, :], po[:, 0:ncs * 128].rearrange("p (c e) -> p c e", e=128), AF.Copy,
                          [pob], [BKtok], scale=sc)

                if s.debug.get("scan_stop") == "conv":
                    return

                def gsl(arr, c, d):
                    if ssd:
                        return arr[:, c, d * 16 + 8 * u:d * 16 + 8 * u + 8].unsqueeze(2).broadcast_to([128, 8, 64])
                    return arr[:, c, d * 8 + u:d * 8 + u + 1]

                with scope() as ul_:
                    ScT = Rot(ul_, s.sbt, "sc_ScT" + kind, [128, 128], BF16, 3)
                    DT = Rot(ul_, s.sbt, "sc_DT" + kind, [128, 4, 128], BF16, 2)
                    PT = Rot(ul_, s.sbt, "sc_PT" + kind, [128, 4, 128], BF16, 8)
                    Vw = Rot(ul_, s.sbt, "sc_Vw" + kind, [128, VW], BF16, 3)
                    zs = Rot(ul_, s.sbt, "sc_zs" + kind, [128, W if ssd else 128], F32, 2 if ssd else 3)
                    t1 = Rot(ul_, s.sbt, "sc_t1" + kind, [128, VW], F32, 2)
                    t2 = Rot(ul_, s.sbt, "sc_t2" + kind, [128, VW], F32, 2)
                    t3 = Rot(ul_, s.sbt, "sc_t3" + kind, [128, VW], F32, 2)
                    yo = Rot(ul_, s.sbt, "sc_yo" + kind, [128, VW], F32 if ssd else BF16, 2 if ssd else 3)
                    sm = Rot(ul_, s.sbt, "sc_sm" + kind, [128, 8], F32, 2)
                    ymc = Rot(ul_, s.sbt, "sc_ymc" + kind, [128, 128], BF16, 2)
                    cTr = Rot(ul_, s.sbt, "sc_cT" + kind, [16, 128], F32, 4)
                    orders = [list(range(NCH)), [1, 0] + list(range(NCH - 1, 1, -1))]
                    for d in range(2):
                        s.memset("dve", S[d][:], 0.0, [BS[d]])
                    for oi in range(NCH):
                        for d in range(2):
                            c = orders[d][oi]
                            s.P.op("dve", (lambda o, i_: (lambda e: e.tensor_copy(out=o, in_=i_)))(Sin[d][:, c, :], S[d][:]),
                                   [BS[d]], [BSin[d]])
                            if oi == NCH - 1:
                                continue
                            vw, vwb = Vw.next()
                            if ssd:
                                s.tt("pool", vw[:, :].rearrange("p (h e) -> p h e", e=64),
                                     V[:, c, :].rearrange("p (h e) -> p h e", e=64), gsl(gst, c, d), ALU.mult,
                                     [BV, Bd], [vwb])
                            else:
                                s.act(vw[:, :], V[:, c, :], AF.Identity, [BV, Bd], [vwb], scale=gsl(gst, c, d))
                            pl, plb = s.ps.next()
                            s.mm(pl[:, 0:W], Ktok[:, c, :], vw[:, 0:W], True, True, [BKtok, vwb], [plb])
                            if ssd:
                                sv = S[d][:, :].rearrange("p (h e) -> p h e", e=64)
                                s.tt("dve", sv, sv, gsl(atot, c, d), ALU.mult, [BS[d], Bd], [BS[d]])
                                s.tt("dve", S[d][:, 0:W], S[d][:, 0:W], pl[:, 0:W], ALU.add, [BS[d], plb], [BS[d]])
                            else:
                                s.stt(S[d][:, 0:W], S[d][:, 0:W], gsl(atot, c, d), pl[:, 0:W], ALU.mult, ALU.add,
                                      [BS[d], plb, Bd], [BS[d]])

                    if s.debug.get("scan_stop") == "pass1":
                        return
                    def front(c):
                        cs = slice(c * 128, c * 128 + 128)
                        pS, pSb = s.ps.next()
                        s.mm(pS[:, 0:128], KT[:, cs], QT[:, cs], True, True, [BKT, BQT], [pSb])
                        sct, sctb = ScT.next()
                        s.ts("dve", sct[:], pS[:, 0:128], sc, None, ALU.mult, None, [pSb], [sctb])
                        pz, pzb = s.ps.next()
                        nz = 512 if ssd else 128
                        for k in range(8):
                            s.mm(pz[:, 0:nz], s.hcols(k, c, perm=not ssd), wz[:, k, 0:nz], k == 0, k == 7,
                                 [wzb, s.BhT], [pzb])
                        z, zb = zs.next()
                        s.act(z[:, 0:nz], pz[:, 0:nz], AF.Silu if ssd else AF.Sigmoid, [pzb], [zb])
                        aset = c % 2
                        pA0, pAb0 = s.pacc.tiles[aset], s.pacc.bufs[aset]
                        pA = [pA0, pA0 if ssd else pA0[:, 256:512]]
                        pts = []
                        for d in range(2):
                            mask = s.cst[:, CS_MASKF:CS_MASKF + 128] if d == 0 else s.cst[:, CS_MASKB:CS_MASKB + 128]
                            pc_, pcb_ = s.ps.next()
                            s.mm(pc_[0:nhd, 0:128], la[:, c, d * nhd:(d + 1) * nhd], s.triu() if d == 0 else s.tril(),
                                 True, True, [Bd, s.Bconst], [pcb_])
                            cT, cTb = cTr.next()
                            s.cp("dve", cT[0:nhd, :], pc_[0:nhd, 0:128], [pcb_], [cTb])
                            ngrp = 2 if ssd else 1
                            for hq in range(ngrp):
                                nh4 = 4 if ssd else 1
                                pD, pDb = s.ps.next()
                                dt_, dtb = DT.next()
                                for hh in range(nh4):
                                    hl = (8 * u + hq * 4 + hh) if ssd else u
                                    s.mm(pD[:, hh * 128:hh * 128 + 128], s.sel[0:nhd, hl * 128:hl * 128 + 128],
                                         cT[0:nhd, :], True, False, [cTb, s.Bconst], [pDb])
                                    s.mm(pD[:, hh * 128:hh * 128 + 128], s.identf(), mask, False, True, [s.Bconst], [pDb])
                                    s.act(dt_[:, hh, :], pD[:, hh * 128:hh * 128 + 128], AF.Exp, [pDb, Bd], [dtb],
                                          bias=bias[:, c, d * nhd + hl:d * nhd + hl + 1])
                                p_, ptb_ = PT.next()
                                s.tt("dve", p_[:, 0:nh4, :], dt_[:, 0:nh4, :],
                                     sct[:, :].unsqueeze(1).broadcast_to([128, nh4, 128]), ALU.mult, [dtb, sctb], [ptb_])
                                pts.append((p_, ptb_, d, hq, nh4))
                        return (z, zb, pA, pAb0, pts)

                    def frontB(c, fr):
                        z, zb, pA, pAb0, pts = fr
                        for (p_, ptb_, d, hq, nh4) in pts:
                            for hh in range(nh4):
                                if ssd:
                                    h = hq * 4 + hh
                                    s.mm(pA[0][:, h * 64:h * 64 + 64], p_[:, hh, :], V[:, c, h * 64:h * 64 + 64],
                                         d == 0 and h == 0, d == 1, [ptb_, BV], [pAb0], skip=True)
                                else:
                                    s.mm(pA[d][:, 0:W], p_[:, 0, :], V[:, c, 0:W], True, True, [ptb_, BV], [pAb0])

                    def back(c, fr):
                        z, zb, pA, pAb0, pts = fr
                        cs = slice(c * 128, c * 128 + 128)
                        pAb = [pAb0, pAb0]
                        if ssd:
                            pB = [s.pacc.tiles[2], s.pacc.tiles[3]]
                            pBb = [s.pacc.bufs[2], s.pacc.bufs[3]]
                        else:
                            pB = [s.pacc.tiles[2], s.pacc.tiles[2][:, 256:512]]
                            pBb = [s.pacc.bufs[2], s.pacc.bufs[2]]
                        for d in range(2):
                            s.mm(pB[d][:, 0:W], QT[:, cs], Sin[d][:, c, 0:W], True, True, [BQT, BSin[d]], [pBb[d]])
                        if ssd:
                            a1, a1b = t1.next()
                            a2, a2b = t2.next()
                            a3, a3b = t3.next()
                            v3 = lambda t: t[:, :].rearrange("p (h e) -> p h e", e=64)
                            s.tt("dve", v3(a1), pB[0][:, 0:512].rearrange("p (h e) -> p h e", e=64), gsl(ecum, c, 0),
                                 ALU.mult, [pBb[0], Bd], [a1b])
                            s.tt("dve", v3(a2), pB[1][:, 0:512].rearrange("p (h e) -> p h e", e=64), gsl(ecum, c, 1),
                                 ALU.mult, [pBb[1], Bd], [a2b])
                            s.tt("pool", a3[:, :], V[:, c, :], dsk[:, 512 * u:512 * u + 512], ALU.mult, [BV, Bdsk], [a3b])
                            s.tt("pool", a1[:, :], a1[:, :], a2[:, :], ALU.add, [a1b, a2b], [a1b])
                            s.tt("pool", a1[:, :], a1[:, :], a3[:, :], ALU.add, [a1b, a3b], [a1b])
                            s.tt("dve", a1[:, :], a1[:, :], pA[0][:, 0:512], ALU.add, [a1b, pAb[0]], [a1b])
                            y, yb = yo.next()
                            s.tt("dve", y[:, :], a1[:, :], z[:, 0:512], ALU.mult, [a1b, zb], [yb])
                            s.act(a2[:, :], y[:, :], AF.Square, [yb, a2b], [a2b, Bssq],
                                  accum_out=ssq[:, 2 * c + u:2 * c + u + 1])
                            s.dma(None, s.ys_tok[c * 128:c * 128 + 128, 512 * u:512 * u + 512], y[:, :], [yb], [s.Bys])
                        else:
                            yd = []
                            ydb = []
                            small, smb = sm.next()
                            for d in range(2):
                                ta, tab = (t1 if d == 0 else t2).next()
                                s.act(ta[:, 0:W], pA[d][:, 0:W], AF.Copy, [pAb[d]], [tab])
                                s.stt(ta[:, 0:W], pB[d][:, 0:W], gsl(ecum, c, d), ta[:, 0:W], ALU.mult, ALU.add,
                                      [pBb[d], tab, Bd], [tab])
                                s.act(small[:, d:d + 1], ta[:, 128:129], AF.Abs, [tab], [smb])
                                s.ts("dve", small[:, d:d + 1], small[:, d:d + 1], 1.0, None, ALU.max, None, [smb], [smb])
                                s.recip(small[:, 2 + d:3 + d], small[:, d:d + 1], [smb], [smb])
                                yd.append(ta)
                                ydb.append(tab)
                            a3, a3b = t3.next()
                            s.ts("dve", a3[:, 0:128], yd[0][:, 0:128], small[:, 2:3], None, ALU.mult, None,
                                 [ydb[0], smb], [a3b])
                            s.stt(a3[:, 0:128], yd[1][:, 0:128], small[:, 3:4], a3[:, 0:128], ALU.mult, ALU.add,
                                  [ydb[1], smb, a3b], [a3b])
                            s.act(yd[0][:, 0:128], a3[:, 0:128], AF.Square, [a3b, ydb[0]], [ydb[0], smb],
                                  accum_out=small[:, 4:5])
                            s.act(small[:, 5:6], small[:, 4:5], AF.Sqrt, [smb], [smb], scale=1.0 / 128.0, bias=EPS)
                            s.recip(small[:, 6:7], small[:, 5:6], [smb], [smb])
                            y, yb = yo.next()
                            s.stt(y[:, 0:128], a3[:, 0:128], small[:, 6:7], z[:, 0:128], ALU.mult, ALU.mult,
                                  [a3b, smb, zb], [yb])
                            return (y, yb)
                        return None

                    def tail(c, yy):
                        y, yb = yy
                        po_, pob = s.ps.next()
                        po = po_[:, 0:128]
                        s.tr(po, y[:, 0:128], [yb], [pob])
                        ym, ymb = ymc.next()
                        s.ts("dve", ym[:, :], po, s.pp[:, PP_MNW + 8 * l + u:PP_MNW + 8 * l + u + 1], None, ALU.mult, None,
                             [pob, s.Bconst], [ymb])
                        s.dma(None, s.ybT[1][u * 128:u * 128 + 128, c * 128:c * 128 + 128], ym[:, :], [ymb],
                              [s.BybT[1]])

                    chunks = list(range(first_c, NCH))
                    fr = front(chunks[0])
                    pend = None
                    for ci, c in enumerate(chunks):
                        nxt = front(chunks[ci + 1]) if ci + 1 < len(chunks) else None
                        frontB(c, fr)
                        if pend is not None:
                            tail(*pend)
                        yy = back(c, fr)
                        pend = (c, yy) if yy is not None else None
                        fr = nxt
                    if pend is not None:
                        tail(*pend)

            if s.debug.get("scan_dump") == kind:
                s.dump("V", V[:], [128, NCH, VW], BV, BF16)
                s.dump("KT", KT[:], [128, T], BKT, BF16)
                s.dump("QT", QT[:], [128, T], BQT, BF16)
                s.dump("Ktok", Ktok[:], [128, NCH, 128], BKtok, BF16)
                s.dump("Sin0", Sin[0][:], [128, NCH, VW], BSin[0], BF16)
                s.dump("Sin1", Sin[1][:], [128, NCH, VW], BSin[1], BF16)
                s.dump("ssq", ssq[:], [128, 2 * NCH], Bssq)
            if ssd:
                ytr = Rot(lst, s.sbt, "sc_ytr", [128, 1024], F32, 3)
                ynr = Rot(lst, s.sbt, "sc_ynr", [128, 1024], BF16, 2)
                ysc = Rot(lst, s.sbt, "sc_ysc", [128, 8, 128], BF16, 2)
                sm2 = Rot(lst, s.sbt, "sc_sm2", [128, 8], F32, 2)
                def yload(c):
                    yt, ytb = ytr.next()
                    s.dma(None, yt[:, :], s.ys_tok[c * 128:c * 128 + 128, :], [s.Bys], [ytb])
                    return yt, ytb
                ynext = yload(first_c)
                for c in range(first_c, NCH):
                    yt, ytb = ynext
                    ynext = yload(c + 1) if c + 1 < NCH else None
                    small, smb = sm2.next()
                    s.tt("dve", small[:, 0:1], ssq[:, 2 * c:2 * c + 1], ssq[:, 2 * c + 1:2 * c + 2], ALU.add, [Bssq], [smb])
                    s.act(small[:, 1:2], small[:, 0:1], AF.Sqrt, [smb], [smb], scale=1.0 / 1024.0, bias=EPS)
                    s.recip(small[:, 2:3], small[:, 1:2], [smb], [smb])
                    yn, ynb = ynr.next()
                    s.ts("dve", yn[:, :], yt[:, :], small[:, 2:3], None, ALU.mult, None, [ytb, smb], [ynb])
                    yc, ycb = ysc.next()
                    for j0 in range(0, 8, 4):
                        po, pob = s.ps.next()
                        for ji in range(4):
                            j = j0 + ji
                            s.tr(po[:, ji * 128:ji * 128 + 128], yn[:, j * 128:j * 128 + 128], [ynb], [pob])
                        for ji in range(4):
                            j = j0 + ji
                            s.act(yc[:, j, :], po[:, ji * 128:ji * 128 + 128], AF.Identity, [pob, s.Bconst], [ycb],
                                  scale=s.pp[:, PP_SNW + 8 * l + j:PP_SNW + 8 * l + j + 1])
                    s.dma(None, s.ybT[0][:, c * 128:c * 128 + 128].rearrange("(j p) t -> p j t", p=128), yc[:, :, :],
                          [ycb], [s.BybT[0]])

    def stage_hyena(self, l, need_ctx):
        self.hyena_seq(l, TL, LT0, TC, 0)
        if need_ctx:
            self.hyena_seq(l, TC, CT0, 0, 16)

    def bload(self, dst, row_ap, n, buf):
        self.dma("sp", dst, row_ap.broadcast_to([128, n]), [], [buf])

    def hyena_seq(self, l, L, col0, tok0, tn0):
        s = self
        nt = L // 128
        nw = _wp(L) // 128
        dft = s.dft[L]
        TWO_PI = 2.0 * math.pi
        MAGIC = 12582912.0
        with scope() as lst:
            def A(name, shape, dt):
                return lst.enter_context(s.sbt("hy_" + name, list(shape), dt))
            feats = A("feats", [33, L], F32)
            w1 = A("w1", [33, 64], F32)
            w2 = A("w2", [64, 64], F32)
            w3 = A("w3", [64, 4096], F32)
            hid = [A("hid1", [64, L], F32), A("hid2", [64, L], F32)]
            absdec = A("absdec", [128, 4096], F32)
            hrow = A("hrow", [128, 4096], F32)
            Bw, Bh, Bdec, Brow = newbuf(), [newbuf(), newbuf()], newbuf(), newbuf()
            ta = Rot(lst, s.sbt, "hy_ta", [64, 512], F32, 2)
            tb = Rot(lst, s.sbt, "hy_tb", [64, 512], F32, 2)
            win = Rot(lst, s.sbt, "hy_win", [128, 512], F32, 2)
            hso = Rot(lst, s.sbt, "hy_hso", [128, 2048], BF16, 2)
            hdo = Rot(lst, s.sbt, "hy_hdo", [128, 2048], BF16, 2)
            s.dma("sp", feats[:], s.feats[L], [], [Bw])
            s.dma("sp", w1[:], s.hyw1[l], [], [Bw])
            s.dma("sp", w2[:], s.hyw2[l], [], [Bw])
            s.dma("sp", w3[:], s.hyw3[l], [], [Bw])
            s.bload(absdec[:], s.rowp[l:l + 1, RP_HDEC:RP_HDEC + 4096], 4096, Bdec)
            s.act(absdec[:], absdec[:], AF.Abs, [Bdec], [Bdec])
            nq = max(1, L // 512)
            qn = min(512, L)
            for layer in range(2):
                wgt = w1 if layer == 0 else w2
                bcol = (PP_HB1 if layer == 0 else PP_HB2) + l
                for q in range(nq):
                    pt, pb = s.ps.next()
                    src = feats[:, q * qn:(q + 1) * qn] if layer == 0 else hid[0][:, q * qn:(q + 1) * qn]
                    s.mm(pt[0:64, 0:qn], wgt[:, :], src, True, True, [Bw] + ([Bh[0]] if layer else []), [pb])
                    a, ab = ta.next()
                    b, bb = tb.next()
                    s.ts("dve", a[:, 0:qn], pt[0:64, 0:qn], s.pp[0:64, bcol:bcol + 1], None, ALU.add, None, [pb, s.Bconst], [ab])
                    s.ts("dve", b[:, 0:qn], a[:, 0:qn], math.pi, None, ALU.is_gt, None, [ab], [bb])
                    s.stt(a[:, 0:qn], b[:, 0:qn], -TWO_PI, a[:, 0:qn], ALU.mult, ALU.add, [bb, ab], [ab])
                    s.ts("dve", b[:, 0:qn], a[:, 0:qn], -math.pi, None, ALU.is_lt, None, [ab], [bb])
                    s.stt(a[:, 0:qn], b[:, 0:qn], TWO_PI, a[:, 0:qn], ALU.mult, ALU.add, [bb, ab], [ab])
                    s.act(hid[layer][:, q * qn:(q + 1) * qn], a[:, 0:qn], AF.Sin, [ab], [Bh[layer]])
            for tc in range(nt):
                for cb in range(8):
                    pt, pb = s.ps.next()
                    s.mm(pt[:, 0:512], hid[1][:, tc * 128:tc * 128 + 128], w3[:, cb * 512:cb * 512 + 512], True, True,
                         [Bh[1], Bw], [pb])
                    w_, wb_ = win.next()
                    s.act(w_[:, :], absdec[:, cb * 512:cb * 512 + 512], AF.Exp, [Bdec, s.Bconst], [wb_],
                          scale=s.tneg[:, tn0 + tc:tn0 + tc + 1])
                    s.tt("dve", hrow[:, cb * 512:cb * 512 + 512], pt[:, 0:512], w_[:, :], ALU.mult, [pb, wb_], [Brow])
                hs, hsb = hso.next()
                hd, hdb = hdo.next()
                for o in range(2):
                    hf = hrow[:, o * 2048:o * 2048 + 1024]
                    hb = hrow[:, o * 2048 + 1024:o * 2048 + 2048]
                    s.tt("dve", hs[:, o * 1024:o * 1024 + 1024], hf, hb, ALU.add, [Brow], [hsb])
                    s.tt("pool", hd[:, o * 1024:o * 1024 + 1024], hb, hf, ALU.subtract, [Brow], [hdb])
                s.dma(None, s.hsd[0][tc * 128:tc * 128 + 128, :], hs[:, :], [hsb], [s.Bhsd])
                s.dma(None, s.hsd[1][tc * 128:tc * 128 + 128, :], hd[:, :], [hdb], [s.Bhsd])
            if s.debug.get("hy_dump") and L == TL:
                s.dump("hid1", hid[0][:], [64, L], Bh[0])
                s.dump("hid2", hid[1][:], [64, L], Bh[1])
                s.dump("hrow", hrow[:], [128, 4096], Brow)
                s.dump("hs", hs[:], [128, 2048], hsb, BF16)
        with scope() as lst:
            hsb_ = Rot(lst, s.sbt, "hy_hsb", [128, nt, 1024], BF16, 1)
            hdb_ = Rot(lst, s.sbt, "hy_hdb", [128, nt, 1024], BF16, 1)
            ctl = Rot(lst, s.sbt, "hy_ct", [128, nt * 128], BF16, 2)
            stl = Rot(lst, s.sbt, "hy_st", [128, nt * 128], BF16, 2)
            kr = Rot(lst, s.sbt, "hy_kr", [128, 512], F32, 2)
            ki = Rot(lst, s.sbt, "hy_ki", [128, 512], F32, 2)
            for cbp in range(2):
                hsT, hsB = hsb_.next()
                hdT, hdB = hdb_.next()
                for tc in range(nt):
                    s.dma(None, hsT[:, tc, :], s.hsd[0][tc * 128:tc * 128 + 128, cbp * 1024:cbp * 1024 + 1024], [s.Bhsd], [hsB])
                    s.dma(None, hdT[:, tc, :], s.hsd[1][tc * 128:tc * 128 + 128, cbp * 1024:cbp * 1024 + 1024], [s.Bhsd], [hdB])
                for wt in range(nw):
                    ct, ctb = ctl.next()
                    st_, stb = stl.next()
                    s.dma(None, ct[:], dft["ct"][wt], [], [ctb])
                    s.dma(None, st_[:], dft["st"][wt], [], [stb])
                    for half in range(2):
                        cb = cbp * 2 + half
                        hc = slice(half * 512, half * 512 + 512)
                        pR, pRb = s.ps.next()
                        pI, pIb = s.ps.next()
                        for tc in range(nt):
                            s.mm(pR[:, 0:512], ct[:, tc * 128:tc * 128 + 128], hsT[:, tc, hc], tc == 0, tc == nt - 1,
                                 [ctb, hsB], [pRb])
                        for tc in range(nt):
                            s.mm(pI[:, 0:512], st_[:, tc * 128:tc * 128 + 128], hdT[:, tc, hc], tc == 0, tc == nt - 1,
                                 [stb, hdB], [pIb])
                        kre, kreb = kr.next()
                        kim, kimb = ki.next()
                        s.act(kre[:, :], pR[:, 0:512], AF.Copy, [pRb], [kreb])
                        s.cp("dve", kim[:, :], pI[:, 0:512], [pIb], [kimb])
                        s.dma(None, s.hyK[0][wt * 128:wt * 128 + 128, cb * 512:cb * 512 + 512], kre[:, :], [kreb], [s.BhyK])
                        s.dma(None, s.hyK[1][wt * 128:wt * 128 + 128, cb * 512:cb * 512 + 512], kim[:, :], [kimb], [s.BhyK])
        if s.debug.get("hy_stop") == "filt":
            return
        for cb2 in range(2):
            with scope() as lst:
                z = lst.enter_context(s.sbt("hy_z", [128, nt, 512], BF16))
                Yre = lst.enter_context(s.sbt("hy_yre", [128, nw, 512], BF16))
                Yim = lst.enter_context(s.sbt("hy_yim", [128, nw, 512], BF16))
                Wk = [lst.enter_context(s.sbt("hy_wk%d" % i, [128, 8, 512], BF16)) for i in range(3)]
                rowt = lst.enter_context(s.sbt("hy_rowt", [128, 3, 512], F32))
                cbias = lst.enter_context(s.sbt("hy_cbias", [128, 512], F32))
                skipb = lst.enter_context(s.sbt("hy_skipb", [128, 512], F32))
                Bz, BY, BWk, Brow, Bcb, Bsk = newbuf(), newbuf(), newbuf(), newbuf(), newbuf(), newbuf()
                kr = Rot(lst, s.sbt, "hy_kr2", [128, 512], F32, 2)
                ki = Rot(lst, s.sbt, "hy_ki2", [128, 512], F32, 2)
                tmp = [Rot(lst, s.sbt, "hy_tmp%d" % i, [128, 512], F32, 1) for i in range(4)]
                xg = Rot(lst, s.sbt, "hy_xg", [128, 512], F32, 2)
                yb16 = Rot(lst, s.sbt, "hy_yb16", [128, 512], BF16, 2)
                yT = Rot(lst, s.sbt, "hy_yT", [128, 4, 128], BF16, 2)

                def prep_part(part):
                    c0 = HY0 + part * 1024 + cb2 * 512
                    w, wb = s.load_w(s.w_in[l][:, c0:c0 + 512], 512)
                    for tap in range(3):
                        o = RP_HCW + tap * 3072 + part * 1024 + cb2 * 512
                        s.bload(rowt[:, tap, :], s.rowp[l:l + 1, o:o + 512], 512, Brow)
                    o = RP_HCB + part * 1024 + cb2 * 512
                    s.bload(cbias[:, :], s.rowp[l:l + 1, o:o + 512], 512, Bcb)
                    for tap in range(3):
                        s.tt("dve" if tap != 1 else "pool", Wk[tap][:], w[:, :, :],
                             rowt[:, tap, :].unsqueeze(1).broadcast_to([128, 8, 512]), ALU.mult, [wb, Brow], [BWk])

                def proj3(tc):
                    pt, pb = s.ps.next()
                    i = 0
                    for tap in range(3):
                        for k in range(8):
                            c_ = col0 + tc * 128 + tap - 1
                            s.mm(pt[:, 0:512], s.hT[:, k, c_:c_ + 128], Wk[tap][:, k, :], i == 0, i == 23, [s.BhT, BWk], [pb])
                            i += 1
                    return pt, pb

                prep_part(0)
                for tc in range(nt):
                    pt, pb = proj3(tc)
                    s.tt("dve", z[:, tc, :], pt[:, 0:512], cbias[:, :], ALU.add, [pb, Bcb], [Bz])
                for n in range(2):
                    prep_part(n + 1)
                    o = RP_HSKIP + n * 1024 + cb2 * 512
                    s.bload(skipb[:, :], s.rowp[l:l + 1, o:o + 512], 512, Bsk)
                    with scope() as fl_:
                        ctl = Rot(fl_, s.sbt, "hy_ct2", [128, nt * 128], BF16, 2)
                        stl = Rot(fl_, s.sbt, "hy_st2", [128, nt * 128], BF16, 2)
                        for wt in range(nw):
                            ct, ctb = ctl.next()
                            st_, stb = stl.next()
                            s.dma(None, ct[:], dft["ct"][wt], [], [ctb])
                            s.dma(None, st_[:], dft["st"][wt], [], [stb])
                            kre, kreb = kr.next()
                            kim, kimb = ki.next()
                            kc = n * 1024 + cb2 * 512
                            s.dma(None, kre[:, :], s.hyK[0][wt * 128:wt * 128 + 128, kc:kc + 512], [s.BhyK], [kreb])
                            s.dma(None, kim[:, :], s.hyK[1][wt * 128:wt * 128 + 128, kc:kc + 512], [s.BhyK], [kimb])
                            pR, pRb = s.ps.next()
                            pS, pSb = s.ps.next()
                            for tc in range(nt):
                                s.mm(pR[:, 0:512], ct[:, tc * 128:tc * 128 + 128], z[:, tc, :], tc == 0, tc == nt - 1,
                                     [ctb, Bz], [pRb])
                            for tc in range(nt):
                                s.mm(pS[:, 0:512], st_[:, tc * 128:tc * 128 + 128], z[:, tc, :], tc == 0, tc == nt - 1,
                                     [stb, Bz], [pSb])
                            t = [r_.next() for r_ in tmp]
                            s.tt("dve", t[0][0][:, :], pR[:, 0:512], kre[:, :], ALU.mult, [pRb, kreb], [t[0][1]])
                            s.tt("dve", t[1][0][:, :], pS[:, 0:512], kim[:, :], ALU.mult, [pSb, kimb], [t[1][1]])
                            s.tt("dve", t[2][0][:, :], pR[:, 0:512], kim[:, :], ALU.mult, [pRb, kimb], [t[2][1]])
                            s.tt("dve", t[3][0][:, :], pS[:, 0:512], kre[:, :], ALU.mult, [pSb, kreb], [t[3][1]])
                            s.tt("pool", Yre[:, wt, :], t[0][0][:, :], t[1][0][:, :], ALU.add, [t[0][1], t[1][1]], [BY])
                            s.tt("pool", Yim[:, wt, :], t[2][0][:, :], t[3][0][:, :], ALU.subtract, [t[2][1], t[3][1]], [BY])
                    with scope() as il_:
                        cil = Rot(il_, s.sbt, "hy_ci2", [128, nw * 128], BF16, 2)
                        sil = Rot(il_, s.sbt, "hy_si2", [128, nw * 128], BF16, 2)
                        for tc in range(nt):
                            ci, cib = cil.next()
                            si, sib = sil.next()
                            s.dma(None, ci[:], dft["ci"][tc], [], [cib])
                            s.dma(None, si[:], dft["si"][tc], [], [sib])
                            pY, pYb = s.ps.next()
                            for wc in range(nw):
                                s.mm(pY[:, 0:512], ci[:, wc * 128:wc * 128 + 128], Yre[:, wc, :], wc == 0, False,
                                     [cib, BY], [pYb])
                            for wc in range(nw):
                                s.mm(pY[:, 0:512], si[:, wc * 128:wc * 128 + 128], Yim[:, wc, :], False, wc == nw - 1,
                                     [sib, BY], [pYb])
                            pX, pXb = proj3(tc)
                            x_, xb_ = xg.next()
                            s.tt("dve", x_[:, :], pX[:, 0:512], cbias[:, :], ALU.add, [pXb, Bcb], [xb_])
                            t0, t0b = tmp[0].next()
                            s.tt("pool", t0[:, :], z[:, tc, :], skipb[:, :], ALU.mult, [Bz, Bsk], [t0b])
                            s.tt("dve", t0[:, :], t0[:, :], pY[:, 0:512], ALU.add, [t0b, pYb], [t0b])
                            if n == 0:
                                s.tt("dve", z[:, tc, :], t0[:, :], x_[:, :], ALU.mult, [t0b, xb_], [Bz])
                            else:
                                y_, yb_ = yb16.next()
                                s.tt("dve", y_[:, :], t0[:, :], x_[:, :], ALU.mult, [t0b, xb_], [yb_])
                                po, pob = s.ps.next()
                                for j in range(4):
                                    s.tr(po[:, j * 128:j * 128 + 128], y_[:, j * 128:j * 128 + 128], [yb_], [pob])
                                yt_, ytb_ = yT.next()
                                s.act(yt_[:, :, :], po[:, 0:512].rearrange("p (j t) -> p j t", t=128), AF.Copy, [pob], [ytb_])
                                tk = tok0 + tc * 128
                                s.dma(None, s.ybT[2][cb2 * 512:cb2 * 512 + 512, tk:tk + 128].rearrange("(j p) t -> p j t", p=128),
                                      yt_[:, :, :], [ytb_], [s.BybT[2]])

    def tok_tiles(self, need_ctx):
        out = []
        if need_ctx:
            out.append((0, TC, 0, 1))
        for q in range(4):
            out.append((q + 1, 512, TC + 512 * q, 0))
        return out

    def resid_load(self, n, dt2, off, xr):
        s = self
        xt, xb = xr.next()
        rows = slice(dt2 * 128, dt2 * 128 + 128)
        s.dma(None, xt[:, 0:n], s.xres[rows, off:off + n], [s.BxresL[dt2]], [xb])
        return xt, xb

    def resid_finish(self, pO, pOb, n, dt2, off, gate_m, which, pre, ob):
        s = self
        xt, xb = pre
        rows = slice(dt2 * 128, dt2 * 128 + 128)
        xo, xob = ob.next()
        s.stt(xo[:, 0:n], pO[:, 0:n], s.modcol(gate_m, dt2, which), xt[:, 0:n], ALU.mult, ALU.add,
              [pOb, xb, s.Bmod], [xob])
        s.dma(None, s.xres[rows, off:off + n], xo[:, 0:n], [xob], [s.BxresL[dt2]])

    def stage_merge(self, l, need_ctx):
        s = self
        tiles = s.tok_tiles(need_ctx)
        with scope() as lst:
            mT = lst.enter_context(s.sbt("mg_mT", [128, 8, T], BF16))
            yT = lst.enter_context(s.sbt("mg_yT", [128, 8, T], BF16))
            BmT, ByT = newbuf(), newbuf()
            sg = Rot(lst, s.sbt, "mg_sg", [128, 512], F32, 2)
            tm = Rot(lst, s.sbt, "mg_tm", [128, 512], F32, 2)
            with scope() as l2:
                yP = l2.enter_context(s.sbt("mg_yP", [128, 8, TL], BF16))
                ByP = newbuf()
                for n in range(3):
                    for k in range(8):
                        rows = slice(k * 128, k * 128 + 128)
                        if n == 1:
                            s.dma(None, yT[:, k, 0:TC], s.ybT[n][rows, 0:TC], [s.BybT[n]], [ByT])
                            s.dma(None, yP[:, k, :], s.ybT[n][rows, TC:T], [s.BybT[n]], [ByP])
                            src = yP[:, k, :].rearrange("p (w r) -> p r w", r=32)
                            dst = yT[:, k, TC:T].rearrange("p (r w) -> p r w", w=64)
                            if k % 2 == 0:
                                s.act(dst, src, AF.Copy, [ByP], [ByT])
                            else:
                                s.P.op("dve", (lambda o, i_: (lambda e: e.tensor_copy(out=o, in_=i_)))(dst, src), [ByP], [ByT])
                        else:
                            s.dma(None, yT[:, k, :], s.ybT[n][rows, :], [s.BybT[n]], [ByT])
                    for dt in range(8):
                        wb_t, wbb = s.load_w(s.w_branch[l][n][:, dt * 128:dt * 128 + 128], 128)
                        c0 = G0 + n * 1024 + dt * 128
                        wg_t, wgb = s.load_w(s.w_in[l][:, c0:c0 + 128], 128)
                        for (ti, nt_, off, which) in tiles:
                            pP, pPb = s.ps.next()
                            for k in range(8):
                                s.mm(pP[:, 0:nt_], wb_t[:, k, 0:128], yT[:, k, off:off + nt_], k == 0, k == 7, [wbb, ByT], [pPb])
                            pG, pGb = s.ps.next()
                            for k in range(8):
                                s.mm(pG[:, 0:nt_], wg_t[:, k, 0:128], s.htile(k, ti)[0], k == 0, k == 7, [wgb, s.BhT], [pGb])
                            g, gb = sg.next()
                            s.act(g[:, 0:nt_], pG[:, 0:nt_], AF.Sigmoid, [pGb], [gb])
                            if n == 0:
                                s.tt("dve", mT[:, dt, off:off + nt_], pP[:, 0:nt_], g[:, 0:nt_], ALU.mult, [pPb, gb], [BmT])
                            else:
                                t, tb = tm.next()
                                s.tt("dve", t[:, 0:nt_], pP[:, 0:nt_], g[:, 0:nt_], ALU.mult, [pPb, gb], [tb])
                                s.tt("dve", mT[:, dt, off:off + nt_], mT[:, dt, off:off + nt_], t[:, 0:nt_], ALU.add,
                                     [tb, BmT], [BmT])
            if s.debug.get("merge_dump"):
                s.dump("mT", mT[:], [128, 8, T], BmT, BF16)
            wo = lst.enter_context(s.sbt("mg_wo", [128, 8, D], BF16))
            Bwo = newbuf()
            s.dma("pool", wo[:], s.w_out[l].rearrange("(k p) n -> p k n", p=128), [], [Bwo])
            xr = Rot(lst, s.sbt, "mg_xr", [128, 512], F32, 3)
            ob = Rot(lst, s.sbt, "mg_ob", [128, 512], F32, 2)
            items = [(ti, nt_, off, which, dt2) for (ti, nt_, off, which) in tiles for dt2 in range(8)]
            pre = s.resid_load(items[0][1], items[0][4], items[0][2], xr)
            for i, (ti, nt_, off, which, dt2) in enumerate(items):
                pO, pOb = s.ps.next()
                for k in range(8):
                    s.mm(pO[:, 0:nt_], wo[:, k, dt2 * 128:dt2 * 128 + 128], mT[:, k, off:off + nt_], k == 0, k == 7,
                         [Bwo, BmT], [pOb])
                nxt = s.resid_load(items[i + 1][1], items[i + 1][4], items[i + 1][2], xr) if i + 1 < len(items) else None
                s.resid_finish(pO, pOb, nt_, dt2, off, 2, which, pre, ob)
                pre = nxt

    def stage_mlp(self, l, need_ctx):
        s = self
        tiles = s.tok_tiles(need_ctx)
        with scope() as lst:
            hid = lst.enter_context(s.sbt("ml_hid", [128, 32, 512], BF16))
            Bhid = newbuf()
            rl = Rot(lst, s.sbt, "ml_rl", [128, 512], BF16, 2)
            xr = Rot(lst, s.sbt, "ml_xr", [128, 512], F32, 3)
            ob = Rot(lst, s.sbt, "ml_ob", [128, 512], F32, 2)
            for (ti, nt_, off, which) in tiles:
                for fb in range(8):
                    w1t, w1b = s.load_w(s.mlp_w1[l][:, fb * 512:fb * 512 + 512], 512)
                    for fj in range(4):
                        f = fb * 4 + fj
                        pH, pHb = s.ps.next()
                        for k in range(8):
                            s.mm(pH[:, 0:nt_], w1t[:, k, fj * 128:fj * 128 + 128], s.htile(k, ti)[0], k == 0, k == 7,
                                 [w1b, s.BhT], [pHb])
                        r, rb = rl.next()
                        s.act(r[:, 0:nt_], pH[:, 0:nt_], AF.Relu, [pHb], [rb])
                        s.tt("dve", hid[:, f, 0:nt_], r[:, 0:nt_], r[:, 0:nt_], ALU.mult, [rb], [Bhid])
                pre = s.resid_load(nt_, 0, off, xr)
                for dt2 in range(8):
                    w2t, w2b = s.wrot.next()
                    w2v = w2t[:, :, :].rearrange("p k (a n) -> p (k a) n", n=128)
                    s.dma("pool", w2v, s.mlp_w2[l][:, dt2 * 128:dt2 * 128 + 128].rearrange("(f p) n -> p f n", p=128),
                          [], [w2b])
                    pO, pOb = s.ps.next()
                    for f in range(32):
                        s.mm(pO[:, 0:nt_], w2v[:, f, :], hid[:, f, 0:nt_], f == 0, f == 31, [w2b, Bhid], [pOb])
                    nxt = s.resid_load(nt_, dt2 + 1, off, xr) if dt2 + 1 < 8 else None
                    s.resid_finish(pO, pOb, nt_, dt2, off, 5, which, pre, ob)
                    pre = nxt

    def stage_final(self):
        s = self
        with scope() as lst:
            xt = Rot(lst, s.sbt, "fx", [128, 8, 512], F32, 2)
            sq = Rot(lst, s.sbt, "fsq", [128, 8, 512], F32, 1)
            rs = Rot(lst, s.sbt, "frs", [128, 512], F32, 2)
            ot = Rot(lst, s.sbt, "fo", [128, 8, 512], F32, 2)
            for q in range(4):
                t0 = TC + 512 * q
                x, xb = xt.next()
                s.dma(None, x[:, :, :], s.xres[:, t0:t0 + 512].rearrange("(k p) t -> p k t", p=128), list(s.BxresL), [xb])
                q_, qb = sq.next()
                s.act(q_[:, :, :], x[:, :, :], AF.Square, [xb], [qb])
                pt, pb = s.ps.next()
                for k in range(8):
                    s.mm(pt[:, 0:512], s.ones(), q_[:, k, :], k == 0, k == 7, [qb, s.Bconst], [pb])
                r, rb = rs.next()
                s.act(r[:, :], pt[:, 0:512], AF.Sqrt, [pb], [rb], scale=1.0 / D, bias=EPS)
                s.recip(r[:, :], r[:, :], [rb], [rb])
                o, ob_ = ot.next()
                for k in range(8):
                    s.stt(o[:, k, :], x[:, k, :], s.pp[:, PP_NF + k:PP_NF + k + 1], r[:, :], ALU.mult, ALU.mult,
                          [xb, rb, s.Bconst], [ob_])
                s.dma(None, s.out[:, 512 * q:512 * q + 512].rearrange("(k p) t -> p k t", p=128), o[:, :, :], [ob_], [Buf()])


def _bf16(a):
    import ml_dtypes
    return np.ascontiguousarray(a.astype(np.float32)).astype(ml_dtypes.bfloat16)


def _consts():
    k = np.arange(128)
    cst = np.zeros((128, CS_N), np.float32)
    cst[:, CS_TRIU:CS_TRIU + 128] = (k[:, None] <= k[None, :])
    cst[:, CS_TRIL:CS_TRIL + 128] = (k[:, None] >= k[None, :])
    cst[:, CS_MASKF:CS_MASKF + 128] = np.where(k[:, None] <= k[None, :], 0.0, NEG)
    cst[:, CS_MASKB:CS_MASKB + 128] = np.where(k[:, None] >= k[None, :], 0.0, NEG)
    cst[:, CS_IDENT:CS_IDENT + 128] = np.eye(128)
    cst[:, CS_ONES:CS_ONES + 128] = 1.0
    sel = np.zeros((16, 16, 128), np.float32)
    for h in range(16):
        sel[h, h, :] = 1.0
    out = {"cst": cst, "sel": sel.reshape(16, 16 * 128)}
    tneg = np.zeros((128, 18), np.float32)
    for j in range(16):
        tneg[:, j] = -(j * 128 + k) / float(TL)
    for j in range(2):
        tneg[:, 16 + j] = -(j * 128 + k) / float(TC)
    out["tneg"] = tneg
    for L, nm in ((TL, "L"), (TC, "C")):
        t = np.arange(L, dtype=np.float32)
        t_norm = t / np.float32(L)
        bands = np.linspace(1e-4, 15.0, 16, dtype=np.float32)
        ang = (np.float32(2.0 * math.pi / L) * t[:, None] * bands[None, :]).astype(np.float32)
        feats = np.concatenate([t_norm[:, None], np.cos(ang), -np.sin(ang)], axis=-1).astype(np.float32)
        out["feats" + nm] = np.ascontiguousarray(feats.T)
        wp = _wp(L)
        nw, nt = wp // 128, L // 128
        sidx = np.arange(L, dtype=np.float64)
        widx = np.arange(wp, dtype=np.float64)
        ph = np.pi * np.outer(sidx, widx) / L
        valid = (widx <= L)[None, :]
        ctm = np.where(valid, np.cos(ph), 0.0)
        stm = np.where(valid, np.sin(ph), 0.0)
        cw = np.where((widx == 0) | (widx == L), 1.0, 2.0) * (widx <= L) / (2.0 * L)
        cim = (cw[:, None] * np.cos(ph.T))
        sim = -(cw[:, None] * np.sin(ph.T))
        def tile_fwd(m):
            return _bf16(m.reshape(nt, 128, nw, 128).transpose(2, 1, 0, 3).reshape(nw, 128, nt * 128))

        def tile_inv(m):
            return _bf16(m.reshape(nw, 128, nt, 128).transpose(2, 1, 0, 3).reshape(nt, 128, nw * 128))
        out["ct" + nm] = tile_fwd(ctm)
        out["st" + nm] = tile_fwd(stm)
        out["ci" + nm] = tile_inv(cim)
        out["si" + nm] = tile_inv(sim)
    return out


def _fm(v, n):
    return np.ascontiguousarray(np.asarray(v, np.float32).reshape(n, 128).T)


def _prep_shared(inp):
    f = lambda n: np.asarray(inp[n], np.float32)
    pp = np.zeros((128, PP_N), np.float32)
    for l in range(DEPTH):
        pp[:, PP_N1W + 8 * l:PP_N1W + 8 * l + 8] = _fm(f("norm1_w")[l], 8)
        pp[:, PP_N2W + 8 * l:PP_N2W + 8 * l + 8] = _fm(f("norm2_w")[l], 8)
        pp[:, PP_MODB + 48 * l:PP_MODB + 48 * l + 48] = _fm(f("mod_b")[l], 48)
        pp[:, PP_SCB + 12 * l:PP_SCB + 12 * l + 12] = _fm(f("ssd_conv_b")[l], 12)
        cw = f("ssd_conv_w")[l]
        pp[:, PP_SCW + 60 * l:PP_SCW + 60 * l + 60] = cw.reshape(5, 12, 128).transpose(2, 1, 0).reshape(128, 60)
        pp[:, PP_MCB + 16 * l:PP_MCB + 16 * l + 16] = _fm(f("ml_conv_b")[l], 16)
        cw = f("ml_conv_w")[l]
        pp[:, PP_MCW + 80 * l:PP_MCW + 80 * l + 80] = cw.reshape(5, 16, 128).transpose(2, 1, 0).reshape(128, 80)
        pp[:, PP_SNW + 8 * l:PP_SNW + 8 * l + 8] = _fm(f("ssd_norm_w")[l], 8)
        pp[:, PP_MNW + 8 * l:PP_MNW + 8 * l + 8] = _fm(f("ml_norm_w")[l], 8)
        pp[0:64, PP_HB1 + l] = f("hy_ffn_b1")[l]
        pp[0:64, PP_HB2 + l] = f("hy_ffn_b2")[l]
    pp[:, PP_NF:PP_NF + 8] = _fm(f("norm_f_w"), 8)
    rowp = np.zeros((DEPTH, RP_N), np.float32)
    for l in range(DEPTH):
        rowp[l, RP_DTB:RP_DTB + 32] = f("ssd_dt_bias")[l].reshape(-1)
        rowp[l, RP_ALOG:RP_ALOG + 32] = f("ssd_a_log")[l].reshape(-1)
        rowp[l, RP_DSK:RP_DSK + 1024] = np.repeat(f("ssd_d")[l], 64)
        rowp[l, RP_MGB:RP_MGB + 32] = f("ml_gate_b")[l].reshape(-1)
        rowp[l, RP_HCB:RP_HCB + 3072] = f("hy_conv_b")[l]
        rowp[l, RP_HCW:RP_HCW + 9216] = f("hy_conv_w")[l].reshape(-1)
        rowp[l, RP_HDEC:RP_HDEC + 4096] = f("hy_decay")[l].reshape(-1)
        rowp[l, RP_HSKIP:RP_HSKIP + 2048] = f("hy_skip")[l].reshape(-1)
    sh = {"mod_w": f("mod_w"), "w_in": f("w_in"), "w_branch": f("w_branch"), "w_out": f("w_out"),
          "mlp_w1": f("mlp_w1"), "mlp_w2": f("mlp_w2"), "pp": pp, "rowp": rowp,
          "hyw1": f("hy_ffn_w1"), "hyw2": f("hy_ffn_w2"), "hyw3": f("hy_ffn_w3")}
    sh.update(_consts())
    return sh


def _prep_core(inp, b):
    x = np.asarray(inp["x"], np.float32)[b]
    ctx = np.asarray(inp["ctx"], np.float32)[b]
    xT = np.ascontiguousarray(np.concatenate([ctx.T, x.T], axis=1))
    cvec = np.zeros((128, 8, 2), np.float32)
    cvec[:, :, 0] = _fm(np.asarray(inp["c"], np.float32)[b], 8)
    cvec[:, :, 1] = _fm(np.asarray(inp["c_ctx"], np.float32), 8)
    return {"xT": xT, "cvec": cvec.reshape(128, 16)}


_CACHE = {}


def kernel(**inputs):
    if "nc" not in _CACHE:
        _CACHE["nc"] = Builder().build()
    nc = _CACHE["nc"]
    shared = _prep_shared(inputs)
    in_maps = []
    for b in range(8):
        m = dict(shared)
        m.update(_prep_core(inputs, b))
        in_maps.append(m)
    res = run_bass_kernel_spmd(nc, in_maps, core_ids=list(range(8)))
    out = np.stack([np.ascontiguousarray(res.results[b]["out"].T) for b in range(8)], axis=0)
    return out.astype(np.float32)
```

```python
import contextlib
import math
import numpy as np
import concourse.bass as bass
import concourse.mybir as mybir
from concourse.bass_utils import run_bass_kernel_spmd

F32 = mybir.dt.float32
BF16 = mybir.dt.bfloat16
AF = mybir.ActivationFunctionType
ALU = mybir.AluOpType

D = 1024
TC = 256
TL = 2048
T = TC + TL
NCH = T // 128
DEPTH = 2
EPS = 1e-6
CT0 = 1
LT0 = 259
TP = 2308
SSD_COLS = 2592
ML0 = 2592
REC_COLS = 6720
HY0 = 6720
G0 = 9792
IN_COLS = 12864
NEG = -30000.0

EPOCH = 30000
SYNC_SAME = ("act", "dve", "pool")


class Buf:
    __slots__ = ("name", "w", "r", "excl")

    def __init__(self, name="", excl=False):
        self.name = name
        self.w = None
        self.r = []
        self.excl = excl


class Prog:
    ENG = ("pe", "act", "dve", "pool", "sp")

    def __init__(self, nc, st, n_dma_sems=56):
        self.nc = nc
        self.st = st
        self.streams = {e: [] for e in self.ENG}
        self.cnt = {e: 0 for e in self.ENG}
        self.cur_sem = {}
        self.seen = {e: {} for e in self.ENG}
        for e in ("pe", "act", "dve", "pool"):
            self.cur_sem[e] = self._new_sem()
        self.dma_sems = [self._new_sem() for _ in range(n_dma_sems)]
        self.dma_val = [0] * n_dma_sems
        self.dma_rr = 0
        self.n_sw = 12
        self.dma_rr_sw = 0
        self.qrr = 0
        self.nops = 0

    def _new_sem(self):
        self.nsem = getattr(self, "nsem", 0) + 1
        return self.st.enter_context(self.nc.semaphore("sem%d" % self.nsem))

    def _need(self, eng, tok, waits, raw=True):
        if tok is None:
            return
        sem, val, src = tok
        if src == eng and (eng not in SYNC_SAME or not raw):
            return
        k = id(sem)
        if self.seen[eng].get(k, 0) >= val:
            return
        self.seen[eng][k] = val
        waits.append((sem, val))

    def _deps(self, eng, reads, writes):
        waits = []
        for b in reads:
            self._need(eng, b.w, waits)
        for b in writes:
            self._need(eng, b.w, waits)
            for t in b.r:
                self._need(eng, t, waits)
        return waits

    def _commit(self, tok, reads, writes):
        for b in reads:
            b.r.append(tok)
            if len(b.r) > 48:
                last = {}
                for t in b.r:
                    last[(id(t[0]))] = t if (id(t[0]) not in last or last[id(t[0])][1] < t[1]) else last[id(t[0])]
                b.r = list(last.values())
        for b in writes:
            b.w = tok
            b.r = []

    def op(self, eng, fn, reads=(), writes=()):
        writes = list(writes) + [b for b in reads if b.excl and b not in writes]
        reads = [b for b in reads if not b.excl]
        waits = self._deps(eng, reads, writes)
        if self.cnt[eng] >= EPOCH:
            self.cur_sem[eng] = self._new_sem()
            self.cnt[eng] = 0
        self.cnt[eng] += 1
        sem = self.cur_sem[eng]
        tok = (sem, self.cnt[eng], eng)
        self.streams[eng].append((waits, fn, (sem, 1)))
        self._commit(tok, reads, writes)
        self.nops += 1
        return tok

    def dma(self, q, fn, reads=(), writes=()):
        if q is None:
            q = "sp"
        waits = self._deps(q, reads, writes)
        if q == "pool":
            i = self.dma_rr_sw
            self.dma_rr_sw = (self.dma_rr_sw + 1) % self.n_sw
        else:
            i = self.n_sw + self.dma_rr
            self.dma_rr = (self.dma_rr + 1) % (len(self.dma_sems) - self.n_sw)
        sem = self.dma_sems[i]
        if self.dma_val[i] > 0:
            self._need(q, (sem, self.dma_val[i], "dma"), waits)
        self.dma_val[i] += 16
        tok = (sem, self.dma_val[i], "dma")
        self.streams[q].append((waits, fn, (sem, 16)))
        self._commit(tok, reads, writes)
        self.nops += 1
        return tok

    def finish(self, eng="sp"):
        waits = []
        for e in ("pe", "act", "dve", "pool"):
            if self.cnt[e] > 0:
                self._need(eng, (self.cur_sem[e], self.cnt[e], e), waits)
        for i, s in enumerate(self.dma_sems):
            if self.dma_val[i] > 0:
                self._need(eng, (s, self.dma_val[i], "dma"), waits)
        self.streams[eng].append((waits, None, None))

    def emit(self):
        nc = self.nc
        engmap = {"pe": "tensor", "act": "scalar", "dve": "vector", "pool": "gpsimd", "sp": "sync"}
        with nc.Block() as block:
            for e in self.ENG:
                stream = self.streams[e]
                if not stream:
                    continue

                def body(eng, stream=stream):
                    for waits, fn, inc in stream:
                        for (s, v) in waits:
                            eng.wait_ge(s, v)
                        if fn is not None:
                            ins = fn(eng)
                            if inc is not None:
                                ins.then_inc(inc[0], inc[1])
                getattr(block, engmap[e])(body)


_FREED = {}
_SCOPES = []


def newbuf(name="", excl=False):
    b = Buf(name, excl)
    b.r = list(_FREED.values())
    if _SCOPES:
        _SCOPES[-1].append(b)
    return b


@contextlib.contextmanager
def scope():
    bufs = []
    _SCOPES.append(bufs)
    with contextlib.ExitStack() as lst:
        yield lst
    _SCOPES.pop()
    for b in bufs:
        for t in ([b.w] if b.w is not None else []) + list(b.r):
            k = id(t[0])
            if k not in _FREED or _FREED[k][1] < t[1]:
                _FREED[k] = t


class Rot:
    def __init__(self, st, alloc, name, shape, dtype, n, excl=False):
        self.tiles = [st.enter_context(alloc("%s%d" % (name, i), shape, dtype)) for i in range(n)]
        self.bufs = [newbuf("%s%d" % (name, i), excl) for i in range(n)]
        self.i = 0

    def next(self):
        i = self.i
        self.i = (self.i + 1) % len(self.tiles)
        return self.tiles[i], self.bufs[i]


PP_N1W, PP_N2W, PP_NF, PP_MODB, PP_SCB, PP_SCW, PP_MCB, PP_MCW, PP_SNW, PP_MNW, PP_HB1, PP_HB2, PP_N = (
    0, 16, 32, 40, 136, 160, 280, 312, 472, 488, 504, 506, 512)
RP_DTB, RP_ALOG, RP_DSK, RP_MGB, RP_HCB, RP_HCW, RP_HDEC, RP_HSKIP, RP_N = (
    0, 32, 64, 1088, 1120, 4192, 13408, 17504, 19552)
CS_TRIU, CS_TRIL, CS_MASKF, CS_MASKB, CS_IDENT, CS_ONES, CS_N = 0, 128, 256, 384, 512, 640, 768


def _wp(L):
    return ((L + 1 + 127) // 128) * 128


class Builder:
    def __init__(self, debug=None, nlayers=DEPTH):
        self.debug = debug or {}
        self.nlayers = nlayers
        self.nc = bass.Bass("TRN2", target_bir_lowering=False)
        self.dbg_out = {}
        _FREED.clear()
        del _SCOPES[:]

    def din(self, name, shape, dt=F32):
        return self.nc.dram_tensor(name, list(shape), dt, kind="ExternalInput").ap()

    def dscr(self, name, shape, dt=F32):
        return self.nc.dram_tensor(name, list(shape), dt, kind="Internal").ap()

    def dout(self, name, shape, dt=F32):
        return self.nc.dram_tensor(name, list(shape), dt, kind="ExternalOutput").ap()

    def mm(self, out, lhsT, rhs, start, stop, reads, writes, skip=False):
        self.P.op("pe", lambda e: e.matmul(out, lhsT=lhsT, rhs=rhs, start=start, stop=stop, skip_group_check=skip),
                  reads, writes)

    def tr(self, out, in_, reads, writes):
        ident = self.identb[:]
        self.P.op("pe", lambda e: e.matmul(out, lhsT=in_, rhs=ident, start=True, stop=True), reads + [self.Bconst], writes)

    def act(self, out, in_, func, reads, writes, **kw):
        self.P.op("act", lambda e: e.activation(out=out, in_=in_, func=func, **kw), reads, writes)

    def tt(self, eng, out, in0, in1, op, reads, writes):
        self.P.op(eng, lambda e: e.tensor_tensor(out=out, in0=in0, in1=in1, op=op), reads, writes)

    def ts(self, eng, out, in0, s1, s2, op0, op1, reads, writes):
        if op1 is None:
            self.P.op(eng, lambda e: e.tensor_scalar(out=out, in0=in0, scalar1=s1, scalar2=None, op0=op0), reads, writes)
        else:
            self.P.op(eng, lambda e: e.tensor_scalar(out=out, in0=in0, scalar1=s1, scalar2=s2, op0=op0, op1=op1),
                      reads, writes)

    def stt(self, out, in0, scalar, in1, op0, op1, reads, writes):
        self.P.op("dve", lambda e: e.scalar_tensor_tensor(out=out, in0=in0, scalar=scalar, in1=in1, op0=op0, op1=op1),
                  reads, writes)

    def cp(self, eng, out, in_, reads, writes):
        self.P.op(eng, lambda e: e.tensor_copy(out=out, in_=in_), reads, writes)

    def recip(self, out, in_, reads, writes):
        self.P.op("dve", lambda e: e.reciprocal(out=out, in_=in_), reads, writes)

    def memset(self, eng, ap, val, writes):
        self.P.op(eng, lambda e: e.memset(ap, val), [], writes)

    def dma(self, q, out, in_, reads, writes):
        self.P.dma(q, lambda e: e.dma_start(out=out, in_=in_), reads, writes)

    def sbt(self, name, shape, dt):
        self.uid = getattr(self, "uid", 0) + 1
        return self.nc.sbuf_tensor("%s_%d" % (name, self.uid), list(shape), dt)

    def sb(self, name, shape, dt):
        return self.st.enter_context(self.sbt("sb_" + name, list(shape), dt))

    def dump(self, name, ap_sb, shape, buf, dt=F32):
        o = self.dout("dbg_" + name, shape, dt)
        self.dbg_out[name] = ("dbg_" + name, tuple(shape))
        self.dma("sp", o, ap_sb, [buf], [Buf()])

    def hcols(self, k, c, perm=False):
        if c < 2:
            return self.hT[:, k, CT0 + 128 * c: CT0 + 128 * c + 128]
        cl = c - 2
        if not perm:
            return self.hT[:, k, LT0 + 128 * cl: LT0 + 128 * cl + 128]
        return self.hTp[:, k, 128 * cl:128 * cl + 128]

    def htile(self, k, ti, perm=False):
        if ti == 0:
            return self.hT[:, k, CT0:CT0 + TC], TC
        q = ti - 1
        if not perm:
            return self.hT[:, k, LT0 + 512 * q: LT0 + 512 * q + 512], 512
        return self.hTp[:, k, 512 * q:512 * q + 512], 512

    def load_w(self, src, n, q="pool"):
        t, b = self.wrot.next()
        self.dma(q, t[:, :, 0:n], src.rearrange("(k p) n -> p k n", p=128), [], [b])
        return t, b

    def build(self):
        nc = self.nc
        with contextlib.ExitStack() as st:
            self.st = st
            self.P = Prog(nc, st)
            self.declare()
            self.setup()
            stop = self.debug.get("stop")
            for l in range(self.nlayers):
                need_ctx = l < DEPTH - 1
                self.stage_mod(l)
                self.stage_norm(l, first=True)
                if stop == "norm%d" % l:
                    self.dump("hT", self.hT[:], [128, 8, TP], self.BhT, BF16)
                    break
                if not self.debug.get("skip_scan"):
                    self.stage_scan(l, "ssd", need_ctx)
                    if stop == "ssd%d" % l:
                        break
                    self.stage_scan(l, "ml", need_ctx)
                    if stop == "ml%d" % l:
                        break
                self.stage_hyena(l, need_ctx)
                if stop == "hy%d" % l:
                    break
                self.stage_merge(l, need_ctx)
                if stop == "merge%d" % l:
                    break
                self.stage_norm(l, first=False, need_ctx=need_ctx)
                self.stage_mlp(l, need_ctx)
                if stop == "mlp%d" % l:
                    break
            else:
                self.stage_final()
            dmp = self.debug.get("dump")
            if dmp:
                srcs = {"ybT0": (self.ybT[0], [D, T], BF16, self.BybT[0]), "ybT1": (self.ybT[1], [D, T], BF16, self.BybT[1]),
                        "ybT2": (self.ybT[2], [D, T], BF16, self.BybT[2]), "xres": (self.xres, [D, T], F32, self.BxresL),
                        "hyK": (self.hyK, [2, _wp(TL), 2048], F32, self.BhyK),
                        "ys_tok": (self.ys_tok, [T, D], F32, self.Bys)}
                src, shp, dt, bf = srcs[dmp]
                o = self.dout("dbg_" + dmp, shp, dt)
                if len(shp) == 3:
                    o = o.rearrange("a r c -> (a r) c")
                    src = src.rearrange("a r c -> (a r) c")
                for r0 in range(0, o.shape[0], 128):
                    self.dma("sp", o[r0:r0 + 128, :], src[r0:r0 + 128, :], bf if isinstance(bf, list) else [bf], [Buf()])
            self.P.finish("sp")
            self.P.emit()
        return nc

    def declare(self):
        s = self
        s.xT = s.din("xT", [D, T])
        s.cvec = s.din("cvec", [128, 16])
        s.mod_w = s.din("mod_w", [DEPTH, D, 6 * D])
        s.w_in = s.din("w_in", [DEPTH, D, IN_COLS])
        s.w_branch = s.din("w_branch", [DEPTH, 3, D, D])
        s.w_out = s.din("w_out", [DEPTH, D, D])
        s.mlp_w1 = s.din("mlp_w1", [DEPTH, D, 4 * D])
        s.mlp_w2 = s.din("mlp_w2", [DEPTH, 4 * D, D])
        s.pp_d = s.din("pp", [128, PP_N])
        s.rowp = s.din("rowp", [DEPTH, RP_N])
        s.hyw1 = s.din("hyw1", [DEPTH, 33, 64])
        s.hyw2 = s.din("hyw2", [DEPTH, 64, 64])
        s.hyw3 = s.din("hyw3", [DEPTH, 64, 4096])
        s.cst_d = s.din("cst", [128, CS_N])
        s.sel_d = s.din("sel", [16, 16 * 128])
        s.feats = {TL: s.din("featsL", [33, TL]), TC: s.din("featsC", [33, TC])}
        s.tneg_d = s.din("tneg", [128, 18])
        s.dft = {}
        for L, nm in ((TL, "L"), (TC, "C")):
            nw = _wp(L) // 128
            nt = L // 128
            s.dft[L] = dict(
                ct=s.din("ct" + nm, [nw, 128, nt * 128], BF16), st=s.din("st" + nm, [nw, 128, nt * 128], BF16),
                ci=s.din("ci" + nm, [nt, 128, nw * 128], BF16), si=s.din("si" + nm, [nt, 128, nw * 128], BF16))
        s.out = s.dout("out", [D, TL])
        s.xres = s.dscr("xres", [D, T])
        s.ys_tok = s.dscr("ys_tok", [T, D])
        s.ybT = [s.dscr("ybT%d" % n, [D, T], BF16) for n in range(3)]
        s.hsd = s.dscr("hsd", [2, TL, 2048], BF16)
        s.hyK = s.dscr("hyK", [2, _wp(TL), 2048])
        s.BxresL = [Buf("xres%d" % k) for k in range(8)]
        s.Bys = Buf("ys_tok")
        s.BybT = [Buf("ybT%d" % n) for n in range(3)]
        s.Bhsd = Buf("hsd")
        s.BhyK = Buf("hyK")

    def setup(self):
        s = self
        nc, st = s.nc, s.st
        s.ps = Rot(st, nc.psum_tensor, "ps", [128, 512], F32, 4, excl=True)
        s.pacc = Rot(st, nc.psum_tensor, "pacc", [128, 512], F32, 4, excl=True)
        s.Bconst = Buf("const")
        s.cst = s.sb("cst", [128, CS_N], F32)
        s.sel = s.sb("sel", [16, 16 * 128], F32)
        s.pp = s.sb("pp", [128, PP_N], F32)
        s.identb = s.sb("identb", [128, 128], BF16)
        s.tneg = s.sb("tneg", [128, 18], F32)
        s.csil = s.sb("csil", [128, 16], F32)
        s.mod = s.sb("mod", [128, 96], F32)
        s.gv = s.sb("gv", [128, 64], F32)
        s.Bmod = Buf("mod")
        s.hT = s.sb("hT", [128, 8, TP], BF16)
        s.BhT = Buf("hT")
        s.wrot = Rot(st, nc.sbuf_tensor, "wr", [128, 8, 512], BF16, 3)
        c = [s.Bconst]
        s.dma("sp", s.cst[:], s.cst_d, [], c)
        s.dma("sp", s.sel[:], s.sel_d, [], c)
        s.dma("sp", s.pp[:], s.pp_d, [], c)
        s.dma("sp", s.tneg[:], s.tneg_d, [], c)
        s.dma("sp", s.csil[:], s.cvec, [], c)
        s.dma("pool", s.identb[:], s.cst_d[:, CS_IDENT:CS_IDENT + 128], [], c)
        s.maskb = s.sb("maskb", [128, 256], BF16)
        s.dma("pool", s.maskb[:], s.cst_d[:, CS_MASKF:CS_MASKF + 256], [], c)
        s.act(s.csil[:], s.csil[:], AF.Silu, c, c)
        s.memset("dve", s.hT[:], 0.0, [s.BhT])
        for k in range(8):
            s.dma(None, s.xres[k * 128:(k + 1) * 128, :], s.xT[k * 128:(k + 1) * 128, :], [], [s.BxresL[k]])

    def triu(self):
        return self.cst[:, CS_TRIU:CS_TRIU + 128]

    def tril(self):
        return self.cst[:, CS_TRIL:CS_TRIL + 128]

    def ones(self):
        return self.cst[:, CS_ONES:CS_ONES + 128]

    def identf(self):
        return self.cst[:, CS_IDENT:CS_IDENT + 128]

    def stage_mod(self, l):
        s = self
        with scope() as lst:
            wm = [lst.enter_context(s.sbt("wm%d" % i, [128, 8, 512], F32)) for i in range(2)]
            wmb = [newbuf(), newbuf()]
            pt, pb = s.ps.next()
            src = s.mod_w[l].rearrange("(k p) n -> p k n", p=128)
            csv = s.csil[:, :].rearrange("p (k w) -> p k w", w=2)
            for blk in range(12):
                t, b = wm[blk % 2], wmb[blk % 2]
                s.dma(None, t[:], src[:, :, blk * 512:(blk + 1) * 512], [], [b])
                for jt in range(4):
                    j = blk * 4 + jt
                    for k in range(8):
                        s.mm(pt[:, 2 * j:2 * j + 2], t[:, k, jt * 128:(jt + 1) * 128], csv[:, k, :],
                             k == 0, k == 7, [b, s.Bconst], [pb])
            modv = s.mod[:, :].rearrange("p (j w) -> p j w", w=2)
            mb = s.pp[:, PP_MODB + l * 48:PP_MODB + l * 48 + 48].unsqueeze(2).broadcast_to([128, 48, 2])
            s.tt("dve", modv, pt[:, 0:96].rearrange("p (j w) -> p j w", w=2), mb, ALU.add, [pb, s.Bconst], [s.Bmod])
            for i, (m, npp) in enumerate(((1, PP_N1W), (4, PP_N2W))):
                gsl = s.gv[:, 16 * i:16 * i + 16].rearrange("p (j w) -> p j w", w=2)
                msl = s.mod[:, m * 16:(m + 1) * 16].rearrange("p (j w) -> p j w", w=2)
                nw = s.pp[:, npp + l * 8:npp + l * 8 + 8].unsqueeze(2).broadcast_to([128, 8, 2])
                s.stt(gsl, msl, 1.0, nw, ALU.add, ALU.mult, [s.Bmod, s.Bconst], [s.Bmod])

    def modcol(self, m, j, which):
        c = (m * 8 + j) * 2 + which
        return self.mod[:, c:c + 1]

    def gcol(self, i, j, which):
        c = 16 * i + j * 2 + which
        return self.gv[:, c:c + 1]

    def stage_norm(self, l, first, need_ctx=True):
        s = self
        gi = 0 if first else 1
        mshift = 0 if first else 3
        with scope() as lst:
            xt = Rot(lst, s.sbt, "nx", [128, 8, 512], F32, 2)
            sq = Rot(lst, s.sbt, "nsq", [128, 8, 512], F32, 1)
            rs = Rot(lst, s.sbt, "nrs", [128, 512], F32, 2)
            tm = Rot(lst, s.sbt, "ntm", [128, 512], F32, 2)
            for ti in range(5):
                if ti == 0 and not need_ctx:
                    continue
                n = TC if ti == 0 else 512
                t0 = 0 if ti == 0 else TC + 512 * (ti - 1)
                which = 1 if ti == 0 else 0
                x, xb = xt.next()
                s.dma(None, x[:, :, 0:n], s.xres[:, t0:t0 + n].rearrange("(k p) t -> p k t", p=128), list(s.BxresL), [xb])
                q, qb = sq.next()
                s.act(q[:, :, 0:n], x[:, :, 0:n], AF.Square, [xb], [qb])
                pt, pb = s.ps.next()
                for k in range(8):
                    s.mm(pt[:, 0:n], s.ones(), q[:, k, 0:n], k == 0, k == 7, [qb, s.Bconst], [pb])
                r, rb = rs.next()
                s.act(r[:, 0:n], pt[:, 0:n], AF.Sqrt, [pb], [rb], scale=1.0 / D, bias=EPS)
                s.recip(r[:, 0:n], r[:, 0:n], [rb], [rb])
                for k in range(8):
                    t, tb = tm.next()
                    s.stt(t[:, 0:n], x[:, k, 0:n], s.gcol(gi, k, which), r[:, 0:n], ALU.mult, ALU.mult,
                          [xb, rb, s.Bmod], [tb])
                    dst, _ = s.htile(k, ti)
                    s.act(dst, t[:, 0:n], AF.Identity, [tb, s.Bmod], [s.BhT], bias=s.modcol(mshift, k, which))

    def stage_gates_prep(self, l):
        pass

    def conv_tile(self, l, col0, cwcol, cbcol, perm, cin, Bcin, acc, Bacc, dst, Bdst):
        s = self
        w, wb = s.load_w(s.w_in[l][:, col0:col0 + 128], 128)
        for ti in range(5):
            pt, pb = s.ps.next()
            n = TC if ti == 0 else 512
            for k in range(8):
                rhs, _ = s.htile(k, ti, perm)
                out = pt[:, 0:n]
                s.mm(out, w[:, k, 0:128], rhs, k == 0, k == 7, [wb, s.BhT], [pb])
            off = 2 if ti == 0 else 262 + 512 * (ti - 1)
            s.act(cin[:, off:off + n], pt[:, 0:n], AF.Copy, [pb], [Bcin])
        NV = 2308
        if s.debug.get("conv_stop") == "mm":
            return
        s.ts("dve", acc[:, 0:NV], cin[:, 0:NV], s.pp[:, cwcol:cwcol + 1], None, ALU.mult, None, [Bcin, s.Bconst], [Bacc])
        for k in range(1, 5):
            s.stt(acc[:, 0:NV], cin[:, k:k + NV], s.pp[:, cwcol + k:cwcol + k + 1], acc[:, 0:NV], ALU.mult, ALU.add,
                  [Bcin, Bacc, s.Bconst], [Bacc])
        if s.debug.get("conv_stop") == "dve":
            return
        s.act(dst[:, 0:TC], acc[:, 0:TC], AF.Silu, [Bacc, s.Bconst], [Bdst], bias=s.pp[:, cbcol:cbcol + 1])
        s.act(dst[:, TC:T], acc[:, 260:260 + TL], AF.Silu, [Bacc, s.Bconst], [Bdst], bias=s.pp[:, cbcol:cbcol + 1])

    def stage_scan(self, l, kind, need_ctx):
        s = self
        ssd = kind == "ssd"
        nhd = 16 if ssd else 8
        NG = 2 * nhd
        W = 512 if ssd else 129
        VW = 512 if ssd else 130
        nunits = 2 if ssd else 8
        sc = 1.0 if ssd else 128.0 ** -0.5
        first_c = 0 if need_ctx else 2
        with scope() as lst:
            def A(name, shape, dt, stack=None):
                return (stack or lst).enter_context(s.sbt("sc_" + kind + name, list(shape), dt))
            if not ssd:
                s.hTp = A("hTp", [128, 8, TL], BF16)
                for k in range(8):
                    src = s.hT[:, k, LT0:LT0 + TL].rearrange("p (r w) -> p w r", w=64)
                    dst = s.hTp[:, k, :].rearrange("p (w r) -> p w r", r=32)
                    if k % 2 == 0:
                        s.act(dst, src, AF.Copy, [s.BhT], [s.BhT])
                    else:
                        s.P.op("dve", (lambda o, i_: (lambda e: e.tensor_copy(out=o, in_=i_)))(dst, src), [s.BhT], [s.BhT])
            la = A("la", [128, NCH, NG], F32)
            wl = A("wl", [128, NCH, NG], F32)
            cum = A("cum", [128, NCH, NG], F32)
            tot = A("tot", [128, NCH, NG], F32)
            bias = A("bias", [128, NCH, NG], F32)
            ecum = A("ecum", [128, NCH, NG], F32)
            gst = A("gst", [128, NCH, NG], F32)
            atot = A("atot", [128, NCH, NG], F32)
            Bd = newbuf("decay")
            with scope() as pl_:
                tmpa = A("tmpa", [128, NCH, 32], F32, pl_)
                tmpb = A("tmpb", [128, NCH, 32], F32, pl_)
                rowb = A("rowb", [128, 64], F32, pl_)
                Bt = newbuf("dtmp")
                gcol0 = 2560 if ssd else ML0 + 4096
                wg, wgb = s.load_w(s.w_in[l][:, gcol0:gcol0 + 32], 32)
                if ssd:
                    s.dma("sp", rowb[:, 0:64], s.rowp[l:l + 1, RP_DTB:RP_DTB + 64].broadcast_to([128, 64]), [], [Bt])
                    s.act(rowb[:, 32:64], rowb[:, 32:64], AF.Exp, [Bt], [Bt])
                else:
                    s.dma("sp", rowb[:, 0:32], s.rowp[l:l + 1, RP_MGB:RP_MGB + 32].broadcast_to([128, 32]), [], [Bt])
                for half in range(2):
                    pt, pb = s.ps.next()
                    for ci in range(9):
                        c = half * 9 + ci
                        for k in range(8):
                            s.mm(pt[:, ci * 32:ci * 32 + 32], s.hcols(k, c, perm=not ssd), wg[:, k, 0:32], k == 0, k == 7,
                                 [wgb, s.BhT], [pb])
                    s.tt("dve", tmpa[:, half * 9:half * 9 + 9, :], pt[:, 0:288].rearrange("p (c g) -> p c g", g=32),
                         rowb[:, 0:32].unsqueeze(1).broadcast_to([128, 9, 32]), ALU.add, [pb, Bt], [Bt])
                if ssd:
                    s.act(tmpb[:], tmpa[:], AF.Exp, [Bt], [Bt])
                    s.act(tmpb[:], tmpb[:], AF.Ln, [Bt], [Bt], bias=1.0)
                    s.act(wl[:], tmpb[:], AF.Ln, [Bt], [Bd])
                    s.stt(la[:], tmpb[:], -1.0, rowb[:, 32:64].unsqueeze(1).broadcast_to([128, NCH, 32]), ALU.mult, ALU.mult,
                          [Bt], [Bd])
                else:
                    for d in range(2):
                        s.act(wl[:, :, d * 8:d * 8 + 8], tmpa[:, :, d * 16:d * 16 + 8], AF.Copy, [Bt], [Bd])
                        s.act(tmpb[:, :, d * 8:d * 8 + 8], tmpa[:, :, d * 16 + 8:d * 16 + 16], AF.Exp, [Bt], [Bt], scale=-1.0)
                    s.act(tmpb[:, :, 0:16], tmpb[:, :, 0:16], AF.Ln, [Bt], [Bt], bias=1.0)
                    s.ts("dve", la[:], tmpb[:, :, 0:16], -1.0, None, ALU.mult, None, [Bt], [Bd])
            for half in range(2):
                pc, pcb = s.ps.next()
                ptt, ptb = s.ps.next()
                for ci in range(9):
                    c = half * 9 + ci
                    o = ci * NG
                    s.mm(pc[:, o:o + nhd], s.triu(), la[:, c, 0:nhd], True, True, [Bd, s.Bconst], [pcb])
                    s.mm(pc[:, o + nhd:o + NG], s.tril(), la[:, c, nhd:NG], True, True, [Bd, s.Bconst], [pcb])
                    s.mm(ptt[:, o:o + NG], s.ones(), la[:, c, :], True, True, [Bd, s.Bconst], [ptb])
                s.act(cum[:, half * 9:half * 9 + 9, :], pc[:, 0:9 * NG].rearrange("p (c g) -> p c g", g=NG), AF.Copy,
                      [pcb], [Bd])
                s.act(tot[:, half * 9:half * 9 + 9, :], ptt[:, 0:9 * NG].rearrange("p (c g) -> p c g", g=NG), AF.Copy,
                      [ptb], [Bd])
            s.tt("dve", bias[:], wl[:], cum[:], ALU.subtract, [Bd], [Bd])
            s.act(ecum[:], cum[:], AF.Exp, [Bd], [Bd])
            s.tt("dve", gst[:], bias[:], tot[:], ALU.add, [Bd], [Bd])
            s.act(gst[:], gst[:], AF.Exp, [Bd], [Bd])
            s.act(atot[:], tot[:], AF.Exp, [Bd], [Bd])
            if s.debug.get("dump_decay") == kind:
                s.dump("la", la[:], [128, NCH, NG], Bd)
                s.dump("cum", cum[:], [128, NCH, NG], Bd)
                s.dump("wl", wl[:], [128, NCH, NG], Bd)
            if s.debug.get("scan_stop") == "decay":
                return
            QT = A("QT", [128, T], BF16)
            KT = A("KT", [128, T], BF16)
            Ktok = A("Ktok", [128, NCH, 128], BF16)
            V = A("V", [128, NCH, VW], BF16)
            Sin = [A("Sin0", [128, NCH, VW], BF16), A("Sin1", [128, NCH, VW], BF16)]
            S = [A("S0", [128, VW], F32), A("S1", [128, VW], F32)]
            BQT, BKT, BKtok, BV = newbuf(), newbuf(), newbuf(), newbuf()
            BSin = [newbuf(), newbuf()]
            BS = [newbuf(), newbuf()]
            ssq = A("ssq", [128, 2 * NCH], F32)
            Bssq = newbuf()
            s.memset("dve", ssq[:], 0.0, [Bssq])
            if ssd:
                dsk = A("dsk", [128, 1024], F32)
                Bdsk = newbuf()
                s.dma("sp", dsk[:], s.rowp[l:l + 1, RP_DSK:RP_DSK + 1024].broadcast_to([128, 1024]), [], [Bdsk])
            else:
                s.memset("dve", V[:], 0.0, [BV])
                s.memset("dve", V[:, :, 128:129], 1.0, [BV])

            for u in range(nunits):
                with scope() as cl_:
                    cin = A("cin", [128, 2312], F32, cl_)
                    acc = A("acc", [128, 2312], F32, cl_)
                    Bcin, Bacc = newbuf(), newbuf()
                    s.memset("dve", cin[:], 0.0, [Bcin])
                    if ssd:
                        g = u
                        cout = Rot(cl_, s.sbt, "sc_cout", [128, T], BF16, 1)
                        for i in range(4):
                            co, cob = cout.next()
                            idx = 4 * g + i
                            s.conv_tile(l, 512 * g + 128 * i, PP_SCW + 60 * l + 5 * idx, PP_SCB + 12 * l + idx, False,
                                        cin, Bcin, acc, Bacc, co, cob)
                            if s.debug.get("conv_stop") in ("mm", "dve", "silu"):
                                return
                            for c0 in range(0, NCH, 4):
                                ncs = min(4, NCH - c0)
                                po, pob = s.ps.next()
                                for ci in range(ncs):
                                    c = c0 + ci
                                    s.tr(po[:, ci * 128:ci * 128 + 128], co[:, c * 128:c * 128 + 128], [cob], [pob])
                                s.act(V[:, c0:c0 + ncs, i * 128:i * 128 + 128],
                                      po[:, 0:ncs * 128].rearrange("p (c e) -> p c e", e=128), AF.Copy, [pob], [BV])
                            if s.debug.get("conv_stop") == "tr1":
                                return
                        if s.debug.get("conv_stop") == "x":
                            return
                        s.conv_tile(l, 1024 + 128 * g, PP_SCW + 60 * l + 5 * (8 + g), PP_SCB + 12 * l + 8 + g, False,
                                    cin, Bcin, acc, Bacc, KT, BKT)
                        s.conv_tile(l, 1280 + 128 * g, PP_SCW + 60 * l + 5 * (10 + g), PP_SCB + 12 * l + 10 + g, False,
                                    cin, Bcin, acc, Bacc, QT, BQT)
                    else:
                        hd = u
                        s.conv_tile(l, ML0 + 128 * hd, PP_MCW + 80 * l + 5 * hd, PP_MCB + 16 * l + hd, True,
                                    cin, Bcin, acc, Bacc, QT, BQT)
                        s.conv_tile(l, ML0 + 1024 + 128 * hd, PP_MCW + 80 * l + 5 * (8 + hd), PP_MCB + 16 * l + 8 + hd, True,
                                    cin, Bcin, acc, Bacc, KT, BKT)
                if s.debug.get("conv_stop") == "bc":
                    return
                if ssd:
                    wz, wzb = s.load_w(s.w_in[l][:, 1536 + 512 * u:1536 + 512 * u + 512], 512)
                else:
                    hd = u
                    wv, wvb = s.load_w(s.w_in[l][:, ML0 + 2048 + 128 * hd:ML0 + 2048 + 128 * hd + 128], 128)
                    for c in range(NCH):
                        pt, pb = s.ps.next()
                        for k in range(8):
                            s.mm(pt[:, 0:128], s.hcols(k, c, True), wv[:, k, 0:128], k == 0, k == 7, [wvb, s.BhT], [pb])
                        s.act(V[:, c, 0:128], pt[:, 0:128], AF.Copy, [pb], [BV])
                    wz, wzb = s.load_w(s.w_in[l][:, ML0 + 3072 + 128 * hd:ML0 + 3072 + 128 * hd + 128], 128)
                for c0 in range(0, NCH, 4):
                    ncs = min(4, NCH - c0)
                    po, pob = s.ps.next()
                    for ci in range(ncs):
                        c = c0 + ci
                        s.tr(po[:, ci * 128:ci * 128 + 128], KT[:, c * 128:c * 128 + 128], [BKT], [pob])
                    s.act(Ktok[:, c0:c0 + ncs, :], po[:, 0:ncs * 128].rearrange("p (c e) -> p c e", e=128), AF.Copy,
                          [pob], [BKtok], scale=sc)

                if s.debug.get("scan_stop") == "conv":
                    return

                def gsl(arr, c, d):
                    if ssd:
                        return arr[:, c, d * 16 + 8 * u:d * 16 + 8 * u + 8].unsqueeze(2).broadcast_to([128, 8, 64])
                    return arr[:, c, d * 8 + u:d * 8 + u + 1]

                with scope() as ul_:
                    ScT = Rot(ul_, s.sbt, "sc_ScT" + kind, [128, 128], BF16, 3)
                    DT = Rot(ul_, s.sbt, "sc_DT" + kind, [128, 4, 128], BF16, 2)
                    PT = Rot(ul_, s.sbt, "sc_PT" + kind, [128, 4, 128], BF16, 8)
                    Vw = Rot(ul_, s.sbt, "sc_Vw" + kind, [128, VW], BF16, 3)
                    zs = Rot(ul_, s.sbt, "sc_zs" + kind, [128, W if ssd else 128], F32, 2 if ssd else 3)
                    t1 = Rot(ul_, s.sbt, "sc_t1" + kind, [128, VW], F32, 2)
                    t2 = Rot(ul_, s.sbt, "sc_t2" + kind, [128, VW], F32, 2)
                    t3 = Rot(ul_, s.sbt, "sc_t3" + kind, [128, VW], F32, 2)
                    yo = Rot(ul_, s.sbt, "sc_yo" + kind, [128, VW], F32 if ssd else BF16, 2 if ssd else 3)
                    sm = Rot(ul_, s.sbt, "sc_sm" + kind, [128, 8], F32, 2)
                    ymc = Rot(ul_, s.sbt, "sc_ymc" + kind, [128, 128], BF16, 2)
                    cTr = Rot(ul_, s.sbt, "sc_cT" + kind, [16, 128], F32, 4)
                    orders = [list(range(NCH)), [1, 0] + list(range(NCH - 1, 1, -1))]
                    for d in range(2):
                        s.memset("dve", S[d][:], 0.0, [BS[d]])
                    for oi in range(NCH):
                        for d in range(2):
                            c = orders[d][oi]
                            s.P.op("dve", (lambda o, i_: (lambda e: e.tensor_copy(out=o, in_=i_)))(Sin[d][:, c, :], S[d][:]),
                                   [BS[d]], [BSin[d]])
                            if oi == NCH - 1:
                                continue
                            vw, vwb = Vw.next()
                            if ssd:
                                s.tt("pool", vw[:, :].rearrange("p (h e) -> p h e", e=64),
                                     V[:, c, :].rearrange("p (h e) -> p h e", e=64), gsl(gst, c, d), ALU.mult,
                                     [BV, Bd], [vwb])
                            else:
                                s.act(vw[:, :], V[:, c, :], AF.Identity, [BV, Bd], [vwb], scale=gsl(gst, c, d))
                            pl, plb = s.ps.next()
                            s.mm(pl[:, 0:W], Ktok[:, c, :], vw[:, 0:W], True, True, [BKtok, vwb], [plb])
                            if ssd:
                                sv = S[d][:, :].rearrange("p (h e) -> p h e", e=64)
                                s.tt("dve", sv, sv, gsl(atot, c, d), ALU.mult, [BS[d], Bd], [BS[d]])
                                s.tt("dve", S[d][:, 0:W], S[d][:, 0:W], pl[:, 0:W], ALU.add, [BS[d], plb], [BS[d]])
                            else:
                                s.stt(S[d][:, 0:W], S[d][:, 0:W], gsl(atot, c, d), pl[:, 0:W], ALU.mult, ALU.add,
                                      [BS[d], plb, Bd], [BS[d]])

                    if s.debug.get("scan_stop") == "pass1":
                        return
                    def front(c):
                        cs = slice(c * 128, c * 128 + 128)
                        pS, pSb = s.ps.next()
                        s.mm(pS[:, 0:128], KT[:, cs], QT[:, cs], True, True, [BKT, BQT], [pSb])
                        sct, sctb = ScT.next()
                        s.ts("dve", sct[:], pS[:, 0:128], sc, None, ALU.mult, None, [pSb], [sctb])
                        pz, pzb = s.ps.next()
                        nz = 512 if ssd else 128
                        for k in range(8):
                            s.mm(pz[:, 0:nz], s.hcols(k, c, perm=not ssd), wz[:, k, 0:nz], k == 0, k == 7,
                                 [wzb, s.BhT], [pzb])
                        z, zb = zs.next()
                        s.act(z[:, 0:nz], pz[:, 0:nz], AF.Silu if ssd else AF.Sigmoid, [pzb], [zb])
                        aset = c % 2
                        pA0, pAb0 = s.pacc.tiles[aset], s.pacc.bufs[aset]
                        pA = [pA0, pA0 if ssd else pA0[:, 256:512]]
                        pts = []
                        for d in range(2):
                            mask = s.maskb[:, 0:128] if d == 0 else s.maskb[:, 128:256]
                            pc_, pcb_ = s.ps.next()
                            s.mm(pc_[0:nhd, 0:128], la[:, c, d * nhd:(d + 1) * nhd], s.triu() if d == 0 else s.tril(),
                                 True, True, [Bd, s.Bconst], [pcb_])
                            cT, cTb = cTr.next()
                            s.cp("dve", cT[0:nhd, :], pc_[0:nhd, 0:128], [pcb_], [cTb])
                            ngrp = 2 if ssd else 1
                            for hq in range(ngrp):
                                nh4 = 4 if ssd else 1
                                pD, pDb = s.ps.next()
                                dt_, dtb = DT.next()
                                for hh in range(nh4):
                                    hl = (8 * u + hq * 4 + hh) if ssd else u
                                    s.mm(pD[:, hh * 128:hh * 128 + 128], s.sel[0:nhd, hl * 128:hl * 128 + 128],
                                         cT[0:nhd, :], True, False, [cTb, s.Bconst], [pDb])
                                    s.mm(pD[:, hh * 128:hh * 128 + 128], s.identb[:], mask, False, True, [s.Bconst], [pDb])
                                    s.act(dt_[:, hh, :], pD[:, hh * 128:hh * 128 + 128], AF.Exp, [pDb, Bd], [dtb],
                                          bias=bias[:, c, d * nhd + hl:d * nhd + hl + 1])
                                p_, ptb_ = PT.next()
                                s.tt("dve", p_[:, 0:nh4, :], dt_[:, 0:nh4, :],
                                     sct[:, :].unsqueeze(1).broadcast_to([128, nh4, 128]), ALU.mult, [dtb, sctb], [ptb_])
                                pts.append((p_, ptb_, d, hq, nh4))
                        return (z, zb, pA, pAb0, pts)

                    def frontB(c, fr):
                        z, zb, pA, pAb0, pts = fr
                        for (p_, ptb_, d, hq, nh4) in pts:
                            for hh in range(nh4):
                                if ssd:
                                    h = hq * 4 + hh
                                    s.mm(pA[0][:, h * 64:h * 64 + 64], p_[:, hh, :], V[:, c, h * 64:h * 64 + 64],
                                         d == 0 and h == 0, d == 1, [ptb_, BV], [pAb0], skip=True)
                                else:
                                    s.mm(pA[d][:, 0:W], p_[:, 0, :], V[:, c, 0:W], True, True, [ptb_, BV], [pAb0])

                    def back(c, fr):
                        z, zb, pA, pAb0, pts = fr
                        cs = slice(c * 128, c * 128 + 128)
                        pAb = [pAb0, pAb0]
                        if ssd:
                            pB = [s.pacc.tiles[2], s.pacc.tiles[3]]
                            pBb = [s.pacc.bufs[2], s.pacc.bufs[3]]
                        else:
                            pB = [s.pacc.tiles[2], s.pacc.tiles[2][:, 256:512]]
                            pBb = [s.pacc.bufs[2], s.pacc.bufs[2]]
                        for d in range(2):
                            s.mm(pB[d][:, 0:W], QT[:, cs], Sin[d][:, c, 0:W], True, True, [BQT, BSin[d]], [pBb[d]])
                        if ssd:
                            a1, a1b = t1.next()
                            a2, a2b = t2.next()
                            a3, a3b = t3.next()
                            v3 = lambda t: t[:, :].rearrange("p (h e) -> p h e", e=64)
                            s.tt("dve", v3(a1), pB[0][:, 0:512].rearrange("p (h e) -> p h e", e=64), gsl(ecum, c, 0),
                                 ALU.mult, [pBb[0], Bd], [a1b])
                            s.tt("dve", v3(a2), pB[1][:, 0:512].rearrange("p (h e) -> p h e", e=64), gsl(ecum, c, 1),
                                 ALU.mult, [pBb[1], Bd], [a2b])
                            s.tt("pool", a3[:, :], V[:, c, :], dsk[:, 512 * u:512 * u + 512], ALU.mult, [BV, Bdsk], [a3b])
                            s.tt("pool", a1[:, :], a1[:, :], a2[:, :], ALU.add, [a1b, a2b], [a1b])
                            s.tt("pool", a1[:, :], a1[:, :], a3[:, :], ALU.add, [a1b, a3b], [a1b])
                            s.tt("dve", a1[:, :], a1[:, :], pA[0][:, 0:512], ALU.add, [a1b, pAb[0]], [a1b])
                            y, yb = yo.next()
                            s.tt("dve", y[:, :], a1[:, :], z[:, 0:512], ALU.mult, [a1b, zb], [yb])
                            s.act(a2[:, :], y[:, :], AF.Square, [yb, a2b], [a2b, Bssq],
                                  accum_out=ssq[:, 2 * c + u:2 * c + u + 1])
                            s.dma(None, s.ys_tok[c * 128:c * 128 + 128, 512 * u:512 * u + 512], y[:, :], [yb], [s.Bys])
                        else:
                            yd = []
                            ydb = []
                            small, smb = sm.next()
                            for d in range(2):
                                ta, tab = (t1 if d == 0 else t2).next()
                                s.act(ta[:, 0:W], pA[d][:, 0:W], AF.Copy, [pAb[d]], [tab])
                                s.stt(ta[:, 0:W], pB[d][:, 0:W], gsl(ecum, c, d), ta[:, 0:W], ALU.mult, ALU.add,
                                      [pBb[d], tab, Bd], [tab])
                                s.act(small[:, d:d + 1], ta[:, 128:129], AF.Abs, [tab], [smb])
                                s.ts("dve", small[:, d:d + 1], small[:, d:d + 1], 1.0, None, ALU.max, None, [smb], [smb])
                                s.recip(small[:, 2 + d:3 + d], small[:, d:d + 1], [smb], [smb])
                                yd.append(ta)
                                ydb.append(tab)
                            a3, a3b = t3.next()
                            s.ts("dve", a3[:, 0:128], yd[0][:, 0:128], small[:, 2:3], None, ALU.mult, None,
                                 [ydb[0], smb], [a3b])
                            s.stt(a3[:, 0:128], yd[1][:, 0:128], small[:, 3:4], a3[:, 0:128], ALU.mult, ALU.add,
                                  [ydb[1], smb, a3b], [a3b])
                            s.act(yd[0][:, 0:128], a3[:, 0:128], AF.Square, [a3b, ydb[0]], [ydb[0], smb],
                                  accum_out=small[:, 4:5])
                            s.act(small[:, 5:6], small[:, 4:5], AF.Sqrt, [smb], [smb], scale=1.0 / 128.0, bias=EPS)
                            s.recip(small[:, 6:7], small[:, 5:6], [smb], [smb])
                            y, yb = yo.next()
                            s.stt(y[:, 0:128], a3[:, 0:128], small[:, 6:7], z[:, 0:128], ALU.mult, ALU.mult,
                                  [a3b, smb, zb], [yb])
                            return (y, yb)
                        return None

                    def tail(c, yy):
                        y, yb = yy
                        po_, pob = s.ps.next()
                        po = po_[:, 0:128]
                        s.tr(po, y[:, 0:128], [yb], [pob])
                        ym, ymb = ymc.next()
                        s.ts("dve", ym[:, :], po, s.pp[:, PP_MNW + 8 * l + u:PP_MNW + 8 * l + u + 1], None, ALU.mult, None,
                             [pob, s.Bconst], [ymb])
                        s.dma(None, s.ybT[1][u * 128:u * 128 + 128, c * 128:c * 128 + 128], ym[:, :], [ymb],
                              [s.BybT[1]])

                    chunks = list(range(first_c, NCH))
                    fr = front(chunks[0])
                    pend = None
                    for ci, c in enumerate(chunks):
                        nxt = front(chunks[ci + 1]) if ci + 1 < len(chunks) else None
                        frontB(c, fr)
                        if pend is not None:
                            tail(*pend)
                        yy = back(c, fr)
                        pend = (c, yy) if yy is not None else None
                        fr = nxt
                    if pend is not None:
                        tail(*pend)

            if s.debug.get("scan_dump") == kind:
                s.dump("V", V[:], [128, NCH, VW], BV, BF16)
                s.dump("KT", KT[:], [128, T], BKT, BF16)
                s.dump("QT", QT[:], [128, T], BQT, BF16)
                s.dump("Ktok", Ktok[:], [128, NCH, 128], BKtok, BF16)
                s.dump("Sin0", Sin[0][:], [128, NCH, VW], BSin[0], BF16)
                s.dump("Sin1", Sin[1][:], [128, NCH, VW], BSin[1], BF16)
                s.dump("ssq", ssq[:], [128, 2 * NCH], Bssq)
            if ssd:
                ytr = Rot(lst, s.sbt, "sc_ytr", [128, 1024], F32, 3)
                ynr = Rot(lst, s.sbt, "sc_ynr", [128, 1024], BF16, 2)
                ysc = Rot(lst, s.sbt, "sc_ysc", [128, 8, 128], BF16, 2)
                sm2 = Rot(lst, s.sbt, "sc_sm2", [128, 8], F32, 2)
                def yload(c):
                    yt, ytb = ytr.next()
                    s.dma(None, yt[:, :], s.ys_tok[c * 128:c * 128 + 128, :], [s.Bys], [ytb])
                    return yt, ytb
                ynext = yload(first_c)
                for c in range(first_c, NCH):
                    yt, ytb = ynext
                    ynext = yload(c + 1) if c + 1 < NCH else None
                    small, smb = sm2.next()
                    s.tt("dve", small[:, 0:1], ssq[:, 2 * c:2 * c + 1], ssq[:, 2 * c + 1:2 * c + 2], ALU.add, [Bssq], [smb])
                    s.act(small[:, 1:2], small[:, 0:1], AF.Sqrt, [smb], [smb], scale=1.0 / 1024.0, bias=EPS)
                    s.recip(small[:, 2:3], small[:, 1:2], [smb], [smb])
                    yn, ynb = ynr.next()
                    s.ts("dve", yn[:, :], yt[:, :], small[:, 2:3], None, ALU.mult, None, [ytb, smb], [ynb])
                    yc, ycb = ysc.next()
                    for j0 in range(0, 8, 4):
                        po, pob = s.ps.next()
                        for ji in range(4):
                            j = j0 + ji
                            s.tr(po[:, ji * 128:ji * 128 + 128], yn[:, j * 128:j * 128 + 128], [ynb], [pob])
                        for ji in range(4):
                            j = j0 + ji
                            s.act(yc[:, j, :], po[:, ji * 128:ji * 128 + 128], AF.Identity, [pob, s.Bconst], [ycb],
                                  scale=s.pp[:, PP_SNW + 8 * l + j:PP_SNW + 8 * l + j + 1])
                    s.dma(None, s.ybT[0][:, c * 128:c * 128 + 128].rearrange("(j p) t -> p j t", p=128), yc[:, :, :],
                          [ycb], [s.BybT[0]])

    def stage_hyena(self, l, need_ctx):
        self.hyena_seq(l, TL, LT0, TC, 0)
        if need_ctx:
            self.hyena_seq(l, TC, CT0, 0, 16)

    def bload(self, dst, row_ap, n, buf):
        self.dma("sp", dst, row_ap.broadcast_to([128, n]), [], [buf])

    def hyena_seq(self, l, L, col0, tok0, tn0):
        s = self
        nt = L // 128
        nw = _wp(L) // 128
        dft = s.dft[L]
        TWO_PI = 2.0 * math.pi
        MAGIC = 12582912.0
        with scope() as lst:
            def A(name, shape, dt):
                return lst.enter_context(s.sbt("hy_" + name, list(shape), dt))
            feats = A("feats", [33, L], F32)
            w1 = A("w1", [33, 64], F32)
            w2 = A("w2", [64, 64], F32)
            w3 = A("w3", [64, 4096], F32)
            hid = [A("hid1", [64, L], F32), A("hid2", [64, L], F32)]
            absdec = A("absdec", [128, 4096], F32)
            hrow = A("hrow", [128, 4096], F32)
            Bw, Bh, Bdec, Brow = newbuf(), [newbuf(), newbuf()], newbuf(), newbuf()
            ta = Rot(lst, s.sbt, "hy_ta", [64, 512], F32, 2)
            tb = Rot(lst, s.sbt, "hy_tb", [64, 512], F32, 2)
            win = Rot(lst, s.sbt, "hy_win", [128, 512], F32, 2)
            hso = Rot(lst, s.sbt, "hy_hso", [128, 2048], BF16, 2)
            hdo = Rot(lst, s.sbt, "hy_hdo", [128, 2048], BF16, 2)
            s.dma("sp", feats[:], s.feats[L], [], [Bw])
            s.dma("sp", w1[:], s.hyw1[l], [], [Bw])
            s.dma("sp", w2[:], s.hyw2[l], [], [Bw])
            s.dma("sp", w3[:], s.hyw3[l], [], [Bw])
            s.bload(absdec[:], s.rowp[l:l + 1, RP_HDEC:RP_HDEC + 4096], 4096, Bdec)
            s.act(absdec[:], absdec[:], AF.Abs, [Bdec], [Bdec])
            nq = max(1, L // 512)
            qn = min(512, L)
            for layer in range(2):
                wgt = w1 if layer == 0 else w2
                bcol = (PP_HB1 if layer == 0 else PP_HB2) + l
                for q in range(nq):
                    pt, pb = s.ps.next()
                    src = feats[:, q * qn:(q + 1) * qn] if layer == 0 else hid[0][:, q * qn:(q + 1) * qn]
                    s.mm(pt[0:64, 0:qn], wgt[:, :], src, True, True, [Bw] + ([Bh[0]] if layer else []), [pb])
                    a, ab = ta.next()
                    b, bb = tb.next()
                    s.ts("dve", a[:, 0:qn], pt[0:64, 0:qn], s.pp[0:64, bcol:bcol + 1], None, ALU.add, None, [pb, s.Bconst], [ab])
                    s.ts("dve", b[:, 0:qn], a[:, 0:qn], math.pi, None, ALU.is_gt, None, [ab], [bb])
                    s.stt(a[:, 0:qn], b[:, 0:qn], -TWO_PI, a[:, 0:qn], ALU.mult, ALU.add, [bb, ab], [ab])
                    s.ts("dve", b[:, 0:qn], a[:, 0:qn], -math.pi, None, ALU.is_lt, None, [ab], [bb])
                    s.stt(a[:, 0:qn], b[:, 0:qn], TWO_PI, a[:, 0:qn], ALU.mult, ALU.add, [bb, ab], [ab])
                    s.act(hid[layer][:, q * qn:(q + 1) * qn], a[:, 0:qn], AF.Sin, [ab], [Bh[layer]])
            for tc in range(nt):
                for cb in range(8):
                    pt, pb = s.ps.next()
                    s.mm(pt[:, 0:512], hid[1][:, tc * 128:tc * 128 + 128], w3[:, cb * 512:cb * 512 + 512], True, True,
                         [Bh[1], Bw], [pb])
                    w_, wb_ = win.next()
                    s.act(w_[:, :], absdec[:, cb * 512:cb * 512 + 512], AF.Exp, [Bdec, s.Bconst], [wb_],
                          scale=s.tneg[:, tn0 + tc:tn0 + tc + 1])
                    s.tt("dve", hrow[:, cb * 512:cb * 512 + 512], pt[:, 0:512], w_[:, :], ALU.mult, [pb, wb_], [Brow])
                hs, hsb = hso.next()
                hd, hdb = hdo.next()
                for o in range(2):
                    hf = hrow[:, o * 2048:o * 2048 + 1024]
                    hb = hrow[:, o * 2048 + 1024:o * 2048 + 2048]
                    s.tt("dve", hs[:, o * 1024:o * 1024 + 1024], hf, hb, ALU.add, [Brow], [hsb])
                    s.tt("pool", hd[:, o * 1024:o * 1024 + 1024], hb, hf, ALU.subtract, [Brow], [hdb])
                s.dma(None, s.hsd[0][tc * 128:tc * 128 + 128, :], hs[:, :], [hsb], [s.Bhsd])
                s.dma(None, s.hsd[1][tc * 128:tc * 128 + 128, :], hd[:, :], [hdb], [s.Bhsd])
            if s.debug.get("hy_dump") and L == TL:
                s.dump("hid1", hid[0][:], [64, L], Bh[0])
                s.dump("hid2", hid[1][:], [64, L], Bh[1])
                s.dump("hrow", hrow[:], [128, 4096], Brow)
                s.dump("hs", hs[:], [128, 2048], hsb, BF16)
        with scope() as lst:
            hsb_ = Rot(lst, s.sbt, "hy_hsb", [128, nt, 1024], BF16, 1)
            hdb_ = Rot(lst, s.sbt, "hy_hdb", [128, nt, 1024], BF16, 1)
            ctl = Rot(lst, s.sbt, "hy_ct", [128, nt * 128], BF16, 2)
            stl = Rot(lst, s.sbt, "hy_st", [128, nt * 128], BF16, 2)
            kr = Rot(lst, s.sbt, "hy_kr", [128, 512], F32, 2)
            ki = Rot(lst, s.sbt, "hy_ki", [128, 512], F32, 2)
            for cbp in range(2):
                hsT, hsB = hsb_.next()
                hdT, hdB = hdb_.next()
                for tc in range(nt):
                    s.dma(None, hsT[:, tc, :], s.hsd[0][tc * 128:tc * 128 + 128, cbp * 1024:cbp * 1024 + 1024], [s.Bhsd], [hsB])
                    s.dma(None, hdT[:, tc, :], s.hsd[1][tc * 128:tc * 128 + 128, cbp * 1024:cbp * 1024 + 1024], [s.Bhsd], [hdB])
                for wt in range(nw):
                    ct, ctb = ctl.next()
                    st_, stb = stl.next()
                    s.dma(None, ct[:], dft["ct"][wt], [], [ctb])
                    s.dma(None, st_[:], dft["st"][wt], [], [stb])
                    for half in range(2):
                        cb = cbp * 2 + half
                        hc = slice(half * 512, half * 512 + 512)
                        pR, pRb = s.ps.next()
                        pI, pIb = s.ps.next()
                        for tc in range(nt):
                            s.mm(pR[:, 0:512], ct[:, tc * 128:tc * 128 + 128], hsT[:, tc, hc], tc == 0, tc == nt - 1,
                                 [ctb, hsB], [pRb])
                        for tc in range(nt):
                            s.mm(pI[:, 0:512], st_[:, tc * 128:tc * 128 + 128], hdT[:, tc, hc], tc == 0, tc == nt - 1,
                                 [stb, hdB], [pIb])
                        kre, kreb = kr.next()
                        kim, kimb = ki.next()
                        s.act(kre[:, :], pR[:, 0:512], AF.Copy, [pRb], [kreb])
                        s.cp("dve", kim[:, :], pI[:, 0:512], [pIb], [kimb])
                        s.dma(None, s.hyK[0][wt * 128:wt * 128 + 128, cb * 512:cb * 512 + 512], kre[:, :], [kreb], [s.BhyK])
                        s.dma(None, s.hyK[1][wt * 128:wt * 128 + 128, cb * 512:cb * 512 + 512], kim[:, :], [kimb], [s.BhyK])
        if s.debug.get("hy_stop") == "filt":
            return
        for cb2 in range(2):
            with scope() as lst:
                z = lst.enter_context(s.sbt("hy_z", [128, nt, 512], BF16))
                Yre = lst.enter_context(s.sbt("hy_yre", [128, nw, 512], BF16))
                Yim = lst.enter_context(s.sbt("hy_yim", [128, nw, 512], BF16))
                Wk = [lst.enter_context(s.sbt("hy_wk%d" % i, [128, 8, 512], BF16)) for i in range(3)]
                rowt = lst.enter_context(s.sbt("hy_rowt", [128, 3, 512], F32))
                cbias = lst.enter_context(s.sbt("hy_cbias", [128, 512], F32))
                skipb = lst.enter_context(s.sbt("hy_skipb", [128, 512], F32))
                Bz, BY, BWk, Brow, Bcb, Bsk = newbuf(), newbuf(), newbuf(), newbuf(), newbuf(), newbuf()
                kr = Rot(lst, s.sbt, "hy_kr2", [128, 512], F32, 2)
                ki = Rot(lst, s.sbt, "hy_ki2", [128, 512], F32, 2)
                tmp = [Rot(lst, s.sbt, "hy_tmp%d" % i, [128, 512], F32, 1) for i in range(4)]
                xg = Rot(lst, s.sbt, "hy_xg", [128, 512], F32, 2)
                yb16 = Rot(lst, s.sbt, "hy_yb16", [128, 512], BF16, 2)
                yT = Rot(lst, s.sbt, "hy_yT", [128, 4, 128], BF16, 2)

                def prep_part(part):
                    c0 = HY0 + part * 1024 + cb2 * 512
                    w, wb = s.load_w(s.w_in[l][:, c0:c0 + 512], 512)
                    for tap in range(3):
                        o = RP_HCW + tap * 3072 + part * 1024 + cb2 * 512
                        s.bload(rowt[:, tap, :], s.rowp[l:l + 1, o:o + 512], 512, Brow)
                    o = RP_HCB + part * 1024 + cb2 * 512
                    s.bload(cbias[:, :], s.rowp[l:l + 1, o:o + 512], 512, Bcb)
                    for tap in range(3):
                        s.tt("dve" if tap != 1 else "pool", Wk[tap][:], w[:, :, :],
                             rowt[:, tap, :].unsqueeze(1).broadcast_to([128, 8, 512]), ALU.mult, [wb, Brow], [BWk])

                def proj3(tc):
                    pt, pb = s.ps.next()
                    i = 0
                    for tap in range(3):
                        for k in range(8):
                            c_ = col0 + tc * 128 + tap - 1
                            s.mm(pt[:, 0:512], s.hT[:, k, c_:c_ + 128], Wk[tap][:, k, :], i == 0, i == 23, [s.BhT, BWk], [pb])
                            i += 1
                    return pt, pb

                prep_part(0)
                for tc in range(nt):
                    pt, pb = proj3(tc)
                    s.tt("dve", z[:, tc, :], pt[:, 0:512], cbias[:, :], ALU.add, [pb, Bcb], [Bz])
                for n in range(2):
                    prep_part(n + 1)
                    o = RP_HSKIP + n * 1024 + cb2 * 512
                    s.bload(skipb[:, :], s.rowp[l:l + 1, o:o + 512], 512, Bsk)
                    with scope() as fl_:
                        ctl = Rot(fl_, s.sbt, "hy_ct2", [128, nt * 128], BF16, 2)
                        stl = Rot(fl_, s.sbt, "hy_st2", [128, nt * 128], BF16, 2)
                        for wt in range(nw):
                            ct, ctb = ctl.next()
                            st_, stb = stl.next()
                            s.dma(None, ct[:], dft["ct"][wt], [], [ctb])
                            s.dma(None, st_[:], dft["st"][wt], [], [stb])
                            kre, kreb = kr.next()
                            kim, kimb = ki.next()
                            kc = n * 1024 + cb2 * 512
                            s.dma(None, kre[:, :], s.hyK[0][wt * 128:wt * 128 + 128, kc:kc + 512], [s.BhyK], [kreb])
                            s.dma(None, kim[:, :], s.hyK[1][wt * 128:wt * 128 + 128, kc:kc + 512], [s.BhyK], [kimb])
                            pR, pRb = s.ps.next()
                            pS, pSb = s.ps.next()
                            for tc in range(nt):
                                s.mm(pR[:, 0:512], ct[:, tc * 128:tc * 128 + 128], z[:, tc, :], tc == 0, tc == nt - 1,
                                     [ctb, Bz], [pRb])
                            for tc in range(nt):
                                s.mm(pS[:, 0:512], st_[:, tc * 128:tc * 128 + 128], z[:, tc, :], tc == 0, tc == nt - 1,
                                     [stb, Bz], [pSb])
                            t = [r_.next() for r_ in tmp]
                            s.tt("dve", t[0][0][:, :], pR[:, 0:512], kre[:, :], ALU.mult, [pRb, kreb], [t[0][1]])
                            s.tt("dve", t[1][0][:, :], pS[:, 0:512], kim[:, :], ALU.mult, [pSb, kimb], [t[1][1]])
                            s.tt("dve", t[2][0][:, :], pR[:, 0:512], kim[:, :], ALU.mult, [pRb, kimb], [t[2][1]])
                            s.tt("dve", t[3][0][:, :], pS[:, 0:512], kre[:, :], ALU.mult, [pSb, kreb], [t[3][1]])
                            s.tt("pool", Yre[:, wt, :], t[0][0][:, :], t[1][0][:, :], ALU.add, [t[0][1], t[1][1]], [BY])
                            s.tt("pool", Yim[:, wt, :], t[2][0][:, :], t[3][0][:, :], ALU.subtract, [t[2][1], t[3][1]], [BY])
                    with scope() as il_:
                        cil = Rot(il_, s.sbt, "hy_ci2", [128, nw * 128], BF16, 2)
                        sil = Rot(il_, s.sbt, "hy_si2", [128, nw * 128], BF16, 2)
                        for tc in range(nt):
                            ci, cib = cil.next()
                            si, sib = sil.next()
                            s.dma(None, ci[:], dft["ci"][tc], [], [cib])
                            s.dma(None, si[:], dft["si"][tc], [], [sib])
                            pY, pYb = s.ps.next()
                            for wc in range(nw):
                                s.mm(pY[:, 0:512], ci[:, wc * 128:wc * 128 + 128], Yre[:, wc, :], wc == 0, False,
                                     [cib, BY], [pYb])
                            for wc in range(nw):
                                s.mm(pY[:, 0:512], si[:, wc * 128:wc * 128 + 128], Yim[:, wc, :], False, wc == nw - 1,
                                     [sib, BY], [pYb])
                            pX, pXb = proj3(tc)
                            x_, xb_ = xg.next()
                            s.tt("dve", x_[:, :], pX[:, 0:512], cbias[:, :], ALU.add, [pXb, Bcb], [xb_])
                            t0, t0b = tmp[0].next()
                            s.tt("pool", t0[:, :], z[:, tc, :], skipb[:, :], ALU.mult, [Bz, Bsk], [t0b])
                            s.tt("dve", t0[:, :], t0[:, :], pY[:, 0:512], ALU.add, [t0b, pYb], [t0b])
                            if n == 0:
                                s.tt("dve", z[:, tc, :], t0[:, :], x_[:, :], ALU.mult, [t0b, xb_], [Bz])
                            else:
                                y_, yb_ = yb16.next()
                                s.tt("dve", y_[:, :], t0[:, :], x_[:, :], ALU.mult, [t0b, xb_], [yb_])
                                po, pob = s.ps.next()
                                for j in range(4):
                                    s.tr(po[:, j * 128:j * 128 + 128], y_[:, j * 128:j * 128 + 128], [yb_], [pob])
                                yt_, ytb_ = yT.next()
                                s.act(yt_[:, :, :], po[:, 0:512].rearrange("p (j t) -> p j t", t=128), AF.Copy, [pob], [ytb_])
                                tk = tok0 + tc * 128
                                s.dma(None, s.ybT[2][cb2 * 512:cb2 * 512 + 512, tk:tk + 128].rearrange("(j p) t -> p j t", p=128),
                                      yt_[:, :, :], [ytb_], [s.BybT[2]])

    def tok_tiles(self, need_ctx):
        out = []
        if need_ctx:
            out.append((0, TC, 0, 1))
        for q in range(4):
            out.append((q + 1, 512, TC + 512 * q, 0))
        return out

    def resid_load(self, n, dt2, off, xr):
        s = self
        xt, xb = xr.next()
        rows = slice(dt2 * 128, dt2 * 128 + 128)
        s.dma(None, xt[:, 0:n], s.xres[rows, off:off + n], [s.BxresL[dt2]], [xb])
        return xt, xb

    def resid_finish(self, pO, pOb, n, dt2, off, gate_m, which, pre, ob):
        s = self
        xt, xb = pre
        rows = slice(dt2 * 128, dt2 * 128 + 128)
        xo, xob = ob.next()
        s.stt(xo[:, 0:n], pO[:, 0:n], s.modcol(gate_m, dt2, which), xt[:, 0:n], ALU.mult, ALU.add,
              [pOb, xb, s.Bmod], [xob])
        s.dma(None, s.xres[rows, off:off + n], xo[:, 0:n], [xob], [s.BxresL[dt2]])

    def stage_merge(self, l, need_ctx):
        s = self
        tiles = s.tok_tiles(need_ctx)
        with scope() as lst:
            mT = lst.enter_context(s.sbt("mg_mT", [128, 8, T], BF16))
            yT = lst.enter_context(s.sbt("mg_yT", [128, 8, T], BF16))
            BmT, ByT = newbuf(), newbuf()
            sg = Rot(lst, s.sbt, "mg_sg", [128, 512], F32, 2)
            tm = Rot(lst, s.sbt, "mg_tm", [128, 512], F32, 2)
            with scope() as l2:
                yP = l2.enter_context(s.sbt("mg_yP", [128, 8, TL], BF16))
                ByP = newbuf()
                for n in range(3):
                    for k in range(8):
                        rows = slice(k * 128, k * 128 + 128)
                        if n == 1:
                            s.dma(None, yT[:, k, 0:TC], s.ybT[n][rows, 0:TC], [s.BybT[n]], [ByT])
                            s.dma(None, yP[:, k, :], s.ybT[n][rows, TC:T], [s.BybT[n]], [ByP])
                            src = yP[:, k, :].rearrange("p (w r) -> p r w", r=32)
                            dst = yT[:, k, TC:T].rearrange("p (r w) -> p r w", w=64)
                            if k % 2 == 0:
                                s.act(dst, src, AF.Copy, [ByP], [ByT])
                            else:
                                s.P.op("dve", (lambda o, i_: (lambda e: e.tensor_copy(out=o, in_=i_)))(dst, src), [ByP], [ByT])
                        else:
                            s.dma(None, yT[:, k, :], s.ybT[n][rows, :], [s.BybT[n]], [ByT])
                    for dt in range(8):
                        wb_t, wbb = s.load_w(s.w_branch[l][n][:, dt * 128:dt * 128 + 128], 128)
                        c0 = G0 + n * 1024 + dt * 128
                        wg_t, wgb = s.load_w(s.w_in[l][:, c0:c0 + 128], 128)
                        for (ti, nt_, off, which) in tiles:
                            pP, pPb = s.ps.next()
                            for k in range(8):
                                s.mm(pP[:, 0:nt_], wb_t[:, k, 0:128], yT[:, k, off:off + nt_], k == 0, k == 7, [wbb, ByT], [pPb])
                            pG, pGb = s.ps.next()
                            for k in range(8):
                                s.mm(pG[:, 0:nt_], wg_t[:, k, 0:128], s.htile(k, ti)[0], k == 0, k == 7, [wgb, s.BhT], [pGb])
                            g, gb = sg.next()
                            s.act(g[:, 0:nt_], pG[:, 0:nt_], AF.Sigmoid, [pGb], [gb])
                            if n == 0:
                                s.tt("dve", mT[:, dt, off:off + nt_], pP[:, 0:nt_], g[:, 0:nt_], ALU.mult, [pPb, gb], [BmT])
                            else:
                                t, tb = tm.next()
                                s.tt("dve", t[:, 0:nt_], pP[:, 0:nt_], g[:, 0:nt_], ALU.mult, [pPb, gb], [tb])
                                s.tt("dve", mT[:, dt, off:off + nt_], mT[:, dt, off:off + nt_], t[:, 0:nt_], ALU.add,
                                     [tb, BmT], [BmT])
            if s.debug.get("merge_dump"):
                s.dump("mT", mT[:], [128, 8, T], BmT, BF16)
            wo = lst.enter_context(s.sbt("mg_wo", [128, 8, D], BF16))
            Bwo = newbuf()
            s.dma("pool", wo[:], s.w_out[l].rearrange("(k p) n -> p k n", p=128), [], [Bwo])
            xr = Rot(lst, s.sbt, "mg_xr", [128, 512], F32, 3)
            ob = Rot(lst, s.sbt, "mg_ob", [128, 512], F32, 2)
            items = [(ti, nt_, off, which, dt2) for (ti, nt_, off, which) in tiles for dt2 in range(8)]
            pre = s.resid_load(items[0][1], items[0][4], items[0][2], xr)
            for i, (ti, nt_, off, which, dt2) in enumerate(items):
                pO, pOb = s.ps.next()
                for k in range(8):
                    s.mm(pO[:, 0:nt_], wo[:, k, dt2 * 128:dt2 * 128 + 128], mT[:, k, off:off + nt_], k == 0, k == 7,
                         [Bwo, BmT], [pOb])
                nxt = s.resid_load(items[i + 1][1], items[i + 1][4], items[i + 1][2], xr) if i + 1 < len(items) else None
                s.resid_finish(pO, pOb, nt_, dt2, off, 2, which, pre, ob)
                pre = nxt

    def stage_mlp(self, l, need_ctx):
        s = self
        tiles = s.tok_tiles(need_ctx)
        with scope() as lst:
            hid = lst.enter_context(s.sbt("ml_hid", [128, 32, 512], BF16))
            Bhid = newbuf()
            rl = Rot(lst, s.sbt, "ml_rl", [128, 512], BF16, 2)
            xr = Rot(lst, s.sbt, "ml_xr", [128, 512], F32, 3)
            ob = Rot(lst, s.sbt, "ml_ob", [128, 512], F32, 2)
            for (ti, nt_, off, which) in tiles:
                for fb in range(8):
                    w1t, w1b = s.load_w(s.mlp_w1[l][:, fb * 512:fb * 512 + 512], 512)
                    for fj in range(4):
                        f = fb * 4 + fj
                        pH, pHb = s.ps.next()
                        for k in range(8):
                            s.mm(pH[:, 0:nt_], w1t[:, k, fj * 128:fj * 128 + 128], s.htile(k, ti)[0], k == 0, k == 7,
                                 [w1b, s.BhT], [pHb])
                        r, rb = rl.next()
                        s.act(r[:, 0:nt_], pH[:, 0:nt_], AF.Relu, [pHb], [rb])
                        s.tt("dve", hid[:, f, 0:nt_], r[:, 0:nt_], r[:, 0:nt_], ALU.mult, [rb], [Bhid])
                pre = s.resid_load(nt_, 0, off, xr)
                for dt2 in range(8):
                    w2t, w2b = s.wrot.next()
                    w2v = w2t[:, :, :].rearrange("p k (a n) -> p (k a) n", n=128)
                    s.dma("pool", w2v, s.mlp_w2[l][:, dt2 * 128:dt2 * 128 + 128].rearrange("(f p) n -> p f n", p=128),
                          [], [w2b])
                    pO, pOb = s.ps.next()
                    for f in range(32):
                        s.mm(pO[:, 0:nt_], w2v[:, f, :], hid[:, f, 0:nt_], f == 0, f == 31, [w2b, Bhid], [pOb])
                    nxt = s.resid_load(nt_, dt2 + 1, off, xr) if dt2 + 1 < 8 else None
                    s.resid_finish(pO, pOb, nt_, dt2, off, 5, which, pre, ob)
                    pre = nxt

    def stage_final(self):
        s = self
        with scope() as lst:
            xt = Rot(lst, s.sbt, "fx", [128, 8, 512], F32, 2)
            sq = Rot(lst, s.sbt, "fsq", [128, 8, 512], F32, 1)
            rs = Rot(lst, s.sbt, "frs", [128, 512], F32, 2)
            ot = Rot(lst, s.sbt, "fo", [128, 8, 512], F32, 2)
            for q in range(4):
                t0 = TC + 512 * q
                x, xb = xt.next()
                s.dma(None, x[:, :, :], s.xres[:, t0:t0 + 512].rearrange("(k p) t -> p k t", p=128), list(s.BxresL), [xb])
                q_, qb = sq.next()
                s.act(q_[:, :, :], x[:, :, :], AF.Square, [xb], [qb])
                pt, pb = s.ps.next()
                for k in range(8):
                    s.mm(pt[:, 0:512], s.ones(), q_[:, k, :], k == 0, k == 7, [qb, s.Bconst], [pb])
                r, rb = rs.next()
                s.act(r[:, :], pt[:, 0:512], AF.Sqrt, [pb], [rb], scale=1.0 / D, bias=EPS)
                s.recip(r[:, :], r[:, :], [rb], [rb])
                o, ob_ = ot.next()
                for k in range(8):
                    s.stt(o[:, k, :], x[:, k, :], s.pp[:, PP_NF + k:PP_NF + k + 1], r[:, :], ALU.mult, ALU.mult,
                          [xb, rb, s.Bconst], [ob_])
                s.dma(None, s.out[:, 512 * q:512 * q + 512].rearrange("(k p) t -> p k t", p=128), o[:, :, :], [ob_], [Buf()])


def _bf16(a):
    import ml_dtypes
    return np.ascontiguousarray(a.astype(np.float32)).astype(ml_dtypes.bfloat16)


def _consts():
    k = np.arange(128)
    cst = np.zeros((128, CS_N), np.float32)
    cst[:, CS_TRIU:CS_TRIU + 128] = (k[:, None] <= k[None, :])
    cst[:, CS_TRIL:CS_TRIL + 128] = (k[:, None] >= k[None, :])
    cst[:, CS_MASKF:CS_MASKF + 128] = np.where(k[:, None] <= k[None, :], 0.0, NEG)
    cst[:, CS_MASKB:CS_MASKB + 128] = np.where(k[:, None] >= k[None, :], 0.0, NEG)
    cst[:, CS_IDENT:CS_IDENT + 128] = np.eye(128)
    cst[:, CS_ONES:CS_ONES + 128] = 1.0
    sel = np.zeros((16, 16, 128), np.float32)
    for h in range(16):
        sel[h, h, :] = 1.0
    out = {"cst": cst, "sel": sel.reshape(16, 16 * 128)}
    tneg = np.zeros((128, 18), np.float32)
    for j in range(16):
        tneg[:, j] = -(j * 128 + k) / float(TL)
    for j in range(2):
        tneg[:, 16 + j] = -(j * 128 + k) / float(TC)
    out["tneg"] = tneg
    for L, nm in ((TL, "L"), (TC, "C")):
        t = np.arange(L, dtype=np.float32)
        t_norm = t / np.float32(L)
        bands = np.linspace(1e-4, 15.0, 16, dtype=np.float32)
        ang = (np.float32(2.0 * math.pi / L) * t[:, None] * bands[None, :]).astype(np.float32)
        feats = np.concatenate([t_norm[:, None], np.cos(ang), -np.sin(ang)], axis=-1).astype(np.float32)
        out["feats" + nm] = np.ascontiguousarray(feats.T)
        wp = _wp(L)
        nw, nt = wp // 128, L // 128
        sidx = np.arange(L, dtype=np.float64)
        widx = np.arange(wp, dtype=np.float64)
        ph = np.pi * np.outer(sidx, widx) / L
        valid = (widx <= L)[None, :]
        ctm = np.where(valid, np.cos(ph), 0.0)
        stm = np.where(valid, np.sin(ph), 0.0)
        cw = np.where((widx == 0) | (widx == L), 1.0, 2.0) * (widx <= L) / (2.0 * L)
        cim = (cw[:, None] * np.cos(ph.T))
        sim = -(cw[:, None] * np.sin(ph.T))
        def tile_fwd(m):
            return _bf16(m.reshape(nt, 128, nw, 128).transpose(2, 1, 0, 3).reshape(nw, 128, nt * 128))

        def tile_inv(m):
            return _bf16(m.reshape(nw, 128, nt, 128).transpose(2, 1, 0, 3).reshape(nt, 128, nw * 128))
        out["ct" + nm] = tile_fwd(ctm)
        out["st" + nm] = tile_fwd(stm)
        out["ci" + nm] = tile_inv(cim)
        out["si" + nm] = tile_inv(sim)
    return out


def _fm(v, n):
    return np.ascontiguousarray(np.asarray(v, np.float32).reshape(n, 128).T)


def _prep_shared(inp):
    f = lambda n: np.asarray(inp[n], np.float32)
    pp = np.zeros((128, PP_N), np.float32)
    for l in range(DEPTH):
        pp[:, PP_N1W + 8 * l:PP_N1W + 8 * l + 8] = _fm(f("norm1_w")[l], 8)
        pp[:, PP_N2W + 8 * l:PP_N2W + 8 * l + 8] = _fm(f("norm2_w")[l], 8)
        pp[:, PP_MODB + 48 * l:PP_MODB + 48 * l + 48] = _fm(f("mod_b")[l], 48)
        pp[:, PP_SCB + 12 * l:PP_SCB + 12 * l + 12] = _fm(f("ssd_conv_b")[l], 12)
        cw = f("ssd_conv_w")[l]
        pp[:, PP_SCW + 60 * l:PP_SCW + 60 * l + 60] = cw.reshape(5, 12, 128).transpose(2, 1, 0).reshape(128, 60)
        pp[:, PP_MCB + 16 * l:PP_MCB + 16 * l + 16] = _fm(f("ml_conv_b")[l], 16)
        cw = f("ml_conv_w")[l]
        pp[:, PP_MCW + 80 * l:PP_MCW + 80 * l + 80] = cw.reshape(5, 16, 128).transpose(2, 1, 0).reshape(128, 80)
        pp[:, PP_SNW + 8 * l:PP_SNW + 8 * l + 8] = _fm(f("ssd_norm_w")[l], 8)
        pp[:, PP_MNW + 8 * l:PP_MNW + 8 * l + 8] = _fm(f("ml_norm_w")[l], 8)
        pp[0:64, PP_HB1 + l] = f("hy_ffn_b1")[l]
        pp[0:64, PP_HB2 + l] = f("hy_ffn_b2")[l]
    pp[:, PP_NF:PP_NF + 8] = _fm(f("norm_f_w"), 8)
    rowp = np.zeros((DEPTH, RP_N), np.float32)
    for l in range(DEPTH):
        rowp[l, RP_DTB:RP_DTB + 32] = f("ssd_dt_bias")[l].reshape(-1)
        rowp[l, RP_ALOG:RP_ALOG + 32] = f("ssd_a_log")[l].reshape(-1)
        rowp[l, RP_DSK:RP_DSK + 1024] = np.repeat(f("ssd_d")[l], 64)
        rowp[l, RP_MGB:RP_MGB + 32] = f("ml_gate_b")[l].reshape(-1)
        rowp[l, RP_HCB:RP_HCB + 3072] = f("hy_conv_b")[l]
        rowp[l, RP_HCW:RP_HCW + 9216] = f("hy_conv_w")[l].reshape(-1)
        rowp[l, RP_HDEC:RP_HDEC + 4096] = f("hy_decay")[l].reshape(-1)
        rowp[l, RP_HSKIP:RP_HSKIP + 2048] = f("hy_skip")[l].reshape(-1)
    sh = {"mod_w": f("mod_w"), "w_in": f("w_in"), "w_branch": f("w_branch"), "w_out": f("w_out"),
          "mlp_w1": f("mlp_w1"), "mlp_w2": f("mlp_w2"), "pp": pp, "rowp": rowp,
          "hyw1": f("hy_ffn_w1"), "hyw2": f("hy_ffn_w2"), "hyw3": f("hy_ffn_w3")}
    sh.update(_consts())
    return sh


def _prep_core(inp, b):
    x = np.asarray(inp["x"], np.float32)[b]
    ctx = np.asarray(inp["ctx"], np.float32)[b]
    xT = np.ascontiguousarray(np.concatenate([ctx.T, x.T], axis=1))
    cvec = np.zeros((128, 8, 2), np.float32)
    cvec[:, :, 0] = _fm(np.asarray(inp["c"], np.float32)[b], 8)
    cvec[:, :, 1] = _fm(np.asarray(inp["c_ctx"], np.float32), 8)
    return {"xT": xT, "cvec": cvec.reshape(128, 16)}


_CACHE = {}


def kernel(**inputs):
    if "nc" not in _CACHE:
        _CACHE["nc"] = Builder().build()
    nc = _CACHE["nc"]
    shared = _prep_shared(inputs)
    in_maps = []
    for b in range(8):
        m = dict(shared)
        m.update(_prep_core(inputs, b))
        in_maps.append(m)
    res = run_bass_kernel_spmd(nc, in_maps, core_ids=list(range(8)))
    out = np.stack([np.ascontiguousarray(res.results[b]["out"].T) for b in range(8)], axis=0)
    return out.astype(np.float32)
```

```python
import contextlib
import math
import numpy as np
import concourse.bass as bass
import concourse.mybir as mybir
from concourse.bass_utils import run_bass_kernel_spmd

F32 = mybir.dt.float32
BF16 = mybir.dt.bfloat16
AF = mybir.ActivationFunctionType
ALU = mybir.AluOpType

D = 1024
TC = 256
TL = 2048
T = TC + TL
NCH = T // 128
DEPTH = 2
EPS = 1e-6
CT0 = 1
LT0 = 259
TP = 2308
SSD_COLS = 2592
ML0 = 2592
REC_COLS = 6720
HY0 = 6720
G0 = 9792
IN_COLS = 12864
NEG = -30000.0

EPOCH = 30000
SYNC_SAME = ("act", "dve", "pool")


class Buf:
    __slots__ = ("name", "w", "r", "excl")

    def __init__(self, name="", excl=False):
        self.name = name
        self.w = None
        self.r = []
        self.excl = excl


class Prog:
    ENG = ("pe", "act", "dve", "pool", "sp")

    def __init__(self, nc, st, n_dma_sems=56):
        self.nc = nc
        self.st = st
        self.streams = {e: [] for e in self.ENG}
        self.cnt = {e: 0 for e in self.ENG}
        self.cur_sem = {}
        self.seen = {e: {} for e in self.ENG}
        for e in ("pe", "act", "dve", "pool"):
            self.cur_sem[e] = self._new_sem()
        self.dma_sems = [self._new_sem() for _ in range(n_dma_sems)]
        self.dma_val = [0] * n_dma_sems
        self.dma_rr = 0
        self.n_sw = 12
        self.dma_rr_sw = 0
        self.qrr = 0
        self.nops = 0

    def _new_sem(self):
        self.nsem = getattr(self, "nsem", 0) + 1
        return self.st.enter_context(self.nc.semaphore("sem%d" % self.nsem))

    def _need(self, eng, tok, waits, raw=True):
        if tok is None:
            return
        sem, val, src = tok
        if src == eng and (eng not in SYNC_SAME or not raw):
            return
        k = id(sem)
        if self.seen[eng].get(k, 0) >= val:
            return
        self.seen[eng][k] = val
        waits.append((sem, val))

    def _deps(self, eng, reads, writes):
        waits = []
        for b in reads:
            self._need(eng, b.w, waits)
        for b in writes:
            self._need(eng, b.w, waits)
            for t in b.r:
                self._need(eng, t, waits)
        return waits

    def _commit(self, tok, reads, writes):
        for b in reads:
            b.r.append(tok)
            if len(b.r) > 48:
                last = {}
                for t in b.r:
                    last[(id(t[0]))] = t if (id(t[0]) not in last or last[id(t[0])][1] < t[1]) else last[id(t[0])]
                b.r = list(last.values())
        for b in writes:
            b.w = tok
            b.r = []

    def op(self, eng, fn, reads=(), writes=()):
        writes = list(writes) + [b for b in reads if b.excl and b not in writes]
        reads = [b for b in reads if not b.excl]
        waits = self._deps(eng, reads, writes)
        if self.cnt[eng] >= EPOCH:
            self.cur_sem[eng] = self._new_sem()
            self.cnt[eng] = 0
        self.cnt[eng] += 1
        sem = self.cur_sem[eng]
        tok = (sem, self.cnt[eng], eng)
        self.streams[eng].append((waits, fn, (sem, 1)))
        self._commit(tok, reads, writes)
        self.nops += 1
        return tok

    def dma(self, q, fn, reads=(), writes=()):
        if q is None:
            q = "sp"
        waits = self._deps(q, reads, writes)
        if q == "pool":
            i = self.dma_rr_sw
            self.dma_rr_sw = (self.dma_rr_sw + 1) % self.n_sw
        else:
            i = self.n_sw + self.dma_rr
            self.dma_rr = (self.dma_rr + 1) % (len(self.dma_sems) - self.n_sw)
        sem = self.dma_sems[i]
        if self.dma_val[i] > 0:
            self._need(q, (sem, self.dma_val[i], "dma"), waits)
        self.dma_val[i] += 16
        tok = (sem, self.dma_val[i], "dma")
        self.streams[q].append((waits, fn, (sem, 16)))
        self._commit(tok, reads, writes)
        self.nops += 1
        return tok

    def finish(self, eng="sp"):
        waits = []
        for e in ("pe", "act", "dve", "pool"):
            if self.cnt[e] > 0:
                self._need(eng, (self.cur_sem[e], self.cnt[e], e), waits)
        for i, s in enumerate(self.dma_sems):
            if self.dma_val[i] > 0:
                self._need(eng, (s, self.dma_val[i], "dma"), waits)
        self.streams[eng].append((waits, None, None))

    def emit(self):
        nc = self.nc
        engmap = {"pe": "tensor", "act": "scalar", "dve": "vector", "pool": "gpsimd", "sp": "sync"}
        with nc.Block() as block:
            for e in self.ENG:
                stream = self.streams[e]
                if not stream:
                    continue

                def body(eng, stream=stream):
                    for waits, fn, inc in stream:
                        for (s, v) in waits:
                            eng.wait_ge(s, v)
                        if fn is not None:
                            ins = fn(eng)
                            if inc is not None:
                                ins.then_inc(inc[0], inc[1])
                getattr(block, engmap[e])(body)


_FREED = {}
_SCOPES = []


def newbuf(name="", excl=False):
    b = Buf(name, excl)
    b.r = list(_FREED.values())
    if _SCOPES:
        _SCOPES[-1].append(b)
    return b


@contextlib.contextmanager
def scope():
    bufs = []
    _SCOPES.append(bufs)
    with contextlib.ExitStack() as lst:
        yield lst
    _SCOPES.pop()
    for b in bufs:
        for t in ([b.w] if b.w is not None else []) + list(b.r):
            k = id(t[0])
            if k not in _FREED or _FREED[k][1] < t[1]:
                _FREED[k] = t


class Rot:
    def __init__(self, st, alloc, name, shape, dtype, n, excl=False):
        self.tiles = [st.enter_context(alloc("%s%d" % (name, i), shape, dtype)) for i in range(n)]
        self.bufs = [newbuf("%s%d" % (name, i), excl) for i in range(n)]
        self.i = 0

    def next(self):
        i = self.i
        self.i = (self.i + 1) % len(self.tiles)
        return self.tiles[i], self.bufs[i]


PP_N1W, PP_N2W, PP_NF, PP_MODB, PP_SCB, PP_SCW, PP_MCB, PP_MCW, PP_SNW, PP_MNW, PP_HB1, PP_HB2, PP_N = (
    0, 16, 32, 40, 136, 160, 280, 312, 472, 488, 504, 506, 512)
RP_DTB, RP_ALOG, RP_DSK, RP_MGB, RP_HCB, RP_HCW, RP_HDEC, RP_HSKIP, RP_N = (
    0, 32, 64, 1088, 1120, 4192, 13408, 17504, 19552)
CS_TRIU, CS_TRIL, CS_MASKF, CS_MASKB, CS_IDENT, CS_ONES, CS_N = 0, 128, 256, 384, 512, 640, 768


def _wp(L):
    return ((L + 1 + 127) // 128) * 128


class Builder:
    def __init__(self, debug=None, nlayers=DEPTH):
        self.debug = debug or {}
        self.nlayers = nlayers
        self.nc = bass.Bass("TRN2", target_bir_lowering=False)
        self.dbg_out = {}
        _FREED.clear()
        del _SCOPES[:]

    def din(self, name, shape, dt=F32):
        return self.nc.dram_tensor(name, list(shape), dt, kind="ExternalInput").ap()

    def dscr(self, name, shape, dt=F32):
        return self.nc.dram_tensor(name, list(shape), dt, kind="Internal").ap()

    def dout(self, name, shape, dt=F32):
        return self.nc.dram_tensor(name, list(shape), dt, kind="ExternalOutput").ap()

    def mm(self, out, lhsT, rhs, start, stop, reads, writes, skip=False):
        self.P.op("pe", lambda e: e.matmul(out, lhsT=lhsT, rhs=rhs, start=start, stop=stop, skip_group_check=skip),
                  reads, writes)

    def tr(self, out, in_, reads, writes):
        ident = self.identb[:]
        self.P.op("pe", lambda e: e.matmul(out, lhsT=in_, rhs=ident, start=True, stop=True), reads + [self.Bconst], writes)

    def act(self, out, in_, func, reads, writes, **kw):
        self.P.op("act", lambda e: e.activation(out=out, in_=in_, func=func, **kw), reads, writes)

    def tt(self, eng, out, in0, in1, op, reads, writes):
        self.P.op(eng, lambda e: e.tensor_tensor(out=out, in0=in0, in1=in1, op=op), reads, writes)

    def ts(self, eng, out, in0, s1, s2, op0, op1, reads, writes):
        if op1 is None:
            self.P.op(eng, lambda e: e.tensor_scalar(out=out, in0=in0, scalar1=s1, scalar2=None, op0=op0), reads, writes)
        else:
            self.P.op(eng, lambda e: e.tensor_scalar(out=out, in0=in0, scalar1=s1, scalar2=s2, op0=op0, op1=op1),
                      reads, writes)

    def stt(self, out, in0, scalar, in1, op0, op1, reads, writes):
        self.P.op("dve", lambda e: e.scalar_tensor_tensor(out=out, in0=in0, scalar=scalar, in1=in1, op0=op0, op1=op1),
                  reads, writes)

    def cp(self, eng, out, in_, reads, writes):
        self.P.op(eng, lambda e: e.tensor_copy(out=out, in_=in_), reads, writes)

    def recip(self, out, in_, reads, writes):
        self.P.op("dve", lambda e: e.reciprocal(out=out, in_=in_), reads, writes)

    def memset(self, eng, ap, val, writes):
        self.P.op(eng, lambda e: e.memset(ap, val), [], writes)

    def dma(self, q, out, in_, reads, writes):
        self.P.dma(q, lambda e: e.dma_start(out=out, in_=in_), reads, writes)

    def sbt(self, name, shape, dt):
        self.uid = getattr(self, "uid", 0) + 1
        return self.nc.sbuf_tensor("%s_%d" % (name, self.uid), list(shape), dt)

    def sb(self, name, shape, dt):
        return self.st.enter_context(self.sbt("sb_" + name, list(shape), dt))

    def dump(self, name, ap_sb, shape, buf, dt=F32):
        o = self.dout("dbg_" + name, shape, dt)
        self.dbg_out[name] = ("dbg_" + name, tuple(shape))
        self.dma("sp", o, ap_sb, [buf], [Buf()])

    def hcols(self, k, c, perm=False):
        if c < 2:
            return self.hT[:, k, CT0 + 128 * c: CT0 + 128 * c + 128]
        cl = c - 2
        if not perm:
            return self.hT[:, k, LT0 + 128 * cl: LT0 + 128 * cl + 128]
        return self.hTp[:, k, 128 * cl:128 * cl + 128]

    def htile(self, k, ti, perm=False):
        if ti == 0:
            return self.hT[:, k, CT0:CT0 + TC], TC
        q = ti - 1
        if not perm:
            return self.hT[:, k, LT0 + 512 * q: LT0 + 512 * q + 512], 512
        return self.hTp[:, k, 512 * q:512 * q + 512], 512

    def load_w(self, src, n, q="pool"):
        t, b = self.wrot.next()
        self.dma(q, t[:, :, 0:n], src.rearrange("(k p) n -> p k n", p=128), [], [b])
        return t, b

    def build(self):
        nc = self.nc
        with contextlib.ExitStack() as st:
            self.st = st
            self.P = Prog(nc, st)
            self.declare()
            self.setup()
            stop = self.debug.get("stop")
            for l in range(self.nlayers):
                need_ctx = l < DEPTH - 1
                self.stage_mod(l)
                self.stage_norm(l, first=True)
                if stop == "norm%d" % l:
                    self.dump("hT", self.hT[:], [128, 8, TP], self.BhT, BF16)
                    break
                if not self.debug.get("skip_scan"):
                    self.stage_scan(l, "ssd", need_ctx)
                    if stop == "ssd%d" % l:
                        break
                    self.stage_scan(l, "ml", need_ctx)
                    if stop == "ml%d" % l:
                        break
                self.stage_hyena(l, need_ctx)
                if stop == "hy%d" % l:
                    break
                self.stage_merge(l, need_ctx)
                if stop == "merge%d" % l:
                    break
                self.stage_norm(l, first=False, need_ctx=need_ctx)
                self.stage_mlp(l, need_ctx)
                if stop == "mlp%d" % l:
                    break
            else:
                self.stage_final()
            dmp = self.debug.get("dump")
            if dmp:
                srcs = {"ybT0": (self.ybT[0], [D, T], BF16, self.BybT[0]), "ybT1": (self.ybT[1], [D, T], BF16, self.BybT[1]),
                        "ybT2": (self.ybT[2], [D, T], BF16, self.BybT[2]), "xres": (self.xres, [D, T], F32, self.BxresL),
                        "hyK": (self.hyK, [2, _wp(TL), 2048], F32, self.BhyK),
                        "ys_tok": (self.ys_tok, [T, D], F32, self.Bys)}
                src, shp, dt, bf = srcs[dmp]
                o = self.dout("dbg_" + dmp, shp, dt)
                if len(shp) == 3:
                    o = o.rearrange("a r c -> (a r) c")
                    src = src.rearrange("a r c -> (a r) c")
                for r0 in range(0, o.shape[0], 128):
                    self.dma("sp", o[r0:r0 + 128, :], src[r0:r0 + 128, :], bf if isinstance(bf, list) else [bf], [Buf()])
            self.P.finish("sp")
            self.P.emit()
        return nc

    def declare(self):
        s = self
        s.xT = s.din("xT", [D, T])
        s.cvec = s.din("cvec", [128, 16])
        s.mod_w = s.din("mod_w", [DEPTH, D, 6 * D])
        s.w_in = s.din("w_in", [DEPTH, D, IN_COLS])
        s.w_branch = s.din("w_branch", [DEPTH, 3, D, D])
        s.w_out = s.din("w_out", [DEPTH, D, D])
        s.mlp_w1 = s.din("mlp_w1", [DEPTH, D, 4 * D])
        s.mlp_w2 = s.din("mlp_w2", [DEPTH, 4 * D, D])
        s.pp_d = s.din("pp", [128, PP_N])
        s.rowp = s.din("rowp", [DEPTH, RP_N])
        s.hyw1 = s.din("hyw1", [DEPTH, 33, 64])
        s.hyw2 = s.din("hyw2", [DEPTH, 64, 64])
        s.hyw3 = s.din("hyw3", [DEPTH, 64, 4096])
        s.cst_d = s.din("cst", [128, CS_N])
        s.sel_d = s.din("sel", [16, 16 * 128])
        s.feats = {TL: s.din("featsL", [33, TL]), TC: s.din("featsC", [33, TC])}
        s.tneg_d = s.din("tneg", [128, 18])
        s.dft = {}
        for L, nm in ((TL, "L"), (TC, "C")):
            nw = _wp(L) // 128
            nt = L // 128
            s.dft[L] = dict(
                ct=s.din("ct" + nm, [nw, 128, nt * 128], BF16), st=s.din("st" + nm, [nw, 128, nt * 128], BF16),
                ci=s.din("ci" + nm, [nt, 128, nw * 128], BF16), si=s.din("si" + nm, [nt, 128, nw * 128], BF16))
        s.out = s.dout("out", [D, TL])
        s.xres = s.dscr("xres", [D, T])
        s.ys_tok = s.dscr("ys_tok", [T, D])
        s.ybT = [s.dscr("ybT%d" % n, [D, T], BF16) for n in range(3)]
        s.hsd = s.dscr("hsd", [2, TL, 2048], BF16)
        s.hyK = s.dscr("hyK", [2, _wp(TL), 2048])
        s.BxresL = [Buf("xres%d" % k) for k in range(8)]
        s.Bys = Buf("ys_tok")
        s.BybT = [Buf("ybT%d" % n) for n in range(3)]
        s.Bhsd = Buf("hsd")
        s.BhyK = Buf("hyK")

    def setup(self):
        s = self
        nc, st = s.nc, s.st
        s.ps = Rot(st, nc.psum_tensor, "ps", [128, 512], F32, 4, excl=True)
        s.pacc = Rot(st, nc.psum_tensor, "pacc", [128, 512], F32, 4, excl=True)
        s.Bconst = Buf("const")
        s.cst = s.sb("cst", [128, CS_N], F32)
        s.pp = s.sb("pp", [128, PP_N], F32)
        s.identb = s.sb("identb", [128, 128], BF16)
        s.tneg = s.sb("tneg", [128, 18], F32)
        s.csil = s.sb("csil", [128, 16], F32)
        s.mod = s.sb("mod", [128, 96], F32)
        s.gv = s.sb("gv", [128, 64], F32)
        s.Bmod = Buf("mod")
        s.hT = s.sb("hT", [128, 8, TP], BF16)
        s.BhT = Buf("hT")
        s.wrot = Rot(st, nc.sbuf_tensor, "wr", [128, 8, 512], BF16, 3)
        c = [s.Bconst]
        s.dma("sp", s.cst[:], s.cst_d, [], c)
        s.dma("sp", s.pp[:], s.pp_d, [], c)
        s.dma("sp", s.tneg[:], s.tneg_d, [], c)
        s.dma("sp", s.csil[:], s.cvec, [], c)
        s.dma("pool", s.identb[:], s.cst_d[:, CS_IDENT:CS_IDENT + 128], [], c)
        s.selb = s.sb("selb", [16, 16 * 128], BF16)
        s.dma("pool", s.selb[:], s.sel_d, [], c)
        s.trib = s.sb("trib", [128, 256], BF16)
        s.dma("pool", s.trib[:], s.cst_d[:, CS_TRIU:CS_TRIU + 256], [], c)
        s.maskb = s.sb("maskb", [128, 256], BF16)
        s.dma("pool", s.maskb[:], s.cst_d[:, CS_MASKF:CS_MASKF + 256], [], c)
        s.act(s.csil[:], s.csil[:], AF.Silu, c, c)
        s.memset("dve", s.hT[:], 0.0, [s.BhT])
        for k in range(8):
            s.dma(None, s.xres[k * 128:(k + 1) * 128, :], s.xT[k * 128:(k + 1) * 128, :], [], [s.BxresL[k]])

    def triu(self):
        return self.cst[:, CS_TRIU:CS_TRIU + 128]

    def tril(self):
        return self.cst[:, CS_TRIL:CS_TRIL + 128]

    def ones(self):
        return self.cst[:, CS_ONES:CS_ONES + 128]

    def identf(self):
        return self.cst[:, CS_IDENT:CS_IDENT + 128]

    def stage_mod(self, l):
        s = self
        with scope() as lst:
            wm = [lst.enter_context(s.sbt("wm%d" % i, [128, 8, 512], F32)) for i in range(2)]
            wmb = [newbuf(), newbuf()]
            pt, pb = s.ps.next()
            src = s.mod_w[l].rearrange("(k p) n -> p k n", p=128)
            csv = s.csil[:, :].rearrange("p (k w) -> p k w", w=2)
            for blk in range(12):
                t, b = wm[blk % 2], wmb[blk % 2]
                s.dma(None, t[:], src[:, :, blk * 512:(blk + 1) * 512], [], [b])
                for jt in range(4):
                    j = blk * 4 + jt
                    for k in range(8):
                        s.mm(pt[:, 2 * j:2 * j + 2], t[:, k, jt * 128:(jt + 1) * 128], csv[:, k, :],
                             k == 0, k == 7, [b, s.Bconst], [pb])
            modv = s.mod[:, :].rearrange("p (j w) -> p j w", w=2)
            mb = s.pp[:, PP_MODB + l * 48:PP_MODB + l * 48 + 48].unsqueeze(2).broadcast_to([128, 48, 2])
            s.tt("dve", modv, pt[:, 0:96].rearrange("p (j w) -> p j w", w=2), mb, ALU.add, [pb, s.Bconst], [s.Bmod])
            for i, (m, npp) in enumerate(((1, PP_N1W), (4, PP_N2W))):
                gsl = s.gv[:, 16 * i:16 * i + 16].rearrange("p (j w) -> p j w", w=2)
                msl = s.mod[:, m * 16:(m + 1) * 16].rearrange("p (j w) -> p j w", w=2)
                nw = s.pp[:, npp + l * 8:npp + l * 8 + 8].unsqueeze(2).broadcast_to([128, 8, 2])
                s.stt(gsl, msl, 1.0, nw, ALU.add, ALU.mult, [s.Bmod, s.Bconst], [s.Bmod])

    def modcol(self, m, j, which):
        c = (m * 8 + j) * 2 + which
        return self.mod[:, c:c + 1]

    def gcol(self, i, j, which):
        c = 16 * i + j * 2 + which
        return self.gv[:, c:c + 1]

    def stage_norm(self, l, first, need_ctx=True):
        s = self
        gi = 0 if first else 1
        mshift = 0 if first else 3
        with scope() as lst:
            xt = Rot(lst, s.sbt, "nx", [128, 8, 512], F32, 2)
            sq = Rot(lst, s.sbt, "nsq", [128, 8, 512], F32, 1)
            rs = Rot(lst, s.sbt, "nrs", [128, 512], F32, 2)
            tm = Rot(lst, s.sbt, "ntm", [128, 512], F32, 2)
            for ti in range(5):
                if ti == 0 and not need_ctx:
                    continue
                n = TC if ti == 0 else 512
                t0 = 0 if ti == 0 else TC + 512 * (ti - 1)
                which = 1 if ti == 0 else 0
                x, xb = xt.next()
                s.dma(None, x[:, :, 0:n], s.xres[:, t0:t0 + n].rearrange("(k p) t -> p k t", p=128), list(s.BxresL), [xb])
                q, qb = sq.next()
                s.act(q[:, :, 0:n], x[:, :, 0:n], AF.Square, [xb], [qb])
                pt, pb = s.ps.next()
                for k in range(8):
                    s.mm(pt[:, 0:n], s.ones(), q[:, k, 0:n], k == 0, k == 7, [qb, s.Bconst], [pb])
                r, rb = rs.next()
                s.act(r[:, 0:n], pt[:, 0:n], AF.Sqrt, [pb], [rb], scale=1.0 / D, bias=EPS)
                s.recip(r[:, 0:n], r[:, 0:n], [rb], [rb])
                for k in range(8):
                    t, tb = tm.next()
                    s.stt(t[:, 0:n], x[:, k, 0:n], s.gcol(gi, k, which), r[:, 0:n], ALU.mult, ALU.mult,
                          [xb, rb, s.Bmod], [tb])
                    dst, _ = s.htile(k, ti)
                    s.act(dst, t[:, 0:n], AF.Identity, [tb, s.Bmod], [s.BhT], bias=s.modcol(mshift, k, which))

    def stage_gates_prep(self, l):
        pass

    def conv_tile(self, l, col0, cwcol, cbcol, perm, cin, Bcin, acc, Bacc, dst, Bdst):
        s = self
        w, wb = s.load_w(s.w_in[l][:, col0:col0 + 128], 128)
        for ti in range(5):
            pt, pb = s.ps.next()
            n = TC if ti == 0 else 512
            for k in range(8):
                rhs, _ = s.htile(k, ti, perm)
                out = pt[:, 0:n]
                s.mm(out, w[:, k, 0:128], rhs, k == 0, k == 7, [wb, s.BhT], [pb])
            off = 2 if ti == 0 else 262 + 512 * (ti - 1)
            s.act(cin[:, off:off + n], pt[:, 0:n], AF.Copy, [pb], [Bcin])
        NV = 2308
        if s.debug.get("conv_stop") == "mm":
            return
        s.ts("dve", acc[:, 0:NV], cin[:, 0:NV], s.pp[:, cwcol:cwcol + 1], None, ALU.mult, None, [Bcin, s.Bconst], [Bacc])
        for k in range(1, 5):
            s.stt(acc[:, 0:NV], cin[:, k:k + NV], s.pp[:, cwcol + k:cwcol + k + 1], acc[:, 0:NV], ALU.mult, ALU.add,
                  [Bcin, Bacc, s.Bconst], [Bacc])
        if s.debug.get("conv_stop") == "dve":
            return
        s.act(dst[:, 0:TC], acc[:, 0:TC], AF.Silu, [Bacc, s.Bconst], [Bdst], bias=s.pp[:, cbcol:cbcol + 1])
        s.act(dst[:, TC:T], acc[:, 260:260 + TL], AF.Silu, [Bacc, s.Bconst], [Bdst], bias=s.pp[:, cbcol:cbcol + 1])

    def stage_scan(self, l, kind, need_ctx):
        s = self
        ssd = kind == "ssd"
        nhd = 16 if ssd else 8
        NG = 2 * nhd
        W = 512 if ssd else 129
        VW = 512 if ssd else 130
        nunits = 2 if ssd else 8
        sc = 1.0 if ssd else 128.0 ** -0.5
        first_c = 0 if need_ctx else 2
        with scope() as lst:
            def A(name, shape, dt, stack=None):
                return (stack or lst).enter_context(s.sbt("sc_" + kind + name, list(shape), dt))
            if not ssd:
                s.hTp = A("hTp", [128, 8, TL], BF16)
                for k in range(8):
                    src = s.hT[:, k, LT0:LT0 + TL].rearrange("p (r w) -> p w r", w=64)
                    dst = s.hTp[:, k, :].rearrange("p (w r) -> p w r", r=32)
                    if k % 2 == 0:
                        s.act(dst, src, AF.Copy, [s.BhT], [s.BhT])
                    else:
                        s.P.op("dve", (lambda o, i_: (lambda e: e.tensor_copy(out=o, in_=i_)))(dst, src), [s.BhT], [s.BhT])
            la = A("la", [128, NCH, NG], F32)
            wl = A("wl", [128, NCH, NG], F32)
            cum = A("cum", [128, NCH, NG], F32)
            tot = A("tot", [128, NCH, NG], F32)
            bias = A("bias", [128, NCH, NG], F32)
            ecum = A("ecum", [128, NCH, NG], F32)
            gst = A("gst", [128, NCH, NG], F32)
            atot = A("atot", [128, NCH, NG], F32)
            Bd = newbuf("decay")
            with scope() as pl_:
                tmpa = A("tmpa", [128, NCH, 32], F32, pl_)
                tmpb = A("tmpb", [128, NCH, 32], F32, pl_)
                rowb = A("rowb", [128, 64], F32, pl_)
                Bt = newbuf("dtmp")
                gcol0 = 2560 if ssd else ML0 + 4096
                wg, wgb = s.load_w(s.w_in[l][:, gcol0:gcol0 + 32], 32)
                if ssd:
                    s.dma("sp", rowb[:, 0:64], s.rowp[l:l + 1, RP_DTB:RP_DTB + 64].broadcast_to([128, 64]), [], [Bt])
                    s.act(rowb[:, 32:64], rowb[:, 32:64], AF.Exp, [Bt], [Bt])
                else:
                    s.dma("sp", rowb[:, 0:32], s.rowp[l:l + 1, RP_MGB:RP_MGB + 32].broadcast_to([128, 32]), [], [Bt])
                for half in range(2):
                    pt, pb = s.ps.next()
                    for ci in range(9):
                        c = half * 9 + ci
                        for k in range(8):
                            s.mm(pt[:, ci * 32:ci * 32 + 32], s.hcols(k, c, perm=not ssd), wg[:, k, 0:32], k == 0, k == 7,
                                 [wgb, s.BhT], [pb])
                    s.tt("dve", tmpa[:, half * 9:half * 9 + 9, :], pt[:, 0:288].rearrange("p (c g) -> p c g", g=32),
                         rowb[:, 0:32].unsqueeze(1).broadcast_to([128, 9, 32]), ALU.add, [pb, Bt], [Bt])
                if ssd:
                    s.act(tmpb[:], tmpa[:], AF.Exp, [Bt], [Bt])
                    s.act(tmpb[:], tmpb[:], AF.Ln, [Bt], [Bt], bias=1.0)
                    s.act(wl[:], tmpb[:], AF.Ln, [Bt], [Bd])
                    s.stt(la[:], tmpb[:], -1.0, rowb[:, 32:64].unsqueeze(1).broadcast_to([128, NCH, 32]), ALU.mult, ALU.mult,
                          [Bt], [Bd])
                else:
                    for d in range(2):
                        s.act(wl[:, :, d * 8:d * 8 + 8], tmpa[:, :, d * 16:d * 16 + 8], AF.Copy, [Bt], [Bd])
                        s.act(tmpb[:, :, d * 8:d * 8 + 8], tmpa[:, :, d * 16 + 8:d * 16 + 16], AF.Exp, [Bt], [Bt], scale=-1.0)
                    s.act(tmpb[:, :, 0:16], tmpb[:, :, 0:16], AF.Ln, [Bt], [Bt], bias=1.0)
                    s.ts("dve", la[:], tmpb[:, :, 0:16], -1.0, None, ALU.mult, None, [Bt], [Bd])
            for half in range(2):
                pc, pcb = s.ps.next()
                ptt, ptb = s.ps.next()
                for ci in range(9):
                    c = half * 9 + ci
                    o = ci * NG
                    s.mm(pc[:, o:o + nhd], s.triu(), la[:, c, 0:nhd], True, True, [Bd, s.Bconst], [pcb])
                    s.mm(pc[:, o + nhd:o + NG], s.tril(), la[:, c, nhd:NG], True, True, [Bd, s.Bconst], [pcb])
                    s.mm(ptt[:, o:o + NG], s.ones(), la[:, c, :], True, True, [Bd, s.Bconst], [ptb])
                s.act(cum[:, half * 9:half * 9 + 9, :], pc[:, 0:9 * NG].rearrange("p (c g) -> p c g", g=NG), AF.Copy,
                      [pcb], [Bd])
                s.act(tot[:, half * 9:half * 9 + 9, :], ptt[:, 0:9 * NG].rearrange("p (c g) -> p c g", g=NG), AF.Copy,
                      [ptb], [Bd])
            s.tt("dve", bias[:], wl[:], cum[:], ALU.subtract, [Bd], [Bd])
            s.act(ecum[:], cum[:], AF.Exp, [Bd], [Bd])
            s.tt("dve", gst[:], bias[:], tot[:], ALU.add, [Bd], [Bd])
            s.act(gst[:], gst[:], AF.Exp, [Bd], [Bd])
            s.act(atot[:], tot[:], AF.Exp, [Bd], [Bd])
            la_hi = A("lahi", [128, NCH, NG], BF16)
            la_lo = A("lalo", [128, NCH, NG], BF16)
            s.cp("dve", la_hi[:], la[:], [Bd], [Bd])
            s.tt("dve", la_lo[:], la[:], la_hi[:], ALU.subtract, [Bd], [Bd])
            if s.debug.get("dump_decay") == kind:
                s.dump("la", la[:], [128, NCH, NG], Bd)
                s.dump("cum", cum[:], [128, NCH, NG], Bd)
                s.dump("wl", wl[:], [128, NCH, NG], Bd)
            if s.debug.get("scan_stop") == "decay":
                return
            QT = A("QT", [128, T], BF16)
            KT = A("KT", [128, T], BF16)
            Ktok = A("Ktok", [128, NCH, 128], BF16)
            V = A("V", [128, NCH, VW], BF16)
            Sin = [A("Sin0", [128, NCH, VW], BF16), A("Sin1", [128, NCH, VW], BF16)]
            S = [A("S0", [128, VW], F32), A("S1", [128, VW], F32)]
            BQT, BKT, BKtok, BV = newbuf(), newbuf(), newbuf(), newbuf()
            BSin = [newbuf(), newbuf()]
            BS = [newbuf(), newbuf()]
            ssq = A("ssq", [128, 2 * NCH], F32)
            Bssq = newbuf()
            s.memset("dve", ssq[:], 0.0, [Bssq])
            if ssd:
                dsk = A("dsk", [128, 1024], F32)
                Bdsk = newbuf()
                s.dma("sp", dsk[:], s.rowp[l:l + 1, RP_DSK:RP_DSK + 1024].broadcast_to([128, 1024]), [], [Bdsk])
            else:
                s.memset("dve", V[:], 0.0, [BV])
                s.memset("dve", V[:, :, 128:129], 1.0, [BV])

            for u in range(nunits):
                with scope() as cl_:
                    cin = A("cin", [128, 2312], F32, cl_)
                    acc = A("acc", [128, 2312], F32, cl_)
                    Bcin, Bacc = newbuf(), newbuf()
                    s.memset("dve", cin[:], 0.0, [Bcin])
                    if ssd:
                        g = u
                        cout = Rot(cl_, s.sbt, "sc_cout", [128, T], BF16, 1)
                        for i in range(4):
                            co, cob = cout.next()
                            idx = 4 * g + i
                            s.conv_tile(l, 512 * g + 128 * i, PP_SCW + 60 * l + 5 * idx, PP_SCB + 12 * l + idx, False,
                                        cin, Bcin, acc, Bacc, co, cob)
                            if s.debug.get("conv_stop") in ("mm", "dve", "silu"):
                                return
                            for c0 in range(0, NCH, 4):
                                ncs = min(4, NCH - c0)
                                po, pob = s.ps.next()
                                for ci in range(ncs):
                                    c = c0 + ci
                                    s.tr(po[:, ci * 128:ci * 128 + 128], co[:, c * 128:c * 128 + 128], [cob], [pob])
                                s.act(V[:, c0:c0 + ncs, i * 128:i * 128 + 128],
                                      po[:, 0:ncs * 128].rearrange("p (c e) -> p c e", e=128), AF.Copy, [pob], [BV])
                            if s.debug.get("conv_stop") == "tr1":
                                return
                        if s.debug.get("conv_stop") == "x":
                            return
                        s.conv_tile(l, 1024 + 128 * g, PP_SCW + 60 * l + 5 * (8 + g), PP_SCB + 12 * l + 8 + g, False,
                                    cin, Bcin, acc, Bacc, KT, BKT)
                        s.conv_tile(l, 1280 + 128 * g, PP_SCW + 60 * l + 5 * (10 + g), PP_SCB + 12 * l + 10 + g, False,
                                    cin, Bcin, acc, Bacc, QT, BQT)
                    else:
                        hd = u
                        s.conv_tile(l, ML0 + 128 * hd, PP_MCW + 80 * l + 5 * hd, PP_MCB + 16 * l + hd, True,
                                    cin, Bcin, acc, Bacc, QT, BQT)
                        s.conv_tile(l, ML0 + 1024 + 128 * hd, PP_MCW + 80 * l + 5 * (8 + hd), PP_MCB + 16 * l + 8 + hd, True,
                                    cin, Bcin, acc, Bacc, KT, BKT)
                if s.debug.get("conv_stop") == "bc":
                    return
                if ssd:
                    wz, wzb = s.load_w(s.w_in[l][:, 1536 + 512 * u:1536 + 512 * u + 512], 512)
                else:
                    hd = u
                    wv, wvb = s.load_w(s.w_in[l][:, ML0 + 2048 + 128 * hd:ML0 + 2048 + 128 * hd + 128], 128)
                    for c in range(NCH):
                        pt, pb = s.ps.next()
                        for k in range(8):
                            s.mm(pt[:, 0:128], s.hcols(k, c, True), wv[:, k, 0:128], k == 0, k == 7, [wvb, s.BhT], [pb])
                        s.act(V[:, c, 0:128], pt[:, 0:128], AF.Copy, [pb], [BV])
                    wz, wzb = s.load_w(s.w_in[l][:, ML0 + 3072 + 128 * hd:ML0 + 3072 + 128 * hd + 128], 128)
                for c0 in range(0, NCH, 4):
                    ncs = min(4, NCH - c0)
                    po, pob = s.ps.next()
                    for ci in range(ncs):
                        c = c0 + ci
                        s.tr(po[:, ci * 128:ci * 128 + 128], KT[:, c * 128:c * 128 + 128], [BKT], [pob])
                    s.act(Ktok[:, c0:c0 + ncs, :], po[:, 0:ncs * 128].rearrange("p (c e) -> p c e", e=128), AF.Copy,
                          [pob], [BKtok], scale=sc)

                if s.debug.get("scan_stop") == "conv":
                    return

                def gsl(arr, c, d):
                    if ssd:
                        return arr[:, c, d * 16 + 8 * u:d * 16 + 8 * u + 8].unsqueeze(2).broadcast_to([128, 8, 64])
                    return arr[:, c, d * 8 + u:d * 8 + u + 1]

                with scope() as ul_:
                    ScT = Rot(ul_, s.sbt, "sc_ScT" + kind, [128, 128], BF16, 3)
                    DT = Rot(ul_, s.sbt, "sc_DT" + kind, [128, 4, 128], BF16, 2)
                    PT = Rot(ul_, s.sbt, "sc_PT" + kind, [128, 4, 128], BF16, 8)
                    Vw = Rot(ul_, s.sbt, "sc_Vw" + kind, [128, VW], BF16, 3)
                    zs = Rot(ul_, s.sbt, "sc_zs" + kind, [128, W if ssd else 128], F32, 2 if ssd else 3)
                    t1 = Rot(ul_, s.sbt, "sc_t1" + kind, [128, VW], F32, 2)
                    t2 = Rot(ul_, s.sbt, "sc_t2" + kind, [128, VW], F32, 2)
                    t3 = Rot(ul_, s.sbt, "sc_t3" + kind, [128, VW], F32, 2)
                    yo = Rot(ul_, s.sbt, "sc_yo" + kind, [128, VW], F32 if ssd else BF16, 2 if ssd else 3)
                    sm = Rot(ul_, s.sbt, "sc_sm" + kind, [128, 8], F32, 2)
                    ymc = Rot(ul_, s.sbt, "sc_ymc" + kind, [128, 128], BF16, 2)
                    cTr = Rot(ul_, s.sbt, "sc_cT" + kind, [16, 128], BF16, 4)
                    cTlr = Rot(ul_, s.sbt, "sc_cTl" + kind, [16, 128], BF16, 4)
                    orders = [list(range(NCH)), [1, 0] + list(range(NCH - 1, 1, -1))]
                    for d in range(2):
                        s.memset("dve", S[d][:], 0.0, [BS[d]])
                    for oi in range(NCH):
                        for d in range(2):
                            c = orders[d][oi]
                            s.P.op("dve", (lambda o, i_: (lambda e: e.tensor_copy(out=o, in_=i_)))(Sin[d][:, c, :], S[d][:]),
                                   [BS[d]], [BSin[d]])
                            if oi == NCH - 1:
                                continue
                            vw, vwb = Vw.next()
                            if ssd:
                                s.tt("pool", vw[:, :].rearrange("p (h e) -> p h e", e=64),
                                     V[:, c, :].rearrange("p (h e) -> p h e", e=64), gsl(gst, c, d), ALU.mult,
                                     [BV, Bd], [vwb])
                            else:
                                s.act(vw[:, :], V[:, c, :], AF.Identity, [BV, Bd], [vwb], scale=gsl(gst, c, d))
                            pl, plb = s.ps.next()
                            s.mm(pl[:, 0:W], Ktok[:, c, :], vw[:, 0:W], True, True, [BKtok, vwb], [plb])
                            if ssd:
                                sv = S[d][:, :].rearrange("p (h e) -> p h e", e=64)
                                s.tt("dve", sv, sv, gsl(atot, c, d), ALU.mult, [BS[d], Bd], [BS[d]])
                                s.tt("dve", S[d][:, 0:W], S[d][:, 0:W], pl[:, 0:W], ALU.add, [BS[d], plb], [BS[d]])
                            else:
                                s.stt(S[d][:, 0:W], S[d][:, 0:W], gsl(atot, c, d), pl[:, 0:W], ALU.mult, ALU.add,
                                      [BS[d], plb, Bd], [BS[d]])

                    if s.debug.get("scan_stop") == "pass1":
                        return
                    def front(c):
                        cs = slice(c * 128, c * 128 + 128)
                        pS, pSb = s.ps.next()
                        s.mm(pS[:, 0:128], KT[:, cs], QT[:, cs], True, True, [BKT, BQT], [pSb])
                        sct, sctb = ScT.next()
                        s.ts("dve", sct[:], pS[:, 0:128], sc, None, ALU.mult, None, [pSb], [sctb])
                        pz, pzb = s.ps.next()
                        nz = 512 if ssd else 128
                        for k in range(8):
                            s.mm(pz[:, 0:nz], s.hcols(k, c, perm=not ssd), wz[:, k, 0:nz], k == 0, k == 7,
                                 [wzb, s.BhT], [pzb])
                        z, zb = zs.next()
                        s.act(z[:, 0:nz], pz[:, 0:nz], AF.Silu if ssd else AF.Sigmoid, [pzb], [zb])
                        aset = c % 2
                        pA0, pAb0 = s.pacc.tiles[aset], s.pacc.bufs[aset]
                        pA = [pA0, pA0 if ssd else pA0[:, 256:512]]
                        pts = []
                        for d in range(2):
                            mask = s.maskb[:, 0:128] if d == 0 else s.maskb[:, 128:256]
                            pc_, pcb_ = s.ps.next()
                            trb = s.trib[:, 0:128] if d == 0 else s.trib[:, 128:256]
                            s.mm(pc_[0:nhd, 0:128], la_hi[:, c, d * nhd:(d + 1) * nhd], trb, True, False, [Bd, s.Bconst], [pcb_])
                            s.mm(pc_[0:nhd, 0:128], la_lo[:, c, d * nhd:(d + 1) * nhd], trb, False, True, [Bd, s.Bconst], [pcb_])
                            cT, cTb = cTr.next()
                            cTl, cTlb = cTlr.next()
                            s.cp("dve", cT[0:nhd, :], pc_[0:nhd, 0:128], [pcb_], [cTb])
                            s.tt("dve", cTl[0:nhd, :], pc_[0:nhd, 0:128], cT[0:nhd, :], ALU.subtract, [pcb_, cTb], [cTlb])
                            ngrp = 2 if ssd else 1
                            for hq in range(ngrp):
                                nh4 = 4 if ssd else 1
                                pD, pDb = s.ps.next()
                                dt_, dtb = DT.next()
                                for hh in range(nh4):
                                    hl = (8 * u + hq * 4 + hh) if ssd else u
                                    s.mm(pD[:, hh * 128:hh * 128 + 128], s.selb[0:nhd, hl * 128:hl * 128 + 128],
                                         cT[0:nhd, :], True, False, [cTb, s.Bconst], [pDb])
                                    s.mm(pD[:, hh * 128:hh * 128 + 128], s.selb[0:nhd, hl * 128:hl * 128 + 128],
                                         cTl[0:nhd, :], False, False, [cTlb, s.Bconst], [pDb])
                                    s.mm(pD[:, hh * 128:hh * 128 + 128], s.identb[:], mask, False, True, [s.Bconst], [pDb])
                                    s.act(dt_[:, hh, :], pD[:, hh * 128:hh * 128 + 128], AF.Exp, [pDb, Bd], [dtb],
                                          bias=bias[:, c, d * nhd + hl:d * nhd + hl + 1])
                                p_, ptb_ = PT.next()
                                s.tt("dve", p_[:, 0:nh4, :], dt_[:, 0:nh4, :],
                                     sct[:, :].unsqueeze(1).broadcast_to([128, nh4, 128]), ALU.mult, [dtb, sctb], [ptb_])
                                pts.append((p_, ptb_, d, hq, nh4))
                        return (z, zb, pA, pAb0, pts)

                    def frontB(c, fr):
                        z, zb, pA, pAb0, pts = fr
                        for (p_, ptb_, d, hq, nh4) in pts:
                            for hh in range(nh4):
                                if ssd:
                                    h = hq * 4 + hh
                                    s.mm(pA[0][:, h * 64:h * 64 + 64], p_[:, hh, :], V[:, c, h * 64:h * 64 + 64],
                                         d == 0 and h == 0, d == 1, [ptb_, BV], [pAb0], skip=True)
                                else:
                                    s.mm(pA[d][:, 0:W], p_[:, 0, :], V[:, c, 0:W], True, True, [ptb_, BV], [pAb0])

                    def back(c, fr):
                        z, zb, pA, pAb0, pts = fr
                        cs = slice(c * 128, c * 128 + 128)
                        pAb = [pAb0, pAb0]
                        if ssd:
                            pB = [s.pacc.tiles[2], s.pacc.tiles[3]]
                            pBb = [s.pacc.bufs[2], s.pacc.bufs[3]]
                        else:
                            pB = [s.pacc.tiles[2], s.pacc.tiles[2][:, 256:512]]
                            pBb = [s.pacc.bufs[2], s.pacc.bufs[2]]
                        for d in range(2):
                            s.mm(pB[d][:, 0:W], QT[:, cs], Sin[d][:, c, 0:W], True, True, [BQT, BSin[d]], [pBb[d]])
                        if ssd:
                            a1, a1b = t1.next()
                            a2, a2b = t2.next()
                            a3, a3b = t3.next()
                            v3 = lambda t: t[:, :].rearrange("p (h e) -> p h e", e=64)
                            s.tt("dve", v3(a1), pB[0][:, 0:512].rearrange("p (h e) -> p h e", e=64), gsl(ecum, c, 0),
                                 ALU.mult, [pBb[0], Bd], [a1b])
                            s.tt("dve", v3(a2), pB[1][:, 0:512].rearrange("p (h e) -> p h e", e=64), gsl(ecum, c, 1),
                                 ALU.mult, [pBb[1], Bd], [a2b])
                            s.tt("pool", a3[:, :], V[:, c, :], dsk[:, 512 * u:512 * u + 512], ALU.mult, [BV, Bdsk], [a3b])
                            s.tt("pool", a1[:, :], a1[:, :], a2[:, :], ALU.add, [a1b, a2b], [a1b])
                            s.tt("pool", a1[:, :], a1[:, :], a3[:, :], ALU.add, [a1b, a3b], [a1b])
                            s.tt("dve", a1[:, :], a1[:, :], pA[0][:, 0:512], ALU.add, [a1b, pAb[0]], [a1b])
                            y, yb = yo.next()
                            s.tt("dve", y[:, :], a1[:, :], z[:, 0:512], ALU.mult, [a1b, zb], [yb])
                            s.act(a2[:, :], y[:, :], AF.Square, [yb, a2b], [a2b, Bssq],
                                  accum_out=ssq[:, 2 * c + u:2 * c + u + 1])
                            s.dma(None, s.ys_tok[c * 128:c * 128 + 128, 512 * u:512 * u + 512], y[:, :], [yb], [s.Bys])
                        else:
                            yd = []
                            ydb = []
                            small, smb = sm.next()
                            for d in range(2):
                                ta, tab = (t1 if d == 0 else t2).next()
                                s.act(ta[:, 0:W], pA[d][:, 0:W], AF.Copy, [pAb[d]], [tab])
                                s.stt(ta[:, 0:W], pB[d][:, 0:W], gsl(ecum, c, d), ta[:, 0:W], ALU.mult, ALU.add,
                                      [pBb[d], tab, Bd], [tab])
                                s.act(small[:, d:d + 1], ta[:, 128:129], AF.Abs, [tab], [smb])
                                s.ts("dve", small[:, d:d + 1], small[:, d:d + 1], 1.0, None, ALU.max, None, [smb], [smb])
                                s.recip(small[:, 2 + d:3 + d], small[:, d:d + 1], [smb], [smb])
                                yd.append(ta)
                                ydb.append(tab)
                            a3, a3b = t3.next()
                            s.ts("dve", a3[:, 0:128], yd[0][:, 0:128], small[:, 2:3], None, ALU.mult, None,
                                 [ydb[0], smb], [a3b])
                            s.stt(a3[:, 0:128], yd[1][:, 0:128], small[:, 3:4], a3[:, 0:128], ALU.mult, ALU.add,
                                  [ydb[1], smb, a3b], [a3b])
                            s.act(yd[0][:, 0:128], a3[:, 0:128], AF.Square, [a3b, ydb[0]], [ydb[0], smb],
                                  accum_out=small[:, 4:5])
                            s.act(small[:, 5:6], small[:, 4:5], AF.Sqrt, [smb], [smb], scale=1.0 / 128.0, bias=EPS)
                            s.recip(small[:, 6:7], small[:, 5:6], [smb], [smb])
                            y, yb = yo.next()
                            s.stt(y[:, 0:128], a3[:, 0:128], small[:, 6:7], z[:, 0:128], ALU.mult, ALU.mult,
                                  [a3b, smb, zb], [yb])
                            return (y, yb)
                        return None

                    def tail(c, yy):
                        y, yb = yy
                        po_, pob = s.ps.next()
                        po = po_[:, 0:128]
                        s.tr(po, y[:, 0:128], [yb], [pob])
                        ym, ymb = ymc.next()
                        s.ts("dve", ym[:, :], po, s.pp[:, PP_MNW + 8 * l + u:PP_MNW + 8 * l + u + 1], None, ALU.mult, None,
                             [pob, s.Bconst], [ymb])
                        s.dma(None, s.ybT[1][u * 128:u * 128 + 128, c * 128:c * 128 + 128], ym[:, :], [ymb],
                              [s.BybT[1]])

                    chunks = list(range(first_c, NCH))
                    fr = front(chunks[0])
                    pend = None
                    for ci, c in enumerate(chunks):
                        nxt = front(chunks[ci + 1]) if ci + 1 < len(chunks) else None
                        frontB(c, fr)
                        if pend is not None:
                            tail(*pend)
                        yy = back(c, fr)
                        pend = (c, yy) if yy is not None else None
                        fr = nxt
                    if pend is not None:
                        tail(*pend)

            if s.debug.get("scan_dump") == kind:
                s.dump("V", V[:], [128, NCH, VW], BV, BF16)
                s.dump("KT", KT[:], [128, T], BKT, BF16)
                s.dump("QT", QT[:], [128, T], BQT, BF16)
                s.dump("Ktok", Ktok[:], [128, NCH, 128], BKtok, BF16)
                s.dump("Sin0", Sin[0][:], [128, NCH, VW], BSin[0], BF16)
                s.dump("Sin1", Sin[1][:], [128, NCH, VW], BSin[1], BF16)
                s.dump("ssq", ssq[:], [128, 2 * NCH], Bssq)
            if ssd:
                ytr = Rot(lst, s.sbt, "sc_ytr", [128, 1024], F32, 3)
                ynr = Rot(lst, s.sbt, "sc_ynr", [128, 1024], BF16, 2)
                ysc = Rot(lst, s.sbt, "sc_ysc", [128, 8, 128], BF16, 2)
                sm2 = Rot(lst, s.sbt, "sc_sm2", [128, 8], F32, 2)
                def yload(c):
                    yt, ytb = ytr.next()
                    s.dma(None, yt[:, :], s.ys_tok[c * 128:c * 128 + 128, :], [s.Bys], [ytb])
                    return yt, ytb
                ynext = yload(first_c)
                for c in range(first_c, NCH):
                    yt, ytb = ynext
                    ynext = yload(c + 1) if c + 1 < NCH else None
                    small, smb = sm2.next()
                    s.tt("dve", small[:, 0:1], ssq[:, 2 * c:2 * c + 1], ssq[:, 2 * c + 1:2 * c + 2], ALU.add, [Bssq], [smb])
                    s.act(small[:, 1:2], small[:, 0:1], AF.Sqrt, [smb], [smb], scale=1.0 / 1024.0, bias=EPS)
                    s.recip(small[:, 2:3], small[:, 1:2], [smb], [smb])
                    yn, ynb = ynr.next()
                    s.ts("dve", yn[:, :], yt[:, :], small[:, 2:3], None, ALU.mult, None, [ytb, smb], [ynb])
                    yc, ycb = ysc.next()
                    for j0 in range(0, 8, 4):
                        po, pob = s.ps.next()
                        for ji in range(4):
                            j = j0 + ji
                            s.tr(po[:, ji * 128:ji * 128 + 128], yn[:, j * 128:j * 128 + 128], [ynb], [pob])
                        for ji in range(4):
                            j = j0 + ji
                            s.act(yc[:, j, :], po[:, ji * 128:ji * 128 + 128], AF.Identity, [pob, s.Bconst], [ycb],
                                  scale=s.pp[:, PP_SNW + 8 * l + j:PP_SNW + 8 * l + j + 1])
                    s.dma(None, s.ybT[0][:, c * 128:c * 128 + 128].rearrange("(j p) t -> p j t", p=128), yc[:, :, :],
                          [ycb], [s.BybT[0]])

    def stage_hyena(self, l, need_ctx):
        self.hyena_seq(l, TL, LT0, TC, 0)
        if need_ctx:
            self.hyena_seq(l, TC, CT0, 0, 16)

    def bload(self, dst, row_ap, n, buf):
        self.dma("sp", dst, row_ap.broadcast_to([128, n]), [], [buf])

    def hyena_seq(self, l, L, col0, tok0, tn0):
        s = self
        nt = L // 128
        nw = _wp(L) // 128
        dft = s.dft[L]
        TWO_PI = 2.0 * math.pi
        MAGIC = 12582912.0
        with scope() as lst:
            def A(name, shape, dt):
                return lst.enter_context(s.sbt("hy_" + name, list(shape), dt))
            feats = A("feats", [33, L], F32)
            w1 = A("w1", [33, 64], F32)
            w2 = A("w2", [64, 64], F32)
            w3 = A("w3", [64, 4096], F32)
            hid = [A("hid1", [64, L], F32), A("hid2", [64, L], F32)]
            absdec = A("absdec", [128, 4096], F32)
            hrow = A("hrow", [128, 4096], F32)
            Bw, Bh, Bdec, Brow = newbuf(), [newbuf(), newbuf()], newbuf(), newbuf()
            ta = Rot(lst, s.sbt, "hy_ta", [64, 512], F32, 2)
            tb = Rot(lst, s.sbt, "hy_tb", [64, 512], F32, 2)
            win = Rot(lst, s.sbt, "hy_win", [128, 512], F32, 2)
            hso = Rot(lst, s.sbt, "hy_hso", [128, 2048], BF16, 2)
            hdo = Rot(lst, s.sbt, "hy_hdo", [128, 2048], BF16, 2)
            s.dma("sp", feats[:], s.feats[L], [], [Bw])
            s.dma("sp", w1[:], s.hyw1[l], [], [Bw])
            s.dma("sp", w2[:], s.hyw2[l], [], [Bw])
            s.dma("sp", w3[:], s.hyw3[l], [], [Bw])
            s.bload(absdec[:], s.rowp[l:l + 1, RP_HDEC:RP_HDEC + 4096], 4096, Bdec)
            s.act(absdec[:], absdec[:], AF.Abs, [Bdec], [Bdec])
            nq = max(1, L // 512)
            qn = min(512, L)
            for layer in range(2):
                wgt = w1 if layer == 0 else w2
                bcol = (PP_HB1 if layer == 0 else PP_HB2) + l
                for q in range(nq):
                    pt, pb = s.ps.next()
                    src = feats[:, q * qn:(q + 1) * qn] if layer == 0 else hid[0][:, q * qn:(q + 1) * qn]
                    s.mm(pt[0:64, 0:qn], wgt[:, :], src, True, True, [Bw] + ([Bh[0]] if layer else []), [pb])
                    a, ab = ta.next()
                    b, bb = tb.next()
                    s.ts("dve", a[:, 0:qn], pt[0:64, 0:qn], s.pp[0:64, bcol:bcol + 1], None, ALU.add, None, [pb, s.Bconst], [ab])
                    s.ts("dve", b[:, 0:qn], a[:, 0:qn], math.pi, None, ALU.is_gt, None, [ab], [bb])
                    s.stt(a[:, 0:qn], b[:, 0:qn], -TWO_PI, a[:, 0:qn], ALU.mult, ALU.add, [bb, ab], [ab])
                    s.ts("dve", b[:, 0:qn], a[:, 0:qn], -math.pi, None, ALU.is_lt, None, [ab], [bb])
                    s.stt(a[:, 0:qn], b[:, 0:qn], TWO_PI, a[:, 0:qn], ALU.mult, ALU.add, [bb, ab], [ab])
                    s.act(hid[layer][:, q * qn:(q + 1) * qn], a[:, 0:qn], AF.Sin, [ab], [Bh[layer]])
            for tc in range(nt):
                for cb in range(8):
                    pt, pb = s.ps.next()
                    s.mm(pt[:, 0:512], hid[1][:, tc * 128:tc * 128 + 128], w3[:, cb * 512:cb * 512 + 512], True, True,
                         [Bh[1], Bw], [pb])
                    w_, wb_ = win.next()
                    s.act(w_[:, :], absdec[:, cb * 512:cb * 512 + 512], AF.Exp, [Bdec, s.Bconst], [wb_],
                          scale=s.tneg[:, tn0 + tc:tn0 + tc + 1])
                    s.tt("dve", hrow[:, cb * 512:cb * 512 + 512], pt[:, 0:512], w_[:, :], ALU.mult, [pb, wb_], [Brow])
                hs, hsb = hso.next()
                hd, hdb = hdo.next()
                for o in range(2):
                    hf = hrow[:, o * 2048:o * 2048 + 1024]
                    hb = hrow[:, o * 2048 + 1024:o * 2048 + 2048]
                    s.tt("dve", hs[:, o * 1024:o * 1024 + 1024], hf, hb, ALU.add, [Brow], [hsb])
                    s.tt("pool", hd[:, o * 1024:o * 1024 + 1024], hb, hf, ALU.subtract, [Brow], [hdb])
                s.dma(None, s.hsd[0][tc * 128:tc * 128 + 128, :], hs[:, :], [hsb], [s.Bhsd])
                s.dma(None, s.hsd[1][tc * 128:tc * 128 + 128, :], hd[:, :], [hdb], [s.Bhsd])
            if s.debug.get("hy_dump") and L == TL:
                s.dump("hid1", hid[0][:], [64, L], Bh[0])
                s.dump("hid2", hid[1][:], [64, L], Bh[1])
                s.dump("hrow", hrow[:], [128, 4096], Brow)
                s.dump("hs", hs[:], [128, 2048], hsb, BF16)
        with scope() as lst:
            hsb_ = Rot(lst, s.sbt, "hy_hsb", [128, nt, 1024], BF16, 1)
            hdb_ = Rot(lst, s.sbt, "hy_hdb", [128, nt, 1024], BF16, 1)
            ctl = Rot(lst, s.sbt, "hy_ct", [128, nt * 128], BF16, 2)
            stl = Rot(lst, s.sbt, "hy_st", [128, nt * 128], BF16, 2)
            kr = Rot(lst, s.sbt, "hy_kr", [128, 512], F32, 2)
            ki = Rot(lst, s.sbt, "hy_ki", [128, 512], F32, 2)
            for cbp in range(2):
                hsT, hsB = hsb_.next()
                hdT, hdB = hdb_.next()
                for tc in range(nt):
                    s.dma(None, hsT[:, tc, :], s.hsd[0][tc * 128:tc * 128 + 128, cbp * 1024:cbp * 1024 + 1024], [s.Bhsd], [hsB])
                    s.dma(None, hdT[:, tc, :], s.hsd[1][tc * 128:tc * 128 + 128, cbp * 1024:cbp * 1024 + 1024], [s.Bhsd], [hdB])
                for wt in range(nw):
                    ct, ctb = ctl.next()
                    st_, stb = stl.next()
                    s.dma(None, ct[:], dft["ct"][wt], [], [ctb])
                    s.dma(None, st_[:], dft["st"][wt], [], [stb])
                    for half in range(2):
                        cb = cbp * 2 + half
                        hc = slice(half * 512, half * 512 + 512)
                        pR, pRb = s.ps.next()
                        pI, pIb = s.ps.next()
                        for tc in range(nt):
                            s.mm(pR[:, 0:512], ct[:, tc * 128:tc * 128 + 128], hsT[:, tc, hc], tc == 0, tc == nt - 1,
                                 [ctb, hsB], [pRb])
                        for tc in range(nt):
                            s.mm(pI[:, 0:512], st_[:, tc * 128:tc * 128 + 128], hdT[:, tc, hc], tc == 0, tc == nt - 1,
                                 [stb, hdB], [pIb])
                        kre, kreb = kr.next()
                        kim, kimb = ki.next()
                        s.act(kre[:, :], pR[:, 0:512], AF.Copy, [pRb], [kreb])
                        s.cp("dve", kim[:, :], pI[:, 0:512], [pIb], [kimb])
                        s.dma(None, s.hyK[0][wt * 128:wt * 128 + 128, cb * 512:cb * 512 + 512], kre[:, :], [kreb], [s.BhyK])
                        s.dma(None, s.hyK[1][wt * 128:wt * 128 + 128, cb * 512:cb * 512 + 512], kim[:, :], [kimb], [s.BhyK])
        if s.debug.get("hy_stop") == "filt":
            return
        for cb2 in range(2):
            with scope() as lst:
                z = lst.enter_context(s.sbt("hy_z", [128, nt, 512], BF16))
                Yre = lst.enter_context(s.sbt("hy_yre", [128, nw, 512], BF16))
                Yim = lst.enter_context(s.sbt("hy_yim", [128, nw, 512], BF16))
                Wk = [lst.enter_context(s.sbt("hy_wk%d" % i, [128, 8, 512], BF16)) for i in range(3)]
                rowt = lst.enter_context(s.sbt("hy_rowt", [128, 3, 512], F32))
                cbias = lst.enter_context(s.sbt("hy_cbias", [128, 512], F32))
                skipb = lst.enter_context(s.sbt("hy_skipb", [128, 512], F32))
                Bz, BY, BWk, Brow, Bcb, Bsk = newbuf(), newbuf(), newbuf(), newbuf(), newbuf(), newbuf()
                kr = Rot(lst, s.sbt, "hy_kr2", [128, 512], F32, 2)
                ki = Rot(lst, s.sbt, "hy_ki2", [128, 512], F32, 2)
                tmp = [Rot(lst, s.sbt, "hy_tmp%d" % i, [128, 512], F32, 1) for i in range(4)]
                xg = Rot(lst, s.sbt, "hy_xg", [128, 512], F32, 2)
                yb16 = Rot(lst, s.sbt, "hy_yb16", [128, 512], BF16, 2)
                yT = Rot(lst, s.sbt, "hy_yT", [128, 4, 128], BF16, 2)

                def prep_part(part):
                    c0 = HY0 + part * 1024 + cb2 * 512
                    w, wb = s.load_w(s.w_in[l][:, c0:c0 + 512], 512)
                    for tap in range(3):
                        o = RP_HCW + tap * 3072 + part * 1024 + cb2 * 512
                        s.bload(rowt[:, tap, :], s.rowp[l:l + 1, o:o + 512], 512, Brow)
                    o = RP_HCB + part * 1024 + cb2 * 512
                    s.bload(cbias[:, :], s.rowp[l:l + 1, o:o + 512], 512, Bcb)
                    for tap in range(3):
                        s.tt("dve" if tap != 1 else "pool", Wk[tap][:], w[:, :, :],
                             rowt[:, tap, :].unsqueeze(1).broadcast_to([128, 8, 512]), ALU.mult, [wb, Brow], [BWk])

                def proj3(tc):
                    pt, pb = s.ps.next()
                    i = 0
                    for tap in range(3):
                        for k in range(8):
                            c_ = col0 + tc * 128 + tap - 1
                            s.mm(pt[:, 0:512], s.hT[:, k, c_:c_ + 128], Wk[tap][:, k, :], i == 0, i == 23, [s.BhT, BWk], [pb])
                            i += 1
                    return pt, pb

                prep_part(0)
                for tc in range(nt):
                    pt, pb = proj3(tc)
                    s.tt("dve", z[:, tc, :], pt[:, 0:512], cbias[:, :], ALU.add, [pb, Bcb], [Bz])
                for n in range(2):
                    prep_part(n + 1)
                    o = RP_HSKIP + n * 1024 + cb2 * 512
                    s.bload(skipb[:, :], s.rowp[l:l + 1, o:o + 512], 512, Bsk)
                    with scope() as fl_:
                        ctl = Rot(fl_, s.sbt, "hy_ct2", [128, nt * 128], BF16, 2)
                        stl = Rot(fl_, s.sbt, "hy_st2", [128, nt * 128], BF16, 2)
                        for wt in range(nw):
                            ct, ctb = ctl.next()
                            st_, stb = stl.next()
                            s.dma(None, ct[:], dft["ct"][wt], [], [ctb])
                            s.dma(None, st_[:], dft["st"][wt], [], [stb])
                            kre, kreb = kr.next()
                            kim, kimb = ki.next()
                            kc = n * 1024 + cb2 * 512
                            s.dma(None, kre[:, :], s.hyK[0][wt * 128:wt * 128 + 128, kc:kc + 512], [s.BhyK], [kreb])
                            s.dma(None, kim[:, :], s.hyK[1][wt * 128:wt * 128 + 128, kc:kc + 512], [s.BhyK], [kimb])
                            pR, pRb = s.ps.next()
                            pS, pSb = s.ps.next()
                            for tc in range(nt):
                                s.mm(pR[:, 0:512], ct[:, tc * 128:tc * 128 + 128], z[:, tc, :], tc == 0, tc == nt - 1,
                                     [ctb, Bz], [pRb])
                            for tc in range(nt):
                                s.mm(pS[:, 0:512], st_[:, tc * 128:tc * 128 + 128], z[:, tc, :], tc == 0, tc == nt - 1,
                                     [stb, Bz], [pSb])
                            t = [r_.next() for r_ in tmp]
                            s.tt("dve", t[0][0][:, :], pR[:, 0:512], kre[:, :], ALU.mult, [pRb, kreb], [t[0][1]])
                            s.tt("dve", t[1][0][:, :], pS[:, 0:512], kim[:, :], ALU.mult, [pSb, kimb], [t[1][1]])
                            s.tt("dve", t[2][0][:, :], pR[:, 0:512], kim[:, :], ALU.mult, [pRb, kimb], [t[2][1]])
                            s.tt("dve", t[3][0][:, :], pS[:, 0:512], kre[:, :], ALU.mult, [pSb, kreb], [t[3][1]])
                            s.tt("pool", Yre[:, wt, :], t[0][0][:, :], t[1][0][:, :], ALU.add, [t[0][1], t[1][1]], [BY])
                            s.tt("pool", Yim[:, wt, :], t[2][0][:, :], t[3][0][:, :], ALU.subtract, [t[2][1], t[3][1]], [BY])
                    with scope() as il_:
                        cil = Rot(il_, s.sbt, "hy_ci2", [128, nw * 128], BF16, 2)
                        sil = Rot(il_, s.sbt, "hy_si2", [128, nw * 128], BF16, 2)
                        for tc in range(nt):
                            ci, cib = cil.next()
                            si, sib = sil.next()
                            s.dma(None, ci[:], dft["ci"][tc], [], [cib])
                            s.dma(None, si[:], dft["si"][tc], [], [sib])
                            pY, pYb = s.ps.next()
                            for wc in range(nw):
                                s.mm(pY[:, 0:512], ci[:, wc * 128:wc * 128 + 128], Yre[:, wc, :], wc == 0, False,
                                     [cib, BY], [pYb])
                            for wc in range(nw):
                                s.mm(pY[:, 0:512], si[:, wc * 128:wc * 128 + 128], Yim[:, wc, :], False, wc == nw - 1,
                                     [sib, BY], [pYb])
                            pX, pXb = proj3(tc)
                            x_, xb_ = xg.next()
                            s.tt("dve", x_[:, :], pX[:, 0:512], cbias[:, :], ALU.add, [pXb, Bcb], [xb_])
                            t0, t0b = tmp[0].next()
                            s.tt("pool", t0[:, :], z[:, tc, :], skipb[:, :], ALU.mult, [Bz, Bsk], [t0b])
                            s.tt("dve", t0[:, :], t0[:, :], pY[:, 0:512], ALU.add, [t0b, pYb], [t0b])
                            if n == 0:
                                s.tt("dve", z[:, tc, :], t0[:, :], x_[:, :], ALU.mult, [t0b, xb_], [Bz])
                            else:
                                y_, yb_ = yb16.next()
                                s.tt("dve", y_[:, :], t0[:, :], x_[:, :], ALU.mult, [t0b, xb_], [yb_])
                                po, pob = s.ps.next()
                                for j in range(4):
                                    s.tr(po[:, j * 128:j * 128 + 128], y_[:, j * 128:j * 128 + 128], [yb_], [pob])
                                yt_, ytb_ = yT.next()
                                s.act(yt_[:, :, :], po[:, 0:512].rearrange("p (j t) -> p j t", t=128), AF.Copy, [pob], [ytb_])
                                tk = tok0 + tc * 128
                                s.dma(None, s.ybT[2][cb2 * 512:cb2 * 512 + 512, tk:tk + 128].rearrange("(j p) t -> p j t", p=128),
                                      yt_[:, :, :], [ytb_], [s.BybT[2]])

    def tok_tiles(self, need_ctx):
        out = []
        if need_ctx:
            out.append((0, TC, 0, 1))
        for q in range(4):
            out.append((q + 1, 512, TC + 512 * q, 0))
        return out

    def resid_load(self, n, dt2, off, xr):
        s = self
        xt, xb = xr.next()
        rows = slice(dt2 * 128, dt2 * 128 + 128)
        s.dma(None, xt[:, 0:n], s.xres[rows, off:off + n], [s.BxresL[dt2]], [xb])
        return xt, xb

    def resid_finish(self, pO, pOb, n, dt2, off, gate_m, which, pre, ob):
        s = self
        xt, xb = pre
        rows = slice(dt2 * 128, dt2 * 128 + 128)
        xo, xob = ob.next()
        s.stt(xo[:, 0:n], pO[:, 0:n], s.modcol(gate_m, dt2, which), xt[:, 0:n], ALU.mult, ALU.add,
              [pOb, xb, s.Bmod], [xob])
        s.dma(None, s.xres[rows, off:off + n], xo[:, 0:n], [xob], [s.BxresL[dt2]])

    def stage_merge(self, l, need_ctx):
        s = self
        tiles = s.tok_tiles(need_ctx)
        with scope() as lst:
            mT = lst.enter_context(s.sbt("mg_mT", [128, 8, T], BF16))
            yT = lst.enter_context(s.sbt("mg_yT", [128, 8, T], BF16))
            BmT, ByT = newbuf(), newbuf()
            sg = Rot(lst, s.sbt, "mg_sg", [128, 512], F32, 2)
            tm = Rot(lst, s.sbt, "mg_tm", [128, 512], F32, 2)
            with scope() as l2:
                yP = l2.enter_context(s.sbt("mg_yP", [128, 8, TL], BF16))
                ByP = newbuf()
                for n in range(3):
                    for k in range(8):
                        rows = slice(k * 128, k * 128 + 128)
                        if n == 1:
                            s.dma(None, yT[:, k, 0:TC], s.ybT[n][rows, 0:TC], [s.BybT[n]], [ByT])
                            s.dma(None, yP[:, k, :], s.ybT[n][rows, TC:T], [s.BybT[n]], [ByP])
                            src = yP[:, k, :].rearrange("p (w r) -> p r w", r=32)
                            dst = yT[:, k, TC:T].rearrange("p (r w) -> p r w", w=64)
                            if k % 2 == 0:
                                s.act(dst, src, AF.Copy, [ByP], [ByT])
                            else:
                                s.P.op("dve", (lambda o, i_: (lambda e: e.tensor_copy(out=o, in_=i_)))(dst, src), [ByP], [ByT])
                        else:
                            s.dma(None, yT[:, k, :], s.ybT[n][rows, :], [s.BybT[n]], [ByT])
                    for dt in range(8):
                        wb_t, wbb = s.load_w(s.w_branch[l][n][:, dt * 128:dt * 128 + 128], 128)
                        c0 = G0 + n * 1024 + dt * 128
                        wg_t, wgb = s.load_w(s.w_in[l][:, c0:c0 + 128], 128)
                        for (ti, nt_, off, which) in tiles:
                            pP, pPb = s.ps.next()
                            for k in range(8):
                                s.mm(pP[:, 0:nt_], wb_t[:, k, 0:128], yT[:, k, off:off + nt_], k == 0, k == 7, [wbb, ByT], [pPb])
                            pG, pGb = s.ps.next()
                            for k in range(8):
                                s.mm(pG[:, 0:nt_], wg_t[:, k, 0:128], s.htile(k, ti)[0], k == 0, k == 7, [wgb, s.BhT], [pGb])
                            g, gb = sg.next()
                            s.act(g[:, 0:nt_], pG[:, 0:nt_], AF.Sigmoid, [pGb], [gb])
                            if n == 0:
                                s.tt("dve", mT[:, dt, off:off + nt_], pP[:, 0:nt_], g[:, 0:nt_], ALU.mult, [pPb, gb], [BmT])
                            else:
                                t, tb = tm.next()
                                s.tt("dve", t[:, 0:nt_], pP[:, 0:nt_], g[:, 0:nt_], ALU.mult, [pPb, gb], [tb])
                                s.tt("dve", mT[:, dt, off:off + nt_], mT[:, dt, off:off + nt_], t[:, 0:nt_], ALU.add,
                                     [tb, BmT], [BmT])
            if s.debug.get("merge_dump"):
                s.dump("mT", mT[:], [128, 8, T], BmT, BF16)
            wo = lst.enter_context(s.sbt("mg_wo", [128, 8, D], BF16))
            Bwo = newbuf()
            s.dma("pool", wo[:], s.w_out[l].rearrange("(k p) n -> p k n", p=128), [], [Bwo])
            xr = Rot(lst, s.sbt, "mg_xr", [128, 512], F32, 3)
            ob = Rot(lst, s.sbt, "mg_ob", [128, 512], F32, 2)
            items = [(ti, nt_, off, which, dt2) for (ti, nt_, off, which) in tiles for dt2 in range(8)]
            pre = s.resid_load(items[0][1], items[0][4], items[0][2], xr)
            for i, (ti, nt_, off, which, dt2) in enumerate(items):
                pO, pOb = s.ps.next()
                for k in range(8):
                    s.mm(pO[:, 0:nt_], wo[:, k, dt2 * 128:dt2 * 128 + 128], mT[:, k, off:off + nt_], k == 0, k == 7,
                         [Bwo, BmT], [pOb])
                nxt = s.resid_load(items[i + 1][1], items[i + 1][4], items[i + 1][2], xr) if i + 1 < len(items) else None
                s.resid_finish(pO, pOb, nt_, dt2, off, 2, which, pre, ob)
                pre = nxt

    def stage_mlp(self, l, need_ctx):
        s = self
        tiles = s.tok_tiles(need_ctx)
        with scope() as lst:
            hid = lst.enter_context(s.sbt("ml_hid", [128, 32, 512], BF16))
            Bhid = newbuf()
            rl = Rot(lst, s.sbt, "ml_rl", [128, 512], BF16, 2)
            xr = Rot(lst, s.sbt, "ml_xr", [128, 512], F32, 3)
            ob = Rot(lst, s.sbt, "ml_ob", [128, 512], F32, 2)
            for (ti, nt_, off, which) in tiles:
                for fb in range(8):
                    w1t, w1b = s.load_w(s.mlp_w1[l][:, fb * 512:fb * 512 + 512], 512)
                    for fj in range(4):
                        f = fb * 4 + fj
                        pH, pHb = s.ps.next()
                        for k in range(8):
                            s.mm(pH[:, 0:nt_], w1t[:, k, fj * 128:fj * 128 + 128], s.htile(k, ti)[0], k == 0, k == 7,
                                 [w1b, s.BhT], [pHb])
                        r, rb = rl.next()
                        s.act(r[:, 0:nt_], pH[:, 0:nt_], AF.Relu, [pHb], [rb])
                        s.tt("dve", hid[:, f, 0:nt_], r[:, 0:nt_], r[:, 0:nt_], ALU.mult, [rb], [Bhid])
                pre = s.resid_load(nt_, 0, off, xr)
                for dt2 in range(8):
                    w2t, w2b = s.wrot.next()
                    w2v = w2t[:, :, :].rearrange("p k (a n) -> p (k a) n", n=128)
                    s.dma("pool", w2v, s.mlp_w2[l][:, dt2 * 128:dt2 * 128 + 128].rearrange("(f p) n -> p f n", p=128),
                          [], [w2b])
                    pO, pOb = s.ps.next()
                    for f in range(32):
                        s.mm(pO[:, 0:nt_], w2v[:, f, :], hid[:, f, 0:nt_], f == 0, f == 31, [w2b, Bhid], [pOb])
                    nxt = s.resid_load(nt_, dt2 + 1, off, xr) if dt2 + 1 < 8 else None
                    s.resid_finish(pO, pOb, nt_, dt2, off, 5, which, pre, ob)
                    pre = nxt

    def stage_final(self):
        s = self
        with scope() as lst:
            xt = Rot(lst, s.sbt, "fx", [128, 8, 512], F32, 2)
            sq = Rot(lst, s.sbt, "fsq", [128, 8, 512], F32, 1)
            rs = Rot(lst, s.sbt, "frs", [128, 512], F32, 2)
            ot = Rot(lst, s.sbt, "fo", [128, 8, 512], F32, 2)
            for q in range(4):
                t0 = TC + 512 * q
                x, xb = xt.next()
                s.dma(None, x[:, :, :], s.xres[:, t0:t0 + 512].rearrange("(k p) t -> p k t", p=128), list(s.BxresL), [xb])
                q_, qb = sq.next()
                s.act(q_[:, :, :], x[:, :, :], AF.Square, [xb], [qb])
                pt, pb = s.ps.next()
                for k in range(8):
                    s.mm(pt[:, 0:512], s.ones(), q_[:, k, :], k == 0, k == 7, [qb, s.Bconst], [pb])
                r, rb = rs.next()
                s.act(r[:, :], pt[:, 0:512], AF.Sqrt, [pb], [rb], scale=1.0 / D, bias=EPS)
                s.recip(r[:, :], r[:, :], [rb], [rb])
                o, ob_ = ot.next()
                for k in range(8):
                    s.stt(o[:, k, :], x[:, k, :], s.pp[:, PP_NF + k:PP_NF + k + 1], r[:, :], ALU.mult, ALU.mult,
                          [xb, rb, s.Bconst], [ob_])
                s.dma(None, s.out[:, 512 * q:512 * q + 512].rearrange("(k p) t -> p k t", p=128), o[:, :, :], [ob_], [Buf()])


def _bf16(a):
    import ml_dtypes
    return np.ascontiguousarray(a.astype(np.float32)).astype(ml_dtypes.bfloat16)


def _consts():
    k = np.arange(128)
    cst = np.zeros((128, CS_N), np.float32)
    cst[:, CS_TRIU:CS_TRIU + 128] = (k[:, None] <= k[None, :])
    cst[:, CS_TRIL:CS_TRIL + 128] = (k[:, None] >= k[None, :])
    cst[:, CS_MASKF:CS_MASKF + 128] = np.where(k[:, None] <= k[None, :], 0.0, NEG)
    cst[:, CS_MASKB:CS_MASKB + 128] = np.where(k[:, None] >= k[None, :], 0.0, NEG)
    cst[:, CS_IDENT:CS_IDENT + 128] = np.eye(128)
    cst[:, CS_ONES:CS_ONES + 128] = 1.0
    sel = np.zeros((16, 16, 128), np.float32)
    for h in range(16):
        sel[h, h, :] = 1.0
    out = {"cst": cst, "sel": sel.reshape(16, 16 * 128)}
    tneg = np.zeros((128, 18), np.float32)
    for j in range(16):
        tneg[:, j] = -(j * 128 + k) / float(TL)
    for j in range(2):
        tneg[:, 16 + j] = -(j * 128 + k) / float(TC)
    out["tneg"] = tneg
    for L, nm in ((TL, "L"), (TC, "C")):
        t = np.arange(L, dtype=np.float32)
        t_norm = t / np.float32(L)
        bands = np.linspace(1e-4, 15.0, 16, dtype=np.float32)
        ang = (np.float32(2.0 * math.pi / L) * t[:, None] * bands[None, :]).astype(np.float32)
        feats = np.concatenate([t_norm[:, None], np.cos(ang), -np.sin(ang)], axis=-1).astype(np.float32)
        out["feats" + nm] = np.ascontiguousarray(feats.T)
        wp = _wp(L)
        nw, nt = wp // 128, L // 128
        sidx = np.arange(L, dtype=np.float64)
        widx = np.arange(wp, dtype=np.float64)
        ph = np.pi * np.outer(sidx, widx) / L
        valid = (widx <= L)[None, :]
        ctm = np.where(valid, np.cos(ph), 0.0)
        stm = np.where(valid, np.sin(ph), 0.0)
        cw = np.where((widx == 0) | (widx == L), 1.0, 2.0) * (widx <= L) / (2.0 * L)
        cim = (cw[:, None] * np.cos(ph.T))
        sim = -(cw[:, None] * np.sin(ph.T))
        def tile_fwd(m):
            return _bf16(m.reshape(nt, 128, nw, 128).transpose(2, 1, 0, 3).reshape(nw, 128, nt * 128))

        def tile_inv(m):
            return _bf16(m.reshape(nw, 128, nt, 128).transpose(2, 1, 0, 3).reshape(nt, 128, nw * 128))
        out["ct" + nm] = tile_fwd(ctm)
        out["st" + nm] = tile_fwd(stm)
        out["ci" + nm] = tile_inv(cim)
        out["si" + nm] = tile_inv(sim)
    return out


def _fm(v, n):
    return np.ascontiguousarray(np.asarray(v, np.float32).reshape(n, 128).T)


def _prep_shared(inp):
    f = lambda n: np.asarray(inp[n], np.float32)
    pp = np.zeros((128, PP_N), np.float32)
    for l in range(DEPTH):
        pp[:, PP_N1W + 8 * l:PP_N1W + 8 * l + 8] = _fm(f("norm1_w")[l], 8)
        pp[:, PP_N2W + 8 * l:PP_N2W + 8 * l + 8] = _fm(f("norm2_w")[l], 8)
        pp[:, PP_MODB + 48 * l:PP_MODB + 48 * l + 48] = _fm(f("mod_b")[l], 48)
        pp[:, PP_SCB + 12 * l:PP_SCB + 12 * l + 12] = _fm(f("ssd_conv_b")[l], 12)
        cw = f("ssd_conv_w")[l]
        pp[:, PP_SCW + 60 * l:PP_SCW + 60 * l + 60] = cw.reshape(5, 12, 128).transpose(2, 1, 0).reshape(128, 60)
        pp[:, PP_MCB + 16 * l:PP_MCB + 16 * l + 16] = _fm(f("ml_conv_b")[l], 16)
        cw = f("ml_conv_w")[l]
        pp[:, PP_MCW + 80 * l:PP_MCW + 80 * l + 80] = cw.reshape(5, 16, 128).transpose(2, 1, 0).reshape(128, 80)
        pp[:, PP_SNW + 8 * l:PP_SNW + 8 * l + 8] = _fm(f("ssd_norm_w")[l], 8)
        pp[:, PP_MNW + 8 * l:PP_MNW + 8 * l + 8] = _fm(f("ml_norm_w")[l], 8)
        pp[0:64, PP_HB1 + l] = f("hy_ffn_b1")[l]
        pp[0:64, PP_HB2 + l] = f("hy_ffn_b2")[l]
    pp[:, PP_NF:PP_NF + 8] = _fm(f("norm_f_w"), 8)
    rowp = np.zeros((DEPTH, RP_N), np.float32)
    for l in range(DEPTH):
        rowp[l, RP_DTB:RP_DTB + 32] = f("ssd_dt_bias")[l].reshape(-1)
        rowp[l, RP_ALOG:RP_ALOG + 32] = f("ssd_a_log")[l].reshape(-1)
        rowp[l, RP_DSK:RP_DSK + 1024] = np.repeat(f("ssd_d")[l], 64)
        rowp[l, RP_MGB:RP_MGB + 32] = f("ml_gate_b")[l].reshape(-1)
        rowp[l, RP_HCB:RP_HCB + 3072] = f("hy_conv_b")[l]
        rowp[l, RP_HCW:RP_HCW + 9216] = f("hy_conv_w")[l].reshape(-1)
        rowp[l, RP_HDEC:RP_HDEC + 4096] = f("hy_decay")[l].reshape(-1)
        rowp[l, RP_HSKIP:RP_HSKIP + 2048] = f("hy_skip")[l].reshape(-1)
    sh = {"mod_w": f("mod_w"), "w_in": f("w_in"), "w_branch": f("w_branch"), "w_out": f("w_out"),
          "mlp_w1": f("mlp_w1"), "mlp_w2": f("mlp_w2"), "pp": pp, "rowp": rowp,
          "hyw1": f("hy_ffn_w1"), "hyw2": f("hy_ffn_w2"), "hyw3": f("hy_ffn_w3")}
    sh.update(_consts())
    return sh


def _prep_core(inp, b):
    x = np.asarray(inp["x"], np.float32)[b]
    ctx = np.asarray(inp["ctx"], np.float32)[b]
    xT = np.ascontiguousarray(np.concatenate([ctx.T, x.T], axis=1))
    cvec = np.zeros((128, 8, 2), np.float32)
    cvec[:, :, 0] = _fm(np.asarray(inp["c"], np.float32)[b], 8)
    cvec[:, :, 1] = _fm(np.asarray(inp["c_ctx"], np.float32), 8)
    return {"xT": xT, "cvec": cvec.reshape(128, 16)}


_CACHE = {}


def kernel(**inputs):
    if "nc" not in _CACHE:
        _CACHE["nc"] = Builder().build()
    nc = _CACHE["nc"]
    shared = _prep_shared(inputs)
    in_maps = []
    for b in range(8):
        m = dict(shared)
        m.update(_prep_core(inputs, b))
        in_maps.append(m)
    res = run_bass_kernel_spmd(nc, in_maps, core_ids=list(range(8)))
    out = np.stack([np.ascontiguousarray(res.results[b]["out"].T) for b in range(8)], axis=0)
    return out.astype(np.float32)
```
